# Optimizing a Trainium2 kernel written in Bass

```python
import jax, jax.numpy as jnp
from jax import lax
import numpy as np

D_MODEL = 1024
BATCH = 2
SEQ = 16384
DEPTH = 4
DEC_BATCH = 8
DEC_SEQ = 64
PAST_LEN = 4096

CHUNK = 64
N_EVEN = (DEPTH + 1) // 2
N_ODD = DEPTH // 2
H_A = 8
DH_A = 64
W_A = H_A * DH_A
Q_BLOCK = 128
W_B = D_MODEL // 2
K_B = 3
W_C = D_MODEL // 2
G_C = 4
CG_C = W_C // G_C
GMLP_CHUNK = 128
W_D = D_MODEL // 2
K_D = 31
D_FF = 2816
K_FF = 3
EPS = 1e-6
OFF_K = W_A
OFF_V = 2 * W_A
OFF_F = 3 * W_A
OFF_BG = OFF_F + H_A
OFF_CG = OFF_BG + W_B
OFF_X = OFF_CG + W_B
IN_EVEN = OFF_X + W_B
OFF_AD = 2 * W_C
OFF_GD = OFF_AD + W_D
IN_ODD = OFF_GD + W_D

kernel_name = 'hybrid_fox_shortconv_gmlp_conformer_stream_step'


def rmsnorm(x, g):
    x32 = x.astype(jnp.float32)
    y = x32 * lax.rsqrt(jnp.mean(x32 * x32, axis=-1, keepdims=True) + EPS)
    return (y * g.astype(jnp.float32)).astype(x.dtype)


def causal_dwconv(x, hist, w):
    k = w.shape[0]
    xp = jnp.concatenate([hist.astype(x.dtype), x], axis=1)
    y = lax.conv_general_dilated(xp, w[:, None, :].astype(x.dtype), window_strides=(1,), padding='VALID',
                                 dimension_numbers=('NWC', 'WIO', 'NWC'), feature_group_count=x.shape[-1])
    return y, xp[:, xp.shape[1] - (k - 1):]


def fox_attend(q, k, v, cq, ck, q_pos, k_pos):
    s = jnp.einsum('bqhd,bkhd->bhqk', q, k, preferred_element_type=jnp.float32) * (DH_A ** -0.5)
    s = s + jnp.transpose(cq, (0, 2, 1))[..., :, None] - jnp.transpose(ck, (0, 2, 1))[..., None, :]
    s = jnp.where((q_pos[:, None] >= k_pos[None, :])[None, None], s, -jnp.inf)
    p = jax.nn.softmax(s, axis=-1)
    return jnp.einsum('bhqk,bkhd->bqhd', p.astype(v.dtype), v)


def fox_prompt(q, k, v, logf):
    b, t, h, d = q.shape
    c = jnp.cumsum(logf, axis=1)
    pos = jnp.arange(t)
    nb = t // Q_BLOCK
    qb = q.reshape(b, nb, Q_BLOCK, h, d).transpose(1, 0, 2, 3, 4)
    cb = c.reshape(b, nb, Q_BLOCK, h).transpose(1, 0, 2, 3)
    pb = pos.reshape(nb, Q_BLOCK)
    ob = lax.map(lambda a: fox_attend(a[0], k, v, a[1], c, a[2], pos), (qb, cb, pb))
    return ob.transpose(1, 0, 2, 3, 4).reshape(b, t, h, d)


def fox_sample(q, k, v, logf, k_cache, v_cache, logf_cache):
    past = k_cache.shape[1]
    t = q.shape[1]
    kk = jnp.concatenate([k_cache.astype(k.dtype), k], axis=1)
    vv = jnp.concatenate([v_cache.astype(v.dtype), v], axis=1)
    c = jnp.cumsum(jnp.concatenate([logf_cache.astype(jnp.float32), logf], axis=1), axis=1)
    k_pos = jnp.arange(past + t)
    q_pos = past + jnp.arange(t)
    return fox_attend(q, kk, vv, c[:, past:], c, q_pos, k_pos)


def even_mixer(xn, w_in, b_f, g_q, g_k, conv_w, w_out, hist_b, cache):
    b, t, _ = xn.shape
    proj = xn @ w_in
    q = rmsnorm(proj[..., :OFF_K].reshape(b, t, H_A, DH_A), g_q)
    k = rmsnorm(proj[..., OFF_K:OFF_V].reshape(b, t, H_A, DH_A), g_k)
    v = proj[..., OFF_V:OFF_F].reshape(b, t, H_A, DH_A)
    logf = jax.nn.log_sigmoid((proj[..., OFF_F:OFF_BG] + b_f).astype(jnp.float32))
    if cache is None:
        o_a = fox_prompt(q, k, v, logf)
    else:
        o_a = fox_sample(q, k, v, logf, cache[0], cache[1], cache[2])
    bg = proj[..., OFF_BG:OFF_CG]
    cg = proj[..., OFF_CG:OFF_X]
    xin = proj[..., OFF_X:]
    cx, hist = causal_dwconv(cg * xin, hist_b, conv_w)
    o_b = bg * cx
    y = jnp.concatenate([o_a.reshape(b, t, W_A), o_b], axis=-1) @ w_out
    return y, k, v, logf, hist


def spatial_gate(vc, w_s, b_s):
    b, t, _ = vc.shape
    n = -(-t // GMLP_CHUNK)
    tp = n * GMLP_CHUNK
    vp = jnp.pad(vc, ((0, 0), (0, tp - t), (0, 0))).reshape(b, n, GMLP_CHUNK, G_C, CG_C)
    ws = w_s * jnp.tril(jnp.ones((GMLP_CHUNK, GMLP_CHUNK), w_s.dtype))
    s = jnp.einsum('gts,bnsgc->bntgc', ws, vp) + jnp.transpose(b_s)[None, None, :, :, None]
    return s.reshape(b, tp, W_C)[:, :t]


def odd_mixer(xn, w_in, g_vc, w_s, b_s, conv_w, g_d, w_out, hist_d):
    proj = xn @ w_in
    z = jax.nn.gelu(proj[..., :OFF_AD])
    u = z[..., :W_C]
    vc = rmsnorm(z[..., W_C:], g_vc)
    o_c = u * spatial_gate(vc, w_s, b_s)
    glu = proj[..., OFF_AD:OFF_GD] * jax.nn.sigmoid(proj[..., OFF_GD:])
    cd, hist = causal_dwconv(glu, hist_d, conv_w)
    o_d = jax.nn.silu(rmsnorm(cd, g_d))
    y = jnp.concatenate([o_c, o_d], axis=-1) @ w_out
    return y, vc, hist


def conv_ffn(xn, w_up, conv_w, w_down, hist):
    h, hist = causal_dwconv(xn @ w_up, hist, conv_w)
    return (jax.nn.silu(h[..., :D_FF]) * h[..., D_FF:]) @ w_down, hist


def trunk(x, params, hist_b, hist_d, hist_ffn, att_cache):
    (g_mix, w_in_even, b_f, g_q, g_k, conv_b, w_out_even, w_in_odd, g_vc, w_s, b_s,
     conv_d, g_d, w_out_odd, g_ffn, w_up, conv_ffn_w, w_down) = params
    ks, vs, lfs, hbs, vcs, hds, hfs = [], [], [], [], [], [], []
    for l in range(DEPTH):
        i = l // 2
        xn = rmsnorm(x, g_mix[l])
        if l % 2 == 0:
            cache = None if att_cache is None else (att_cache[0][i], att_cache[1][i], att_cache[2][i])
            y, k, v, lf, hb = even_mixer(xn, w_in_even[i], b_f[i], g_q[i], g_k[i], conv_b[i],
                                         w_out_even[i], hist_b[i], cache)
            ks.append(k)
            vs.append(v)
            lfs.append(lf)
            hbs.append(hb)
        else:
            y, vc, hd = odd_mixer(xn, w_in_odd[i], g_vc[i], w_s[i], b_s[i], conv_d[i], g_d[i],
                                  w_out_odd[i], hist_d[i])
            vcs.append(vc)
            hds.append(hd)
        x = x + y
        y, hf = conv_ffn(rmsnorm(x, g_ffn[l]), w_up[l], conv_ffn_w[l], w_down[l], hist_ffn[l])
        x = x + y
        hfs.append(hf)
    return (x, jnp.stack(ks), jnp.stack(vs), jnp.stack(lfs), jnp.stack(hbs),
            jnp.stack(vcs), jnp.stack(hds), jnp.stack(hfs))


def setup_inputs(seed: int = 0) -> dict:
    key = jax.random.key(seed)
    ks = jax.random.split(key, 32)

    def nrm(k, shape, scale):
        return scale * jax.random.normal(k, shape, jnp.float32)

    f_bias = jnp.linspace(1.0, 7.0, H_A, dtype=jnp.float32)
    w_in_even = nrm(ks[9], (N_EVEN, D_MODEL, IN_EVEN), D_MODEL ** -0.5)
    w_in_even = w_in_even.at[:, :, OFF_F:OFF_BG].multiply(0.1)
    return {
        'x_prompt': nrm(ks[0], (BATCH, SEQ, D_MODEL), 1.0),
        'x_sample': nrm(ks[1], (DEC_BATCH, DEC_SEQ, D_MODEL), 1.0),
        'cache_k': nrm(ks[2], (N_EVEN, DEC_BATCH, PAST_LEN, H_A, DH_A), 1.0),
        'cache_v': nrm(ks[3], (N_EVEN, DEC_BATCH, PAST_LEN, H_A, DH_A), 1.0),
        'cache_logf': jax.nn.log_sigmoid(f_bias + nrm(ks[4], (N_EVEN, DEC_BATCH, PAST_LEN, H_A), 0.5)),
        'state_conv_b': nrm(ks[5], (N_EVEN, DEC_BATCH, K_B - 1, W_B), 1.0),
        'state_conv_d': nrm(ks[6], (N_ODD, DEC_BATCH, K_D - 1, W_D), 0.5),
        'state_conv_ffn': nrm(ks[7], (DEPTH, DEC_BATCH, K_FF - 1, 2 * D_FF), 1.0),
        'g_mix': 1.0 + nrm(ks[8], (DEPTH, D_MODEL), 0.01),
        'w_in_even': w_in_even,
        'b_f': f_bias + nrm(ks[10], (N_EVEN, H_A), 0.01),
        'g_q': 1.0 + nrm(ks[11], (N_EVEN, DH_A), 0.01),
        'g_k': 1.0 + nrm(ks[12], (N_EVEN, DH_A), 0.01),
        'conv_b': nrm(ks[13], (N_EVEN, K_B, W_B), K_B ** -0.5),
        'w_out_even': nrm(ks[14], (N_EVEN, W_A + W_B, D_MODEL), 0.5 * (W_A + W_B) ** -0.5),
        'w_in_odd': nrm(ks[15], (N_ODD, D_MODEL, IN_ODD), D_MODEL ** -0.5),
        'g_vc': 1.0 + nrm(ks[16], (N_ODD, W_C), 0.01),
        'w_s': nrm(ks[17], (N_ODD, G_C, GMLP_CHUNK, GMLP_CHUNK), GMLP_CHUNK ** -0.5),
        'b_s': 1.0 + nrm(ks[18], (N_ODD, G_C, GMLP_CHUNK), 0.01),
        'conv_d': nrm(ks[19], (N_ODD, K_D, W_D), K_D ** -0.5),
        'g_d': 1.0 + nrm(ks[20], (N_ODD, W_D), 0.01),
        'w_out_odd': nrm(ks[21], (N_ODD, W_C + W_D, D_MODEL), 0.5 * (W_C + W_D) ** -0.5),
        'g_ffn': 1.0 + nrm(ks[22], (DEPTH, D_MODEL), 0.01),
        'w_up': nrm(ks[23], (DEPTH, D_MODEL, 2 * D_FF), D_MODEL ** -0.5),
        'conv_ffn': nrm(ks[24], (DEPTH, K_FF, 2 * D_FF), K_FF ** -0.5),
        'w_down': nrm(ks[25], (DEPTH, D_FF, D_MODEL), 0.5 * D_FF ** -0.5),
    }


def reference(x_prompt, x_sample, cache_k, cache_v, cache_logf, state_conv_b, state_conv_d, state_conv_ffn,
              g_mix, w_in_even, b_f, g_q, g_k, conv_b, w_out_even, w_in_odd, g_vc, w_s, b_s,
              conv_d, g_d, w_out_odd, g_ffn, w_up, conv_ffn, w_down):
    params = (g_mix, w_in_even, b_f, g_q, g_k, conv_b, w_out_even, w_in_odd, g_vc, w_s, b_s,
              conv_d, g_d, w_out_odd, g_ffn, w_up, conv_ffn, w_down)
    b = x_prompt.shape[0]
    dt = x_prompt.dtype
    zb = jnp.zeros((N_EVEN, b, K_B - 1, W_B), dt)
    zd = jnp.zeros((N_ODD, b, K_D - 1, W_D), dt)
    zf = jnp.zeros((DEPTH, b, K_FF - 1, 2 * D_FF), dt)
    (y_prompt, p_k, p_v, p_logf, p_conv_b, _p_vc, p_conv_d, p_conv_ffn) = trunk(
        x_prompt, params, zb, zd, zf, None)
    (y_sample, s_k, s_v, s_logf, s_conv_b, s_vc, s_conv_d, s_conv_ffn) = trunk(
        x_sample, params, state_conv_b, state_conv_d, state_conv_ffn, (cache_k, cache_v, cache_logf))
    return (y_prompt, y_sample, p_k, p_v, p_logf, p_conv_b, p_conv_d, p_conv_ffn,
            s_k, s_v, s_logf, s_conv_b, s_vc, s_conv_d, s_conv_ffn)
```

```python
import os
import numpy as np
from contextlib import ExitStack
DBG = float(os.environ.get("MK_DBG", "9"))
import concourse.bass as bass
import concourse.mybir as mybir
from concourse.bass_utils import run_bass_kernel_spmd

F32 = mybir.dt.float32
BF16 = mybir.dt.bfloat16
ALU = mybir.AluOpType
AF = mybir.ActivationFunctionType
AX = mybir.AxisListType

D = 1024
DFF = 2816
PAST = 4096
DEC = 64
EPS = 1e-6
OFF_BG = 1544
NEG = -30000.0


class Sched:
    NDS = 4

    def __init__(self, nc):
        self.nc = nc
        self.eng = {'pe': nc.tensor, 'act': nc.scalar, 'dve': nc.vector, 'pool': nc.gpsimd, 'sp': nc.sync}
        self.sem = {}
        self.cnt = {}
        for e in ['pe', 'act', 'dve', 'pool']:
            self.sem[e] = nc.alloc_semaphore(name=f"s_{e}")
            self.cnt[e] = 0
        self.dq = ['sp', 'pool', 'act']
        for q in self.dq:
            for j in range(self.NDS):
                self.sem[(q, j)] = nc.alloc_semaphore(name=f"d_{q}{j}")
                self.cnt[(q, j)] = 0
        self.dcount = {q: 0 for q in self.dq}
        self.seen = {e: {} for e in self.eng}
        self.lastw = {}
        self.readers = {}
        self.nins = 0
        self.ev = {e: [] for e in self.eng}

    def simulate(self):
        val = {k: 0 for k in self.sem}
        pc = {e: 0 for e in self.ev}
        prog = True
        while prog:
            prog = False
            for e, lst in self.ev.items():
                while pc[e] < len(lst):
                    kind, s, v, info = lst[pc[e]]
                    if kind == 'wait':
                        if val[s] >= v:
                            pc[e] += 1
                            prog = True
                        else:
                            break
                    else:
                        val[s] += v
                        pc[e] += 1
                        prog = True
        stuck = {e: (pc[e], len(l), l[pc[e]] if pc[e] < len(l) else None) for e, l in self.ev.items()}
        ok = all(pc[e] == len(l) for e, l in self.ev.items())
        print("SIM", "OK" if ok else "DEADLOCK", stuck if not ok else "")
        if not ok:
            print({k: v for k, v in val.items()})
        return ok

    def _deps(self, reads, writes):
        deps = set()
        for k in reads:
            if k in self.lastw:
                deps.add(self.lastw[k])
        for k in writes:
            if k in self.lastw:
                deps.add(self.lastw[k])
            for r in self.readers.get(k, ()):
                deps.add(r)
        return deps

    def _wait(self, e, deps):
        need = {}
        for (s, c) in deps:
            if c > need.get(s, 0):
                need[s] = c
        for s, c in need.items():
            if self.seen[e].get(s, 0) >= c:
                continue
            unit = 16 if isinstance(s, tuple) else 1
            self.eng[e].wait_ge(self.sem[s], c * unit)
            self.ev[e].append(('wait', s, c * unit, None))
            self.seen[e][s] = c

    def _record(self, tok, reads, writes):
        for k in reads:
            lst = self.readers.setdefault(k, [])
            lst.append(tok)
            if len(lst) > 64:
                best = {}
                for (s, c) in lst:
                    if c > best.get(s, 0):
                        best[s] = c
                self.readers[k] = [(s, c) for s, c in best.items()]
        for k in writes:
            self.lastw[k] = tok
            self.readers[k] = []

    def op(self, e, fn, reads=(), writes=(), signal=True):
        deps = self._deps(reads, writes)
        if e == 'pe':
            deps = {d for d in deps if d[0] != 'pe'}
        self._wait(e, deps)
        ins = fn()
        tok = (e, self.cnt[e] + 1)
        if signal:
            ins.then_inc(self.sem[e], 1)
            self.cnt[e] += 1
            self.ev[e].append(('inc', e, 1, self.nins))
        self._record(tok, reads, writes)
        self.nins += 1
        return ins

    def dma(self, q, out, in_, reads=(), writes=(), **kw):
        if q == 'act' and os.environ.get("MK_ACTQ", "sp") != "act":
            q = os.environ.get("MK_ACTQ", "sp")
        deps = self._deps(reads, writes)
        j = self.dcount[q] % self.NDS
        self.dcount[q] += 1
        s = (q, j)
        if self.cnt[s] > 0:
            deps = set(deps)
            deps.add((s, self.cnt[s]))
        self._wait(q, deps)
        ins = self.eng[q].dma_start(out=out, in_=in_, **kw)
        ins.then_inc(self.sem[s], 16)
        self.cnt[s] += 1
        self.ev[q].append(('inc', s, 16, self.nins))
        tok = (s, self.cnt[s])
        self._record(tok, reads, writes)
        self.nins += 1
        return ins

    def barrier(self):
        deps = set()
        for s in self.cnt:
            if self.cnt[s] > 0:
                deps.add((s, self.cnt[s]))
        for e in ['pe', 'act', 'dve', 'pool', 'sp']:
            self._wait(e, deps)

    def finish(self, e='sp'):
        deps = set()
        for k, t in self.lastw.items():
            deps.add(t)
        for s in self.cnt:
            if self.cnt[s] > 0:
                deps.add((s, self.cnt[s]))
        self._wait(e, deps)


class Stream:
    pass


def build(NT=32, NL=4, do_sample=True):
    nc = bass.Bass("TRN2", target_bir_lowering=False)
    NTOK = NT * 512

    def din(name, shape):
        return nc.dram_tensor(name, list(shape), F32, kind="ExternalInput").ap()

    def dout(name, shape):
        return nc.dram_tensor(name, list(shape), F32, kind="ExternalOutput").ap()

    def dscr(name, shape, dt=BF16):
        return nc.dram_tensor(name, list(shape), dt, kind="Internal").ap()

    I = dict(
        x_prompt=din("x_prompt", [NTOK, D]), x_sample=din("x_sample", [DEC, D]),
        cache_k=din("cache_k", [2, PAST, 512]), cache_v=din("cache_v", [2, PAST, 512]),
        cache_logf=din("cache_logf", [2, PAST, 8]),
        state_conv_b=din("state_conv_b", [4, 512]), state_conv_d=din("state_conv_d", [60, 512]),
        state_conv_ffn=din("state_conv_ffn", [8, 5632]),
        g_mix=din("g_mix", [4, D]), w_in_even=din("w_in_even", [2, D, 3080]), b_f=din("b_f", [2, 8]),
        g_q=din("g_q", [2, 64]), g_k=din("g_k", [2, 64]), conv_b=din("conv_b", [6, 512]),
        w_out_even=din("w_out_even", [2, D, D]), w_in_odd=din("w_in_odd", [2, D, 2048]),
        g_vc=din("g_vc", [2, 512]), w_s=din("w_s", [2, 4, 128, 128]), b_s=din("b_s", [2, 4, 128]),
        conv_d=din("conv_d", [62, 512]), g_d=din("g_d", [2, 512]), w_out_odd=din("w_out_odd", [2, D, D]),
        g_ffn=din("g_ffn", [4, D]), w_up=din("w_up", [4, D, 5632]), conv_ffn=din("conv_ffn", [12, 5632]),
        w_down=din("w_down", [4, DFF, D]),
    )
    O = dict(
        y_p=dout("y_p", [NTOK, D]), y_s=dout("y_s", [DEC, D]),
        p_k=dout("p_k", [2, NTOK, 512]), p_v=dout("p_v", [2, NTOK, 512]), p_lf=dout("p_lf", [2, NTOK, 8]),
        p_cb=dout("p_cb", [4, 512]), p_cd=dout("p_cd", [60, 512]), p_cf=dout("p_cf", [8, 5632]),
        s_k=dout("s_k", [2, DEC, 512]), s_v=dout("s_v", [2, DEC, 512]), s_lf=dout("s_lf", [2, DEC, 8]),
        s_cb=dout("s_cb", [4, 512]), s_vc=dout("s_vc", [2, DEC, 512]), s_cd=dout("s_cd", [60, 512]),
        s_cf=dout("s_cf", [8, 5632]),
    )
    WB = dict(
        wie=dscr("wie_b", [2, D, 3080]), woe=dscr("woe_b", [2, D, D]), wio=dscr("wio_b", [2, D, 2048]),
        woo=dscr("woo_b", [2, D, D]), wup=dscr("wup_b", [4, D, 5632]), wdn=dscr("wdn_b", [4, DFF, D]),
    )
    cq_scr = dscr("cq_scr", [8, 3, 512])

    S = Sched(nc)
    with ExitStack() as es:
        def SB(name, shape, dt=F32):
            return es.enter_context(nc.sbuf_tensor(name, list(shape), dt))

        ps = [es.enter_context(nc.psum_tensor(f"ps{i}", [128, 512], F32)) for i in range(8)]
        rot = {'gen': [0, [0, 1, 2]], 'S': [0, [3, 4]], 'O': [0, [5, 6]]}

        def bank(kind):
            r = rot[kind]
            b = r[1][r[0] % len(r[1])]
            r[0] += 1
            return b

        def P(b):
            return ('ps', b)

        def mm(out, lhsT, rhs, start, stop, reads, writes, sig=None):
            S.op('pe', lambda: nc.tensor.matmul(out, lhsT=lhsT, rhs=rhs, start=start, stop=stop),
                 reads=reads, writes=writes, signal=(stop if sig is None else sig))

        def act(out, in_, func, reads, writes, **kw):
            S.op('act', lambda: nc.scalar.activation(out, in_, func, **kw), reads=reads, writes=writes)

        def V(e, name, *args, reads, writes, **kw):
            eng = nc.vector if e == 'dve' else nc.gpsimd
            S.op(e, lambda: getattr(eng, name)(*args, **kw), reads=reads, writes=writes)

        def fma(e, acc, src, wcol, rkeys, akey):
            V('dve', 'scalar_tensor_tensor', acc, src, wcol, acc, reads=rkeys + [akey], writes=[akey], op0=ALU.mult, op1=ALU.add)

        ones_f = SB("ones_f", [128, 128])
        ident_f = SB("ident_f", [128, 128])
        utri_f = SB("utri_f", [128, 128])
        ident_b = SB("ident_b", [128, 128], BF16)
        ones_b = SB("ones_b", [128, 128], BF16)
        zeros_b = SB("zeros_b", [128, 512], BF16)
        dmask = SB("dmask", [128, 4, 512], BF16)
        ones3 = SB("ones3", [8, 3, 512], BF16)
        V('pool', 'memset', ones_f[:], 1.0, reads=[], writes=['ones_f'])
        V('pool', 'memset', ones_b[:], 1.0, reads=[], writes=['ones_b'])
        V('pool', 'memset', zeros_b[:], 0.0, reads=[], writes=['zeros_b'])
        V('pool', 'memset', ones3[:], 1.0, reads=[], writes=['ones3'])
        S.op('pool', lambda: nc.gpsimd.affine_select(ident_f[:], ones_f[:], [[-1, 128]], ALU.is_equal, 0.0, base=0, channel_multiplier=1),
             reads=['ones_f'], writes=['ident_f'])
        S.op('pool', lambda: nc.gpsimd.affine_select(utri_f[:], ones_f[:], [[1, 128]], ALU.is_ge, 0.0, base=0, channel_multiplier=-1),
             reads=['ones_f'], writes=['utri_f'])
        V('pool', 'tensor_copy', ident_b[:], ident_f[:], reads=['ident_f'], writes=['ident_b'])
        for r in range(4):
            S.op('pool', lambda: nc.gpsimd.affine_select(dmask[:, r, :], zeros_b[:], [[1, 512]], ALU.is_ge, NEG, base=-128 * r, channel_multiplier=-1),
                 reads=['zeros_b'], writes=['dmask'])

        cx = SB("cx", [128, 4, 512])
        ob = SB("ob", [128, 4, 512], BF16)
        castf = cx[:, :, :].rearrange("p (a b) c -> p a (b c)", a=2)
        castb = ob[:, :, :].rearrange("p (a b) c -> p a (b c)", a=2)
        ci = [0]

        def cast_weight(src, dst, rows, cols):
            for r0 in range(0, rows, 128):
                for c0 in range(0, cols, 1024):
                    cw = min(1024, cols - c0)
                    b = ci[0] % 2
                    e = ['dve', 'pool'][ci[0] % 2]
                    ci[0] += 1
                    S.dma('sp', castf[:, b, :cw], src[r0:r0 + 128, c0:c0 + cw], reads=[], writes=[('castf', b)])
                    V(e, 'tensor_copy', castb[:, b, :cw], castf[:, b, :cw], reads=[('castf', b)], writes=[('castb', b)])
                    S.dma('act', dst[r0:r0 + 128, c0:c0 + cw], castb[:, b, :cw], reads=[('castb', b)], writes=['WB'])

        for i2 in range(2):
            cast_weight(I['w_in_even'][i2], WB['wie'][i2], D, 3080)
            cast_weight(I['w_out_even'][i2], WB['woe'][i2], D, D)
            cast_weight(I['w_in_odd'][i2], WB['wio'][i2], D, 2048)
            cast_weight(I['w_out_odd'][i2], WB['woo'][i2], D, D)
        for l in range(4):
            cast_weight(I['w_up'][l], WB['wup'][l], D, 5632)
            cast_weight(I['w_down'][l], WB['wdn'][l], DFF, D)

        S.barrier()
        xstage = SB("xstage", [128, 1024])
        stage = xstage

        def load_T(src, R, C, dst3, key):
            ncn = C // 128
            per = min(512 // R, 8)
            for c0 in range(0, ncn, per):
                n = min(per, ncn - c0)
                S.dma('sp', stage[:R, :n * 128], src[:, c0 * 128:(c0 + n) * 128], reads=[], writes=['xstage'])
                b = bank('gen')
                for c in range(n):
                    mm(ps[b][:, c * R:(c + 1) * R], stage[:R, c * 128:(c + 1) * 128], ident_f[:R, :R], True, True,
                       ['xstage', 'ident_f'], [P(b)])
                act(dst3[:, c0:c0 + n, :], ps[b][:, :n * R].rearrange("p (c r) -> p c r", r=R), AF.Copy, [P(b)], [key])

        gmix = SB("gmix", [128, 8, 4]); load_T(I['g_mix'], 4, D, gmix, 'gmix')
        gffn = SB("gffn", [128, 8, 4]); load_T(I['g_ffn'], 4, D, gffn, 'gffn')
        gd = SB("gd", [128, 4, 2]); load_T(I['g_d'], 2, 512, gd, 'gd')
        wB = SB("wB", [128, 4, 6]); load_T(I['conv_b'], 6, 512, wB, 'wB')
        wD = SB("wD", [128, 4, 62]); load_T(I['conv_d'], 62, 512, wD, 'wD')
        wF = SB("wF", [128, 44, 12]); load_T(I['conv_ffn'], 12, 5632, wF, 'wF')

        gq_t = SB("gq_t", [128, 2, 64]); gk_t = SB("gk_t", [128, 2, 64])
        gq_bc = SB("gq_bc", [128, 512]); gk_bc = SB("gk_bc", [128, 512])
        gvc_bc = SB("gvc_bc", [128, 512]); bs_bc = SB("bs_bc", [128, 2, 4, 128]); bf_bc = SB("bf_bc", [128, 2, 8])
        for i2 in range(2):
            S.dma('sp', gq_t[:, i2, :], I['g_q'][i2].partition_broadcast(128), reads=[], writes=['gq_t'])
            S.dma('sp', gk_t[:, i2, :], I['g_k'][i2].partition_broadcast(128), reads=[], writes=['gk_t'])
            S.dma('sp', bf_bc[:, i2, :], I['b_f'][i2].partition_broadcast(128), reads=[], writes=['bf_bc'])
            for g in range(4):
                S.dma('sp', bs_bc[:, i2, g, :], I['b_s'][i2, g].partition_broadcast(128), reads=[], writes=['bs_bc'])

        def load_gqk(i2):
            V('dve', 'tensor_scalar', gq_bc[:, :].rearrange("p (h d) -> p h d", h=8), gq_t[:, i2, :].unsqueeze(1).to_broadcast([128, 8, 64]),
              0.125, None, reads=['gq_t'], writes=['gq_bc'], op0=ALU.mult)
            V('dve', 'tensor_scalar', gk_bc[:, :].rearrange("p (h d) -> p h d", h=8), gk_t[:, i2, :].unsqueeze(1).to_broadcast([128, 8, 64]),
              1.0, None, reads=['gk_t'], writes=['gk_bc'], op0=ALU.mult)
        wf_f = SB("wf_f", [128, 2, 8, 8]); wf_b = SB("wf_b", [128, 2, 8, 8], BF16)
        for i2 in range(2):
            S.dma('sp', wf_f[:, i2, :, :], I['w_in_even'][i2].rearrange("(kc p) n -> p kc n", p=128)[:, :, 1536:1544], reads=[], writes=['wf_f'])
        V('dve', 'tensor_copy', wf_b[:], wf_f[:], reads=['wf_f'], writes=['wf_b'])
        wsT = SB("wsT", [128, 2, 4, 128], BF16)
        wsf = SB("wsf", [128, 128])
        for i2 in range(2):
            for g in range(4):
                S.dma('sp', wsf[:], I['w_s'][i2, g], reads=[], writes=['wsf'])
                S.op('pool', lambda: nc.gpsimd.affine_select(wsf[:], wsf[:], [[-1, 128]], ALU.is_ge, 0.0, base=0, channel_multiplier=1),
                     reads=['wsf'], writes=['wsf'])
                b = bank('gen')
                mm(ps[b][:, :128], wsf[:], ident_f[:], True, True, ['wsf', 'ident_f'], [P(b)])
                act(wsT[:, i2, g, :], ps[b][:, :128], AF.Copy, [P(b)], ['wsT'])

        NWB = 4
        wbuf = [SB(f"wbuf{i}", [128, 4096], BF16) for i in range(NWB)]
        wrr = [0]

        def wreq(src3, npart, a, b):
            i = wrr[0] % NWB
            wrr[0] += 1
            view = wbuf[i][:npart, :a * b].rearrange("p (a b) -> p a b", a=a)
            S.dma('sp', view, src3, reads=['WB'], writes=[('wbuf', i)])
            return view, ('wbuf', i)

        xT = SB("xT", [128, 8, 512])
        xn = SB("xn", [128, 8, 512], BF16)
        sq = SB("sq", [128, 2, 512], BF16)
        rstd = SB("rstd", [128, 512])
        gbuf = SB("gbuf", [128, 11, 512], BF16)
        hb = [SB(f"hb{i}", [128, 514]) for i in range(2)]
        ca = [SB(f"ca{i}", [128, 512]) for i in range(2)]
        print("SBUF remaining after hb/ca:", nc.sbuf_bytes_remaining)
        sgt = SB("sgt", [128, 512])
        tmpf = SB("tmpf", [128, 512]); sqh = tmpf; ssq = SB("ssq", [128, 8]); qf = SB("qf", [128, 512]); kf = SB("kf", [128, 512])
        vf = SB("vf", [128, 512]); qb = SB("qb", [128, 512], BF16); kb = SB("kb", [128, 512], BF16)
        lz = SB("lz", [128, 8]); logf = SB("logf", [128, 4, 8])
        vb = SB("vb", [128, 1, 8, 65], BF16)
        V('pool', 'memset', vb[:], 1.0, reads=[], writes=['vb'])
        qT = SB("qT", [70, 8, 512], BF16)
        V('pool', 'memset', qT[:], 1.0, reads=[], writes=['qT'])
        kT = SB("kT", [64, 8, 128], BF16)
        cc = SB("cc", [8, 512]); cr = SB("cr", [8, 512]); caug = SB("caug", [8, 3, 512], BF16); ncaug = SB("ncaug", [8, 3, 512], BF16)
        KB = 1024
        kbuf = [SB(f"kbuf{i}", [70, KB + 512], BF16) for i in range(2)]
        vbuf = [SB(f"vbuf{i}", [128, KB // 128 + 4, 65], BF16) for i in range(2)]
        pT = [SB(f"pT{i}", [128, 512], BF16) for i in range(3)]
        osb = SB("osb", [65, 512]); rec = osb; bcs = SB("bcs", [64, 512])
        oa = SB("oa", [64, 8, 512], BF16)
        ub = SB("ub", [128, 4, 544])
        tmpg = SB("tmpg", [128, 512])
        oc = SB("oc", [128, 4, 512], BF16)
        ug = SB("ug", [128, 4, 512], BF16)
        zf = qf; vcb = kb
        rs1 = SB("rs1", [128, 8])
        ostage = tmpg

        kvrr = [0]
        ptrr = [0]

        def make_stream(sid, T, ntok_scr):
            st = Stream()
            st.sid = sid
            st.T = T
            st.TS = min(T, 128)
            st.NS = T // st.TS
            st.KT = [dscr(f"KT_{sid}_{i}", [8, 70, ntok_scr]) for i in range(2)]
            st.Vs = [dscr(f"V_{sid}_{i}", [8, 128, ntok_scr // 128, 65]) for i in range(2)]
            st.histF = SB(f"histF_{sid}", [128, 44, 8])
            st.histB = SB(f"histB_{sid}", [128, 4, 4])
            st.histD = SB(f"histD_{sid}", [128, 4, 60])
            st.carry = SB(f"carry_{sid}", [8, 2])
            st.kF, st.kB, st.kD, st.kC = f"histF_{sid}", f"histB_{sid}", f"histD_{sid}", f"carry_{sid}"
            st.o0 = 0
            return st

        def rmsnorm_x(st, gt, l):
            T = st.T
            for kc in range(8):
                b = kc % 2
                act(sq[:, b, :T], xT[:, kc, :T], AF.Square, ['xT'], [('sq', b)])
                mm(ps[7][:, :T], ones_b[:], sq[:, b, :T], kc == 0, kc == 7, [('sq', b), 'ones_b'], [P(7)], sig=True)
            V('dve', 'tensor_scalar', rstd[:, :T], ps[7][:, :T], 1.0 / D, EPS, reads=[P(7)], writes=['rstd'], op0=ALU.mult, op1=ALU.add)
            act(rstd[:, :T], rstd[:, :T], AF.Sqrt, ['rstd'], ['rstd'])
            V('dve', 'reciprocal', rstd[:, :T], rstd[:, :T], reads=['rstd'], writes=['rstd'])
            for kc in range(8):
                V('dve', 'scalar_tensor_tensor', xn[:, kc, :T], xT[:, kc, :T], gt[:, kc, l:l + 1], rstd[:, :T],
                  reads=['xT', 'rstd'], writes=['xn'], op0=ALU.mult, op1=ALU.mult)

        def head_norm(st, b, gbc, outf, okey, fin=None, finkey=None):
            TS = st.TS
            act(sqh[:TS, :], ps[b][:TS, :], AF.Square, [P(b)], ['tmpf'])
            V('dve', 'tensor_reduce', ssq[:TS, :], sqh[:TS, :].rearrange("p (h d) -> p h d", h=8), AX.X, ALU.add, reads=['tmpf'], writes=['ssq'])
            V('dve', 'tensor_scalar', ssq[:TS, :], ssq[:TS, :], 1.0 / 64, EPS, reads=['ssq'], writes=['ssq'], op0=ALU.mult, op1=ALU.add)
            act(ssq[:TS, :], ssq[:TS, :], AF.Sqrt, ['ssq'], ['ssq'])
            V('dve', 'reciprocal', ssq[:TS, :], ssq[:TS, :], reads=['ssq'], writes=['ssq'])
            V('dve', 'tensor_tensor', outf[:TS, :].rearrange("p (h d) -> p h d", h=8), ps[b][:TS, :].rearrange("p (h d) -> p h d", h=8),
              ssq[:TS, :].unsqueeze(2).to_broadcast([TS, 8, 64]), ALU.mult, reads=[P(b), 'ssq'], writes=[okey])
            if fin is None:
                fin, finkey = outf, okey
            V('dve', 'tensor_tensor', fin[:TS, :], outf[:TS, :], gbc[:TS, :], ALU.mult, reads=[okey, 'gq_bc', 'gk_bc'], writes=[finkey])

        def transp_heads(st, src_b, dstT, s, key_src, key_dst):
            TS = st.TS
            for h0 in (0, 4):
                b = bank('gen')
                for hh in range(4):
                    h = h0 + hh
                    mm(ps[b][:64, hh * TS:(hh + 1) * TS], src_b[:TS, h * 64:(h + 1) * 64], ident_b[:TS, :TS], True, True,
                       [key_src, 'ident_b'], [P(b)])
                act(dstT[:64, h0:h0 + 4, s * TS:(s + 1) * TS], ps[b][:64, :4 * TS].rearrange("p (h t) -> p h t", h=4), AF.Copy, [P(b)], [key_dst])

        def kv_finish(st, i2, t0):
            T, TS, NS = st.T, st.TS, st.NS
            for s in range(NS):
                mm(ps[7][:8, s * TS:(s + 1) * TS], logf[:TS, s, :], utri_f[:TS, :TS], True, True, ['logf', 'utri_f'], [P(7)])
            for s in range(NS):
                V('dve', 'tensor_scalar', cc[:8, s * TS:(s + 1) * TS], ps[7][:8, s * TS:(s + 1) * TS], st.carry[:8, i2:i2 + 1], None,
                  reads=[P(7), st.kC], writes=['cc'], op0=ALU.add)
                V('dve', 'tensor_copy', st.carry[:8, i2:i2 + 1], cc[:8, (s + 1) * TS - 1:(s + 1) * TS], reads=['cc'], writes=[st.kC])
            V('dve', 'tensor_copy', caug[:, 0, :T], cc[:, :T], reads=['cc'], writes=['caug'])
            V('dve', 'tensor_tensor', cr[:, :T], cc[:, :T], caug[:, 0, :T], ALU.subtract, reads=['cc', 'caug'], writes=['cr'])
            V('dve', 'tensor_copy', caug[:, 1, :T], cr[:, :T], reads=['cr'], writes=['caug'])
            V('dve', 'tensor_tensor', cr[:, :T], cr[:, :T], caug[:, 1, :T], ALU.subtract, reads=['cr', 'caug'], writes=['cr'])
            V('dve', 'tensor_copy', caug[:, 2, :T], cr[:, :T], reads=['cr'], writes=['caug'])
            V('dve', 'tensor_scalar', ncaug[:, :, :T], caug[:, :, :T], -1.0, None, reads=['caug'], writes=['ncaug'], op0=ALU.mult)
            kk = ('KV', st.sid, i2)
            S.dma('act', st.KT[i2][:, 64:67, t0:t0 + T], ones3[:, :, :T], reads=['ones3'], writes=[kk])
            S.dma('act', st.KT[i2][:, 67:70, t0:t0 + T], ncaug[:, :, :T], reads=['ncaug'], writes=[kk])

        def attention(st, i2, t0):
            T, TS, NS = st.T, st.TS, st.NS
            kk = ('KV', st.sid, i2)
            S.dma('act', cq_scr[:, :, :T], caug[:, :, :T], reads=['caug'], writes=['cq_scr'])
            S.dma('act', qT[64:67, :, :T], cq_scr[:, :, :T].rearrange("h a t -> a h t"), reads=['cq_scr'], writes=['qT'])
            ntot = t0 + T
            chunks = []
            c0 = 0
            while c0 < ntot:
                c1 = min(c0 + KB, ntot)
                if ntot - c1 <= 512 and ntot - c1 > 0:
                    c1 = ntot
                chunks.append((c0, c1))
                c0 = c1
            for h in range(8):
                obk = bank('O')
                first = True
                for (c0, c1) in chunks:
                    i = kvrr[0] % 2
                    kvrr[0] += 1
                    n = c1 - c0
                    nkt = (n + 127) // 128
                    S.dma('sp', kbuf[i][:, :n], st.KT[i2][h, :, c0:c1], reads=[kk], writes=[('kbuf', i)])
                    S.dma('sp', vbuf[i][:, :nkt, :], st.Vs[i2][h, :, c0 // 128:c0 // 128 + nkt, :], reads=[kk], writes=[('vbuf', i)])
                    col = 0
                    while col < n:
                        gpos = c0 + col
                        if gpos < t0:
                            ksz, r = 128, None
                        else:
                            ksz, r = TS, (gpos - t0) // TS
                        last = (gpos + ksz >= ntot)
                        sb_ = bank('S')
                        mm(ps[sb_][:ksz, :T], kbuf[i][:70, col:col + ksz], qT[:70, h, :T], True, r is None, [('kbuf', i), 'qT'], [P(sb_)])
                        if r is not None:
                            mm(ps[sb_][:ksz, :T], ident_b[:ksz, :ksz], dmask[:ksz, r, :T], False, True, ['ident_b', 'dmask'], [P(sb_)])
                        pi = ptrr[0] % 3
                        ptrr[0] += 1
                        act(pT[pi][:ksz, :T], ps[sb_][:ksz, :T], AF.Exp, [P(sb_)], [('pT', pi)])
                        mm(ps[obk][:65, :T], vbuf[i][:ksz, col // 128, :], pT[pi][:ksz, :T], first, last, [('vbuf', i), ('pT', pi)], [P(obk)])
                        first = False
                        col += ksz
                act(osb[:65, :T], ps[obk][:65, :T], AF.Copy, [P(obk)], ['osb'])
                V('dve', 'reciprocal', osb[64:65, :T], osb[64:65, :T], reads=['osb'], writes=['osb'])
                mm(ps[7][:64, :T], ones_f[64:65, :64], osb[64:65, :T], True, True, ['osb', 'ones_f'], [P(7)])
                act(bcs[:64, :T], ps[7][:64, :T], AF.Copy, [P(7)], ['bcs'])
                V('dve', 'tensor_tensor', oa[:64, h, :T], osb[:64, :T], bcs[:64, :T], ALU.mult, reads=['osb', 'bcs'], writes=['oa'])

        def residual_add(st, b, dc):
            T = st.T
            V('dve', 'tensor_tensor', xT[:, dc, :T], xT[:, dc, :T], ps[b][:, :T], ALU.add, reads=['xT', P(b)], writes=['xT'])

        def even_layer(st, l, t0, O_k, O_v, O_lf):
            T, TS, NS = st.T, st.TS, st.NS
            i2 = l // 2
            wie3 = WB['wie'][i2].rearrange("(kc p) n -> p kc n", p=128)
            rmsnorm_x(st, gmix, l)
            load_gqk(i2)
            kk = ('KV', st.sid, i2)
            kt0 = t0 // 128
            o0 = st.o0
            Wq, kq = wreq(wie3[:, :, 0:512], 128, 8, 512)
            Wk, kk_ = wreq(wie3[:, :, 512:1024], 128, 8, 512)
            Wv, kv_ = wreq(wie3[:, :, 1024:1536], 128, 8, 512)
            for s in range(NS):
                ts = slice(s * TS, (s + 1) * TS)
                b = bank('gen')
                for kc in range(8):
                    mm(ps[b][:TS, :], xn[:, kc, ts], Wq[:, kc, :], kc == 0, kc == 7, ['xn', kq], [P(b)])
                head_norm(st, b, gq_bc, qf, 'qf', qb, 'qb')
                transp_heads(st, qb, qT, s, 'qb', 'qT')
                if DBG < 0.5:
                    continue
                b = bank('gen')
                for kc in range(8):
                    mm(ps[b][:TS, :], xn[:, kc, ts], Wk[:, kc, :], kc == 0, kc == 7, ['xn', kk_], [P(b)])
                head_norm(st, b, gk_bc, kf, 'kf')
                S.dma('act', O_k[i2, o0 + s * TS:o0 + (s + 1) * TS, :], kf[:TS, :], reads=['kf'], writes=['O_k'])
                act(kb[:TS, :], kf[:TS, :], AF.Copy, ['kf'], ['kb'])
                transp_heads(st, kb, kT, 0, 'kb', 'kT')
                S.dma('act', st.KT[i2][:, 0:64, t0 + s * TS:t0 + (s + 1) * TS].rearrange("h d t -> d h t"), kT[:64, :, :TS], reads=['kT'], writes=[kk])
                if DBG < 0.7:
                    continue
                b = bank('gen')
                for kc in range(8):
                    mm(ps[b][:TS, :], xn[:, kc, ts], Wv[:, kc, :], kc == 0, kc == 7, ['xn', kv_], [P(b)])
                act(vf[:TS, :], ps[b][:TS, :], AF.Copy, [P(b)], ['vf'])
                V('dve', 'tensor_copy', vb[:TS, 0, :, 0:64], vf[:TS, :].rearrange("p (h d) -> p h d", h=8), reads=['vf'], writes=['vb'])
                S.dma('act', O_v[i2, o0 + s * TS:o0 + (s + 1) * TS, :], vf[:TS, :], reads=['vf'], writes=['O_v'])
                S.dma('act', st.Vs[i2][:, 0:TS, kt0 + s, :].rearrange("h p d -> p h d"), vb[:TS, 0, :, :], reads=['vb'], writes=[kk])
                if DBG < 0.9:
                    continue
                for kc in range(8):
                    mm(ps[7][:TS, 0:8], xn[:, kc, ts], wf_b[:, i2, kc, :], kc == 0, kc == 7, ['xn', 'wf_b'], [P(7)])
                V('dve', 'tensor_tensor', lz[:TS, :], ps[7][:TS, 0:8], bf_bc[:TS, i2, :], ALU.add, reads=[P(7), 'bf_bc'], writes=['lz'])
                act(lz[:TS, :], lz[:TS, :], AF.Exp, ['lz'], ['lz'], scale=-1.0)
                act(lz[:TS, :], lz[:TS, :], AF.Ln, ['lz'], ['lz'], bias=1.0)
                V('dve', 'tensor_scalar', logf[:TS, s, :], lz[:TS, :], -1.0, None, reads=['lz'], writes=['logf'], op0=ALU.mult)
                S.dma('act', O_lf[i2, o0 + s * TS:o0 + (s + 1) * TS, :], logf[:TS, s, :], reads=['logf'], writes=['O_lf'])
            if DBG < 2:
                return
            kv_finish(st, i2, t0)
            if DBG < 3:
                return
            attention(st, i2, t0)
            if DBG < 4:
                return
            Wbg, kbg = wreq(wie3[:, :, OFF_BG:OFF_BG + 512], 128, 8, 512)
            Wcg, kcg = wreq(wie3[:, :, OFF_BG + 512:OFF_BG + 1024], 128, 8, 512)
            Wxi, kxi = wreq(wie3[:, :, OFF_BG + 1024:OFF_BG + 1536], 128, 8, 512)
            V('dve', 'tensor_copy', ub[:, :, 0:2], st.histB[:, :, i2 * 2:i2 * 2 + 2], reads=[st.kB], writes=['ub'])
            for c in range(4):
                cs = slice(c * 128, (c + 1) * 128)
                b = bank('gen')
                for kc in range(8):
                    mm(ps[b][:, :T], Wcg[:, kc, cs], xn[:, kc, :T], kc == 0, kc == 7, ['xn', kcg], [P(b)])
                act(tmpf[:, :T], ps[b][:, :T], AF.Copy, [P(b)], ['tmpf'])
                b = bank('gen')
                for kc in range(8):
                    mm(ps[b][:, :T], Wxi[:, kc, cs], xn[:, kc, :T], kc == 0, kc == 7, ['xn', kxi], [P(b)])
                V('dve', 'tensor_tensor', ub[:, c, 2:2 + T], tmpf[:, :T], ps[b][:, :T], ALU.mult, reads=['tmpf', P(b)], writes=['ub'])
                V('dve', 'tensor_scalar', cx[:, c, :T], ub[:, c, 2:2 + T], wB[:, c, i2 * 3 + 2:i2 * 3 + 3], None, reads=['ub'], writes=[('cx', c)], op0=ALU.mult)
                for j in (1, 0):
                    fma('dve', cx[:, c, :T], ub[:, c, j:j + T], wB[:, c, i2 * 3 + j:i2 * 3 + j + 1], ['ub'], ('cx', c))
                b = bank('gen')
                for kc in range(8):
                    mm(ps[b][:, :T], Wbg[:, kc, cs], xn[:, kc, :T], kc == 0, kc == 7, ['xn', kbg], [P(b)])
                V('dve', 'tensor_tensor', ob[:, c, :T], ps[b][:, :T], cx[:, c, :T], ALU.mult, reads=[P(b), ('cx', c)], writes=['ob'])
            V('dve', 'tensor_copy', st.histB[:, :, i2 * 2:i2 * 2 + 2], ub[:, :, T:T + 2], reads=['ub'], writes=[st.kB])
            if DBG < 5:
                return
            woA = WB['woe'][i2][0:512, :].rearrange("(h d) n -> d h n", d=64)
            woB = WB['woe'][i2][512:1024, :].rearrange("(c p) n -> p c n", p=128)
            WA0, kA0 = wreq(woA[:, :, 0:512], 64, 8, 512)
            WA1, kA1 = wreq(woA[:, :, 512:1024], 64, 8, 512)
            WBo, kBo = wreq(woB, 128, 4, 1024)
            for dc in range(8):
                b = bank('gen')
                WA, kA = (WA0, kA0) if dc < 4 else (WA1, kA1)
                dsl = slice((dc % 4) * 128, (dc % 4 + 1) * 128)
                for h in range(8):
                    mm(ps[b][:, :T], WA[:64, h, dsl], oa[:64, h, :T], h == 0, False, ['oa', kA], [P(b)])
                for c in range(4):
                    mm(ps[b][:, :T], WBo[:, c, dc * 128:(dc + 1) * 128], ob[:, c, :T], False, c == 3, ['ob', kBo], [P(b)])
                residual_add(st, b, dc)

        def gelu_from_psum(b, npart, T, outf, key):
            act(tmpf[:npart, :T], ps[b][:npart, :T], AF.Square, [P(b)], ['tmpf'])
            V('dve', 'tensor_scalar', tmpf[:npart, :T], tmpf[:npart, :T], 0.044715, 1.0, reads=['tmpf'], writes=['tmpf'], op0=ALU.mult, op1=ALU.add)
            V('dve', 'tensor_tensor', tmpf[:npart, :T], tmpf[:npart, :T], ps[b][:npart, :T], ALU.mult, reads=['tmpf', P(b)], writes=['tmpf'])
            act(tmpf[:npart, :T], tmpf[:npart, :T], AF.Sigmoid, ['tmpf'], ['tmpf'], scale=1.5957691216057308)
            V('dve', 'tensor_tensor', outf, tmpf[:npart, :T], ps[b][:npart, :T], ALU.mult, reads=['tmpf', P(b)], writes=[key])

        def odd_layer(st, l, t0, O_vc):
            T, TS, NS = st.T, st.TS, st.NS
            i2 = l // 2
            wio3 = WB['wio'][i2].rearrange("(kc p) n -> p kc n", p=128)
            rmsnorm_x(st, gmix, l)
            S.dma('sp', gvc_bc[:, :], I['g_vc'][i2].partition_broadcast(128), reads=[], writes=['gvc_bc'])
            o0 = st.o0
            Wu, ku = wreq(wio3[:, :, 0:512], 128, 8, 512)
            Wvc, kvc = wreq(wio3[:, :, 512:1024], 128, 8, 512)
            for c in range(4):
                b = bank('gen')
                for kc in range(8):
                    mm(ps[b][:, :T], Wu[:, kc, c * 128:(c + 1) * 128], xn[:, kc, :T], kc == 0, kc == 7, ['xn', ku], [P(b)])
                gelu_from_psum(b, 128, T, ug[:, c, :T], 'ug')
            for s in range(NS):
                ts = slice(s * TS, (s + 1) * TS)
                b = bank('gen')
                for kc in range(8):
                    mm(ps[b][:TS, :], xn[:, kc, ts], Wvc[:, kc, :], kc == 0, kc == 7, ['xn', kvc], [P(b)])
                gelu_from_psum(b, TS, 512, zf[:TS, :], 'qf')
                act(sqh[:TS, :], zf[:TS, :], AF.Square, ['qf'], ['tmpf'])
                V('dve', 'tensor_reduce', rs1[:TS, 0:1], sqh[:TS, :], AX.X, ALU.add, reads=['tmpf'], writes=['rs1'])
                V('dve', 'tensor_scalar', rs1[:TS, 0:1], rs1[:TS, 0:1], 1.0 / 512, EPS, reads=['rs1'], writes=['rs1'], op0=ALU.mult, op1=ALU.add)
                act(rs1[:TS, 0:1], rs1[:TS, 0:1], AF.Sqrt, ['rs1'], ['rs1'])
                V('dve', 'reciprocal', rs1[:TS, 0:1], rs1[:TS, 0:1], reads=['rs1'], writes=['rs1'])
                V('dve', 'scalar_tensor_tensor', zf[:TS, :], zf[:TS, :], rs1[:TS, 0:1], gvc_bc[:TS, :], reads=['qf', 'rs1', 'gvc_bc'], writes=['qf'],
                  op0=ALU.mult, op1=ALU.mult)
                if O_vc is not None:
                    S.dma('act', O_vc[i2, o0 + s * TS:o0 + (s + 1) * TS, :], zf[:TS, :], reads=['qf'], writes=['O_vc'])
                act(vcb[:TS, :], zf[:TS, :], AF.Copy, ['qf'], ['kb'])
                b = bank('gen')
                for g in range(4):
                    mm(ps[b][:, g * TS:(g + 1) * TS], vcb[:TS, g * 128:(g + 1) * 128], wsT[:TS, i2, g, :TS], True, True, ['kb', 'wsT'], [P(b)])
                V('dve', 'tensor_tensor', tmpg[:, :4 * TS].rearrange("p (g t) -> p g t", g=4), ps[b][:, :4 * TS].rearrange("p (g t) -> p g t", g=4),
                  bs_bc[:, i2, :, :TS], ALU.add, reads=[P(b), 'bs_bc'], writes=['tmpg'])
                V('dve', 'tensor_tensor', oc[:, :, ts], tmpg[:, :4 * TS].rearrange("p (g t) -> p g t", g=4), ug[:, :, ts], ALU.mult,
                  reads=['tmpg', 'ug'], writes=['oc'])
            Wad, kad = wreq(wio3[:, :, 1024:1536], 128, 8, 512)
            Wgd, kgd = wreq(wio3[:, :, 1536:2048], 128, 8, 512)
            V('dve', 'tensor_copy', ub[:, :, 0:30], st.histD[:, :, i2 * 30:i2 * 30 + 30], reads=[st.kD], writes=['ub'])
            for c in range(4):
                cs = slice(c * 128, (c + 1) * 128)
                b = bank('gen')
                for kc in range(8):
                    mm(ps[b][:, :T], Wad[:, kc, cs], xn[:, kc, :T], kc == 0, kc == 7, ['xn', kad], [P(b)])
                act(tmpf[:, :T], ps[b][:, :T], AF.Copy, [P(b)], ['tmpf'])
                b = bank('gen')
                for kc in range(8):
                    mm(ps[b][:, :T], Wgd[:, kc, cs], xn[:, kc, :T], kc == 0, kc == 7, ['xn', kgd], [P(b)])
                act(tmpg[:, :T], ps[b][:, :T], AF.Sigmoid, [P(b)], ['tmpg'])
                V('dve', 'tensor_tensor', ub[:, c, 30:30 + T], tmpf[:, :T], tmpg[:, :T], ALU.mult, reads=['tmpf', 'tmpg'], writes=['ub'])
                e = 'dve'
                V(e, 'tensor_scalar', cx[:, c, :T], ub[:, c, 30:30 + T], wD[:, c, i2 * 31 + 30:i2 * 31 + 31], None, reads=['ub'], writes=[('cx', c)], op0=ALU.mult)
                for j in range(30):
                    fma(e, cx[:, c, :T], ub[:, c, j:j + T], wD[:, c, i2 * 31 + j:i2 * 31 + j + 1], ['ub'], ('cx', c))
            V('dve', 'tensor_copy', st.histD[:, :, i2 * 30:i2 * 30 + 30], ub[:, :, T:T + 30], reads=['ub'], writes=[st.kD])
            for c in range(4):
                b2 = c % 2
                act(sq[:, b2, :T], cx[:, c, :T], AF.Square, [('cx', c)], [('sq', b2)])
                mm(ps[7][:, :T], ones_b[:], sq[:, b2, :T], c == 0, c == 3, [('sq', b2), 'ones_b'], [P(7)], sig=True)
            V('dve', 'tensor_scalar', rstd[:, :T], ps[7][:, :T], 1.0 / 512, EPS, reads=[P(7)], writes=['rstd'], op0=ALU.mult, op1=ALU.add)
            act(rstd[:, :T], rstd[:, :T], AF.Sqrt, ['rstd'], ['rstd'])
            V('dve', 'reciprocal', rstd[:, :T], rstd[:, :T], reads=['rstd'], writes=['rstd'])
            for c in range(4):
                V('dve', 'scalar_tensor_tensor', tmpf[:, :T], cx[:, c, :T], gd[:, c, i2:i2 + 1], rstd[:, :T], reads=[('cx', c), 'rstd'], writes=['tmpf'],
                  op0=ALU.mult, op1=ALU.mult)
                act(ob[:, c, :T], tmpf[:, :T], AF.Silu, ['tmpf'], ['ob'])
            woo3 = WB['woo'][i2].rearrange("(c p) n -> p c n", p=128)
            W0, k0 = wreq(woo3[:, 0:4, :], 128, 4, 1024)
            W1, k1 = wreq(woo3[:, 4:8, :], 128, 4, 1024)
            for dc in range(8):
                b = bank('gen')
                for c in range(4):
                    mm(ps[b][:, :T], W0[:, c, dc * 128:(dc + 1) * 128], oc[:, c, :T], c == 0, False, ['oc', k0], [P(b)])
                for c in range(4):
                    mm(ps[b][:, :T], W1[:, c, dc * 128:(dc + 1) * 128], ob[:, c, :T], False, c == 3, ['ob', k1], [P(b)])
                residual_add(st, b, dc)

        def conv3_ffn(st, l, j, b, hbi, e, outap, outkey):
            T = st.T
            h = hb[hbi]
            hk = ('hb', hbi)
            hh = ('hbh', hbi)
            V('dve', 'tensor_copy', h[:, 0:2], st.histF[:, j, l * 2:l * 2 + 2], reads=[st.kF], writes=[hh])
            act(h[:, 2:2 + T], ps[b][:, :T], AF.Copy, [P(b)], [hk])
            act(outap, ps[b][:, :T], AF.Copy, [P(b)], [outkey], scale=wF[:, j, l * 3 + 2:l * 3 + 3])
            for jj in (1, 0):
                fma('dve', outap, h[:, jj:jj + T], wF[:, j, l * 3 + jj:l * 3 + jj + 1], [hk, hh], outkey)
            act(st.histF[:, j, l * 2:l * 2 + 2], h[:, T:T + 2], AF.Copy, [hk], [st.kF])

        def ffn(st, l):
            T = st.T
            rmsnorm_x(st, gffn, l)
            wup3 = WB['wup'][l].rearrange("(kc p) n -> p kc n", p=128)
            wdn3 = WB['wdn'][l].rearrange("(fc p) n -> p fc n", p=128)
            for half in range(2):
                f0 = half * 11
                groups = [(f0, 4), (f0 + 4, 4), (f0 + 8, 3)]
                for (g0, nch) in groups:
                    Wg, kg = wreq(wup3[:, :, g0 * 128:(g0 + nch) * 128], 128, 8, nch * 128)
                    Wv2, kv2 = wreq(wup3[:, :, DFF + g0 * 128:DFF + (g0 + nch) * 128], 128, 8, nch * 128)
                    for jj in range(nch):
                        j = g0 + jj
                        cs = slice(jj * 128, (jj + 1) * 128)
                        bg_ = bank('gen')
                        for kc in range(8):
                            mm(ps[bg_][:, :T], Wg[:, kc, cs], xn[:, kc, :T], kc == 0, kc == 7, ['xn', kg], [P(bg_)])
                        bv_ = bank('gen')
                        for kc in range(8):
                            mm(ps[bv_][:, :T], Wv2[:, kc, cs], xn[:, kc, :T], kc == 0, kc == 7, ['xn', kv2], [P(bv_)])
                        conv3_ffn(st, l, j, bg_, 0, 'dve', ca[0][:, :T], ('ca', 0))
                        conv3_ffn(st, l, 22 + j, bv_, 1, 'pool', ca[1][:, :T], ('ca', 1))
                        act(sgt[:, :T], ca[0][:, :T], AF.Silu, [('ca', 0)], ['sgt'])
                        V('dve', 'tensor_tensor', gbuf[:, j - f0, :T], sgt[:, :T], ca[1][:, :T], ALU.mult, reads=['sgt', ('ca', 1)], writes=[('gbuf', j - f0)])
                Wd = [wreq(wdn3[:, g0:g0 + n, :], 128, n, 1024) for (g0, n) in groups]
                for dc in range(8):
                    b = bank('gen')
                    idx = 0
                    for gi, (g0, n) in enumerate(groups):
                        for ff in range(n):
                            fc = g0 + ff - f0
                            mm(ps[b][:, :T], Wd[gi][0][:, ff, dc * 128:(dc + 1) * 128], gbuf[:, fc, :T], idx == 0, idx == 10,
                               [('gbuf', fc), Wd[gi][1]], [P(b)])
                            idx += 1
                    residual_add(st, b, dc)

        def load_x(st, src, t0):
            T, TS, NS = st.T, st.TS, st.NS
            for s in range(NS):
                S.dma('sp', xstage[:TS, :], src[t0 + s * TS:t0 + (s + 1) * TS, :], reads=[], writes=['xstage'])
                for k0 in (0, 4):
                    b = bank('gen')
                    for kk2 in range(4):
                        kc = k0 + kk2
                        mm(ps[b][:, kk2 * TS:(kk2 + 1) * TS], xstage[:TS, kc * 128:(kc + 1) * 128], ident_f[:TS, :TS], True, True,
                           ['xstage', 'ident_f'], [P(b)])
                    act(xT[:, k0:k0 + 4, s * TS:(s + 1) * TS], ps[b][:, :4 * TS].rearrange("p (k t) -> p k t", k=4), AF.Copy, [P(b)], ['xT'])

        def store_x(st, dst, t0):
            T, TS, NS = st.T, st.TS, st.NS
            for s in range(NS):
                for k0 in (0, 4):
                    b = bank('gen')
                    for kk2 in range(4):
                        kc = k0 + kk2
                        mm(ps[b][:TS, kk2 * 128:(kk2 + 1) * 128], xT[:, kc, s * TS:(s + 1) * TS], ident_f[:, :], True, True, ['xT', 'ident_f'], [P(b)])
                    act(xstage[:TS, k0 * 128:(k0 + 4) * 128], ps[b][:TS, :], AF.Copy, [P(b)], ['xstage'])
                S.dma('act', dst[t0 + s * TS:t0 + (s + 1) * TS, :], xstage[:TS, :], reads=['xstage'], writes=['O_y'])

        def store_T(src3, key, ncn, R, r0, dst):
            for c0 in range(0, ncn, 4):
                n = min(4, ncn - c0)
                b = bank('gen')
                for c in range(n):
                    mm(ps[b][:R, c * 128:(c + 1) * 128], src3[:, c0 + c, r0:r0 + R], ident_f[:, :], True, True, [key, 'ident_f'], [P(b)])
                act(ostage[:R, :n * 128], ps[b][:R, :n * 128], AF.Copy, [P(b)], ['tmpg'])
                S.dma('act', dst[:, c0 * 128:(c0 + n) * 128], ostage[:R, :n * 128], reads=['tmpg'], writes=['O_st'])

        def run_layers(st, t0, O_k, O_v, O_lf, O_vc):
            for l in range(NL):
                if l % 2 == 0:
                    even_layer(st, l, t0, O_k, O_v, O_lf)
                else:
                    odd_layer(st, l, t0, O_vc)
                if DBG >= 6:
                    ffn(st, l)

        def store_states(st, O_cb, O_cd, O_cf):
            for i2 in range(2):
                store_T(st.histB, st.kB, 4, 2, i2 * 2, O_cb[i2 * 2:i2 * 2 + 2, :])
                store_T(st.histD, st.kD, 4, 30, i2 * 30, O_cd[i2 * 30:i2 * 30 + 30, :])
            for l in range(4):
                store_T(st.histF, st.kF, 44, 2, l * 2, O_cf[l * 2:l * 2 + 2, :])

        if do_sample:
            ss = make_stream('s', DEC, PAST + 512)
            load_T(I['state_conv_ffn'], 8, 5632, ss.histF, ss.kF)
            load_T(I['state_conv_b'], 4, 512, ss.histB, ss.kB)
            load_T(I['state_conv_d'], 60, 512, ss.histD, ss.kD)
            V('pool', 'memset', ss.carry[:], 0.0, reads=[], writes=[ss.kC])
            pre = Stream()
            pre.sid, pre.T, pre.TS, pre.NS = 's', 512, 128, 4
            pre.KT, pre.Vs, pre.carry = ss.KT, ss.Vs, ss.carry
            pre.kC = ss.kC
            for i2 in range(2):
                if 2 * i2 >= NL:
                    continue
                for t0 in range(0, PAST, 512):
                    for s in range(4):
                        r0 = t0 + s * 128
                        S.dma('sp', kf[:, :], I['cache_k'][i2, r0:r0 + 128, :], reads=[], writes=['kf'])
                        act(kb[:, :], kf[:, :], AF.Copy, ['kf'], ['kb'])
                        transp_heads(pre, kb, kT, 0, 'kb', 'kT')
                        S.dma('act', ss.KT[i2][:, 0:64, r0:r0 + 128].rearrange("h d t -> d h t"), kT[:64, :, :128], reads=['kT'], writes=[('KV', 's', i2)])
                        S.dma('sp', vf[:, :], I['cache_v'][i2, r0:r0 + 128, :], reads=[], writes=['vf'])
                        V('dve', 'tensor_copy', vb[:, 0, :, 0:64], vf[:, :].rearrange("p (h d) -> p h d", h=8), reads=['vf'], writes=['vb'])
                        S.dma('act', ss.Vs[i2][:, 0:128, r0 // 128, :].rearrange("h p d -> p h d"), vb[:, 0, :, :], reads=['vb'], writes=[('KV', 's', i2)])
                        S.dma('sp', logf[:, s, :], I['cache_logf'][i2, r0:r0 + 128, :], reads=[], writes=['logf'])
                    kv_finish(pre, i2, t0)
            load_x(ss, I['x_sample'], 0)
            ss.o0 = 0
            run_layers(ss, PAST, O['s_k'], O['s_v'], O['s_lf'], O['s_vc'])
            store_x(ss, O['y_s'], 0)
            store_states(ss, O['s_cb'], O['s_cd'], O['s_cf'])

        if NT > 0:
            sp_ = make_stream('p', 512, NTOK)
            for tname, tk in ((sp_.histF, sp_.kF), (sp_.histB, sp_.kB), (sp_.histD, sp_.kD), (sp_.carry, sp_.kC)):
                V('pool', 'memset', tname[:], 0.0, reads=[], writes=[tk])
            for ti in range(NT):
                t0 = ti * 512
                sp_.o0 = t0
                load_x(sp_, I['x_prompt'], t0)
                run_layers_prompt = run_layers
                run_layers_prompt(sp_, t0, O['p_k'], O['p_v'], O['p_lf'], None)
                store_x(sp_, O['y_p'], t0)
            store_states(sp_, O['p_cb'], O['p_cd'], O['p_cf'])

        S.finish('sp')
        S.simulate()
        print("instructions:", S.nins, "counts:", {k: v for k, v in S.cnt.items() if v})
    return nc


_NC_CACHE = {}


def _prep_inputs(inputs, c, NT):
    f = lambda a: np.ascontiguousarray(a, dtype=np.float32)
    b = c % 2
    m = {
        'x_prompt': f(inputs['x_prompt'][b, :NT * 512]),
        'x_sample': f(inputs['x_sample'][c]),
        'cache_k': f(inputs['cache_k'][:, c].reshape(2, PAST, 512)),
        'cache_v': f(inputs['cache_v'][:, c].reshape(2, PAST, 512)),
        'cache_logf': f(inputs['cache_logf'][:, c]),
        'state_conv_b': f(inputs['state_conv_b'][:, c].reshape(4, 512)),
        'state_conv_d': f(inputs['state_conv_d'][:, c].reshape(60, 512)),
        'state_conv_ffn': f(inputs['state_conv_ffn'][:, c].reshape(8, 5632)),
        'conv_b': f(inputs['conv_b'].reshape(6, 512)),
        'conv_d': f(inputs['conv_d'].reshape(62, 512)),
        'conv_ffn': f(inputs['conv_ffn'].reshape(12, 5632)),
    }
    for k in ['g_mix', 'w_in_even', 'b_f', 'g_q', 'g_k', 'w_out_even', 'w_in_odd', 'g_vc', 'w_s', 'b_s', 'g_d',
              'w_out_odd', 'g_ffn', 'w_up', 'w_down']:
        m[k] = f(inputs[k])
    return m


def run(inputs, NT=32, NL=4):
    key = (NT, NL)
    if key not in _NC_CACHE:
        _NC_CACHE[key] = build(NT, NL)
    nc = _NC_CACHE[key]
    in_maps = [_prep_inputs(inputs, c, NT) for c in range(8)]
    res = run_bass_kernel_spmd(nc, in_maps, core_ids=list(range(8)))
    R = res.results
    B = 2
    T = NT * 512
    y_p = np.stack([R[b]['y_p'] for b in range(B)])
    y_s = np.stack([R[c]['y_s'] for c in range(8)])
    pk = np.stack([R[b]['p_k'] for b in range(B)], axis=1).reshape(2, B, T, 8, 64)
    pv = np.stack([R[b]['p_v'] for b in range(B)], axis=1).reshape(2, B, T, 8, 64)
    plf = np.stack([R[b]['p_lf'] for b in range(B)], axis=1)
    pcb = np.stack([R[b]['p_cb'].reshape(2, 2, 512) for b in range(B)], axis=1)
    pcd = np.stack([R[b]['p_cd'].reshape(2, 30, 512) for b in range(B)], axis=1)
    pcf = np.stack([R[b]['p_cf'].reshape(4, 2, 5632) for b in range(B)], axis=1)
    sk = np.stack([R[c]['s_k'] for c in range(8)], axis=1).reshape(2, 8, DEC, 8, 64)
    sv = np.stack([R[c]['s_v'] for c in range(8)], axis=1).reshape(2, 8, DEC, 8, 64)
    slf = np.stack([R[c]['s_lf'] for c in range(8)], axis=1)
    scb = np.stack([R[c]['s_cb'].reshape(2, 2, 512) for c in range(8)], axis=1)
    svc = np.stack([R[c]['s_vc'] for c in range(8)], axis=1)
    scd = np.stack([R[c]['s_cd'].reshape(2, 30, 512) for c in range(8)], axis=1)
    scf = np.stack([R[c]['s_cf'].reshape(4, 2, 5632) for c in range(8)], axis=1)
    return tuple(np.ascontiguousarray(a, dtype=np.float32) for a in
                 (y_p, y_s, pk, pv, plf, pcb, pcd, pcf, sk, sv, slf, scb, svc, scd, scf))


def kernel(**inputs):
    return run(inputs, NT=32, NL=4)
```

```python
import os
import numpy as np
from contextlib import ExitStack
DBG = float(os.environ.get("MK_DBG", "9"))
import concourse.bass as bass
import concourse.mybir as mybir
from concourse.bass_utils import run_bass_kernel_spmd

F32 = mybir.dt.float32
BF16 = mybir.dt.bfloat16
ALU = mybir.AluOpType
AF = mybir.ActivationFunctionType
AX = mybir.AxisListType

D = 1024
DFF = 2816
PAST = 4096
DEC = 64
EPS = 1e-6
OFF_BG = 1544
NEG = -30000.0


class Sched:
    NDS = 4

    def __init__(self, nc):
        self.nc = nc
        self.eng = {'pe': nc.tensor, 'act': nc.scalar, 'dve': nc.vector, 'pool': nc.gpsimd, 'sp': nc.sync}
        self.sem = {}
        self.cnt = {}
        for e in ['pe', 'act', 'dve', 'pool']:
            self.sem[e] = nc.alloc_semaphore(name=f"s_{e}")
            self.cnt[e] = 0
        self.dq = ['sp', 'pool', 'act']
        for q in self.dq:
            for j in range(self.NDS):
                self.sem[(q, j)] = nc.alloc_semaphore(name=f"d_{q}{j}")
                self.cnt[(q, j)] = 0
        self.dcount = {q: 0 for q in self.dq}
        self.seen = {e: {} for e in self.eng}
        self.lastw = {}
        self.readers = {}
        self.nins = 0
        self.ev = {e: [] for e in self.eng}

    def simulate(self):
        val = {k: 0 for k in self.sem}
        pc = {e: 0 for e in self.ev}
        prog = True
        while prog:
            prog = False
            for e, lst in self.ev.items():
                while pc[e] < len(lst):
                    kind, s, v, info = lst[pc[e]]
                    if kind == 'wait':
                        if val[s] >= v:
                            pc[e] += 1
                            prog = True
                        else:
                            break
                    else:
                        val[s] += v
                        pc[e] += 1
                        prog = True
        stuck = {e: (pc[e], len(l), l[pc[e]] if pc[e] < len(l) else None) for e, l in self.ev.items()}
        ok = all(pc[e] == len(l) for e, l in self.ev.items())
        print("SIM", "OK" if ok else "DEADLOCK", stuck if not ok else "")
        if not ok:
            print({k: v for k, v in val.items()})
        return ok

    def _deps(self, reads, writes):
        deps = set()
        for k in reads:
            if k in self.lastw:
                deps.add(self.lastw[k])
        for k in writes:
            if k in self.lastw:
                deps.add(self.lastw[k])
            for r in self.readers.get(k, ()):
                deps.add(r)
        return deps

    def _wait(self, e, deps):
        need = {}
        for (s, c) in deps:
            if c > need.get(s, 0):
                need[s] = c
        for s, c in need.items():
            if self.seen[e].get(s, 0) >= c:
                continue
            unit = 16 if isinstance(s, tuple) else 1
            self.eng[e].wait_ge(self.sem[s], c * unit)
            self.ev[e].append(('wait', s, c * unit, None))
            self.seen[e][s] = c

    def _record(self, tok, reads, writes):
        for k in reads:
            lst = self.readers.setdefault(k, [])
            lst.append(tok)
            if len(lst) > 64:
                best = {}
                for (s, c) in lst:
                    if c > best.get(s, 0):
                        best[s] = c
                self.readers[k] = [(s, c) for s, c in best.items()]
        for k in writes:
            self.lastw[k] = tok
            self.readers[k] = []

    def op(self, e, fn, reads=(), writes=(), signal=True):
        deps = self._deps(reads, writes)
        if e == 'pe':
            deps = {d for d in deps if d[0] != 'pe'}
        self._wait(e, deps)
        ins = fn()
        tok = (e, self.cnt[e] + 1)
        if signal:
            ins.then_inc(self.sem[e], 1)
            self.cnt[e] += 1
            self.ev[e].append(('inc', e, 1, self.nins))
        self._record(tok, reads, writes)
        self.nins += 1
        return ins

    def dma(self, q, out, in_, reads=(), writes=(), **kw):
        if q == 'act' and os.environ.get("MK_ACTQ", "sp") != "act":
            q = os.environ.get("MK_ACTQ", "sp")
        deps = self._deps(reads, writes)
        j = self.dcount[q] % self.NDS
        self.dcount[q] += 1
        s = (q, j)
        if self.cnt[s] > 0:
            deps = set(deps)
            deps.add((s, self.cnt[s]))
        self._wait(q, deps)
        ins = self.eng[q].dma_start(out=out, in_=in_, **kw)
        ins.then_inc(self.sem[s], 16)
        self.cnt[s] += 1
        self.ev[q].append(('inc', s, 16, self.nins))
        tok = (s, self.cnt[s])
        self._record(tok, reads, writes)
        self.nins += 1
        return ins

    def barrier(self):
        deps = set()
        for s in self.cnt:
            if self.cnt[s] > 0:
                deps.add((s, self.cnt[s]))
        for e in ['pe', 'act', 'dve', 'pool', 'sp']:
            self._wait(e, deps)

    def finish(self, e='sp'):
        deps = set()
        for k, t in self.lastw.items():
            deps.add(t)
        for s in self.cnt:
            if self.cnt[s] > 0:
                deps.add((s, self.cnt[s]))
        self._wait(e, deps)


class Stream:
    pass


def build(NT=32, NL=4, do_sample=True):
    nc = bass.Bass("TRN2", target_bir_lowering=False)
    NTOK = NT * 512

    def din(name, shape):
        return nc.dram_tensor(name, list(shape), F32, kind="ExternalInput").ap()

    def dout(name, shape):
        return nc.dram_tensor(name, list(shape), F32, kind="ExternalOutput").ap()

    def dscr(name, shape, dt=BF16):
        return nc.dram_tensor(name, list(shape), dt, kind="Internal").ap()

    I = dict(
        x_prompt=din("x_prompt", [NTOK, D]), x_sample=din("x_sample", [DEC, D]),
        cache_k=din("cache_k", [2, PAST, 512]), cache_v=din("cache_v", [2, PAST, 512]),
        cache_logf=din("cache_logf", [2, PAST, 8]),
        state_conv_b=din("state_conv_b", [4, 512]), state_conv_d=din("state_conv_d", [60, 512]),
        state_conv_ffn=din("state_conv_ffn", [8, 5632]),
        g_mix=din("g_mix", [4, D]), w_in_even=din("w_in_even", [2, D, 3080]), b_f=din("b_f", [2, 8]),
        g_q=din("g_q", [2, 64]), g_k=din("g_k", [2, 64]), conv_b=din("conv_b", [6, 512]),
        w_out_even=din("w_out_even", [2, D, D]), w_in_odd=din("w_in_odd", [2, D, 2048]),
        g_vc=din("g_vc", [2, 512]), w_s=din("w_s", [2, 4, 128, 128]), b_s=din("b_s", [2, 4, 128]),
        conv_d=din("conv_d", [62, 512]), g_d=din("g_d", [2, 512]), w_out_odd=din("w_out_odd", [2, D, D]),
        g_ffn=din("g_ffn", [4, D]), w_up=din("w_up", [4, D, 5632]), conv_ffn=din("conv_ffn", [12, 5632]),
        w_down=din("w_down", [4, DFF, D]),
    )
    O = dict(
        y_p=dout("y_p", [NTOK, D]), y_s=dout("y_s", [DEC, D]),
        p_k=dout("p_k", [2, NTOK, 512]), p_v=dout("p_v", [2, NTOK, 512]), p_lf=dout("p_lf", [2, NTOK, 8]),
        p_cb=dout("p_cb", [4, 512]), p_cd=dout("p_cd", [60, 512]), p_cf=dout("p_cf", [8, 5632]),
        s_k=dout("s_k", [2, DEC, 512]), s_v=dout("s_v", [2, DEC, 512]), s_lf=dout("s_lf", [2, DEC, 8]),
        s_cb=dout("s_cb", [4, 512]), s_vc=dout("s_vc", [2, DEC, 512]), s_cd=dout("s_cd", [60, 512]),
        s_cf=dout("s_cf", [8, 5632]),
    )
    WB = dict(
        wie=dscr("wie_b", [2, D, 3080]), woe=dscr("woe_b", [2, D, D]), wio=dscr("wio_b", [2, D, 2048]),
        woo=dscr("woo_b", [2, D, D]), wup=dscr("wup_b", [4, D, 5632]), wdn=dscr("wdn_b", [4, DFF, D]),
    )
    cq_scr = dscr("cq_scr", [8, 3, 512])

    S = Sched(nc)
    with ExitStack() as es:
        def SB(name, shape, dt=F32):
            return es.enter_context(nc.sbuf_tensor(name, list(shape), dt))

        ps = [es.enter_context(nc.psum_tensor(f"ps{i}", [128, 512], F32)) for i in range(8)]
        rot = {'gen': [0, [0, 1, 2]], 'S': [0, [3, 4]], 'O': [0, [5, 6]]}

        def bank(kind):
            r = rot[kind]
            b = r[1][r[0] % len(r[1])]
            r[0] += 1
            return b

        def P(b):
            return ('ps', b)

        def mm(out, lhsT, rhs, start, stop, reads, writes, sig=None):
            S.op('pe', lambda: nc.tensor.matmul(out, lhsT=lhsT, rhs=rhs, start=start, stop=stop),
                 reads=reads, writes=writes, signal=(stop if sig is None else sig))

        def act(out, in_, func, reads, writes, **kw):
            S.op('act', lambda: nc.scalar.activation(out, in_, func, **kw), reads=reads, writes=writes)

        def V(e, name, *args, reads, writes, **kw):
            eng = nc.vector if e == 'dve' else nc.gpsimd
            S.op(e, lambda: getattr(eng, name)(*args, **kw), reads=reads, writes=writes)

        def fma(e, acc, src, wcol, rkeys, akey):
            V('dve', 'scalar_tensor_tensor', acc, src, wcol, acc, reads=rkeys + [akey], writes=[akey], op0=ALU.mult, op1=ALU.add)

        ones_f = SB("ones_f", [128, 128])
        ident_f = SB("ident_f", [128, 128])
        utri_f = SB("utri_f", [128, 128])
        ident_b = SB("ident_b", [128, 128], BF16)
        ones_b = SB("ones_b", [128, 128], BF16)
        zeros_b = SB("zeros_b", [128, 512], BF16)
        dmask = SB("dmask", [128, 4, 512], BF16)
        ones3 = SB("ones3", [8, 3, 512], BF16)
        V('pool', 'memset', ones_f[:], 1.0, reads=[], writes=['ones_f'])
        V('pool', 'memset', ones_b[:], 1.0, reads=[], writes=['ones_b'])
        V('pool', 'memset', zeros_b[:], 0.0, reads=[], writes=['zeros_b'])
        V('pool', 'memset', ones3[:], 1.0, reads=[], writes=['ones3'])
        S.op('pool', lambda: nc.gpsimd.affine_select(ident_f[:], ones_f[:], [[-1, 128]], ALU.is_equal, 0.0, base=0, channel_multiplier=1),
             reads=['ones_f'], writes=['ident_f'])
        S.op('pool', lambda: nc.gpsimd.affine_select(utri_f[:], ones_f[:], [[1, 128]], ALU.is_ge, 0.0, base=0, channel_multiplier=-1),
             reads=['ones_f'], writes=['utri_f'])
        V('pool', 'tensor_copy', ident_b[:], ident_f[:], reads=['ident_f'], writes=['ident_b'])
        for r in range(4):
            S.op('pool', lambda: nc.gpsimd.affine_select(dmask[:, r, :], zeros_b[:], [[1, 512]], ALU.is_ge, NEG, base=-128 * r, channel_multiplier=-1),
                 reads=['zeros_b'], writes=['dmask'])

        cx = SB("cx", [128, 4, 512])
        ob = SB("ob", [128, 4, 512], BF16)
        castf = cx[:, :, :].rearrange("p (a b) c -> p a (b c)", a=2)
        castb = ob[:, :, :].rearrange("p (a b) c -> p a (b c)", a=2)
        ci = [0]

        def cast_weight(src, dst, rows, cols):
            for r0 in range(0, rows, 128):
                for c0 in range(0, cols, 1024):
                    cw = min(1024, cols - c0)
                    b = ci[0] % 2
                    e = ['dve', 'pool'][ci[0] % 2]
                    ci[0] += 1
                    S.dma('sp', castf[:, b, :cw], src[r0:r0 + 128, c0:c0 + cw], reads=[], writes=[('castf', b)])
                    V(e, 'tensor_copy', castb[:, b, :cw], castf[:, b, :cw], reads=[('castf', b)], writes=[('castb', b)])
                    S.dma('act', dst[r0:r0 + 128, c0:c0 + cw], castb[:, b, :cw], reads=[('castb', b)], writes=['WB'])

        for i2 in range(2):
            cast_weight(I['w_in_even'][i2], WB['wie'][i2], D, 3080)
            cast_weight(I['w_out_even'][i2], WB['woe'][i2], D, D)
            cast_weight(I['w_in_odd'][i2], WB['wio'][i2], D, 2048)
            cast_weight(I['w_out_odd'][i2], WB['woo'][i2], D, D)
        for l in range(4):
            cast_weight(I['w_up'][l], WB['wup'][l], D, 5632)
            cast_weight(I['w_down'][l], WB['wdn'][l], DFF, D)

        S.barrier()
        xstage = SB("xstage", [128, 1024])
        stage = xstage

        def load_T(src, R, C, dst3, key):
            ncn = C // 128
            per = min(512 // R, 8)
            for c0 in range(0, ncn, per):
                n = min(per, ncn - c0)
                S.dma('sp', stage[:R, :n * 128], src[:, c0 * 128:(c0 + n) * 128], reads=[], writes=['xstage'])
                b = bank('gen')
                for c in range(n):
                    mm(ps[b][:, c * R:(c + 1) * R], stage[:R, c * 128:(c + 1) * 128], ident_f[:R, :R], True, True,
                       ['xstage', 'ident_f'], [P(b)])
                act(dst3[:, c0:c0 + n, :], ps[b][:, :n * R].rearrange("p (c r) -> p c r", r=R), AF.Copy, [P(b)], [key])

        gmix = SB("gmix", [128, 8, 4]); load_T(I['g_mix'], 4, D, gmix, 'gmix')
        gffn = SB("gffn", [128, 8, 4]); load_T(I['g_ffn'], 4, D, gffn, 'gffn')
        gd = SB("gd", [128, 4, 2]); load_T(I['g_d'], 2, 512, gd, 'gd')
        wB = SB("wB", [128, 4, 6]); load_T(I['conv_b'], 6, 512, wB, 'wB')
        wD = SB("wD", [128, 4, 62]); load_T(I['conv_d'], 62, 512, wD, 'wD')
        wF = SB("wF", [128, 44, 12]); load_T(I['conv_ffn'], 12, 5632, wF, 'wF')

        gq_t = SB("gq_t", [128, 2, 64]); gk_t = SB("gk_t", [128, 2, 64])
        gq_bc = SB("gq_bc", [128, 512]); gk_bc = SB("gk_bc", [128, 512])
        gvc_bc = SB("gvc_bc", [128, 512]); bs_bc = SB("bs_bc", [128, 2, 4, 128]); bf_bc = SB("bf_bc", [128, 2, 8])
        for i2 in range(2):
            S.dma('sp', gq_t[:, i2, :], I['g_q'][i2].partition_broadcast(128), reads=[], writes=['gq_t'])
            S.dma('sp', gk_t[:, i2, :], I['g_k'][i2].partition_broadcast(128), reads=[], writes=['gk_t'])
            S.dma('sp', bf_bc[:, i2, :], I['b_f'][i2].partition_broadcast(128), reads=[], writes=['bf_bc'])
            for g in range(4):
                S.dma('sp', bs_bc[:, i2, g, :], I['b_s'][i2, g].partition_broadcast(128), reads=[], writes=['bs_bc'])

        def load_gqk(i2):
            V('dve', 'tensor_scalar', gq_bc[:, :].rearrange("p (h d) -> p h d", h=8), gq_t[:, i2, :].unsqueeze(1).to_broadcast([128, 8, 64]),
              0.125, None, reads=['gq_t'], writes=['gq_bc'], op0=ALU.mult)
            V('dve', 'tensor_scalar', gk_bc[:, :].rearrange("p (h d) -> p h d", h=8), gk_t[:, i2, :].unsqueeze(1).to_broadcast([128, 8, 64]),
              1.0, None, reads=['gk_t'], writes=['gk_bc'], op0=ALU.mult)
        wf_f = SB("wf_f", [128, 2, 8, 8]); wf_b = SB("wf_b", [128, 2, 8, 8], BF16)
        for i2 in range(2):
            S.dma('sp', wf_f[:, i2, :, :], I['w_in_even'][i2].rearrange("(kc p) n -> p kc n", p=128)[:, :, 1536:1544], reads=[], writes=['wf_f'])
        V('dve', 'tensor_copy', wf_b[:], wf_f[:], reads=['wf_f'], writes=['wf_b'])
        wsT = SB("wsT", [128, 2, 4, 128], BF16)
        wsf = SB("wsf", [128, 128])
        for i2 in range(2):
            for g in range(4):
                S.dma('sp', wsf[:], I['w_s'][i2, g], reads=[], writes=['wsf'])
                S.op('pool', lambda: nc.gpsimd.affine_select(wsf[:], wsf[:], [[-1, 128]], ALU.is_ge, 0.0, base=0, channel_multiplier=1),
                     reads=['wsf'], writes=['wsf'])
                b = bank('gen')
                mm(ps[b][:, :128], wsf[:], ident_f[:], True, True, ['wsf', 'ident_f'], [P(b)])
                act(wsT[:, i2, g, :], ps[b][:, :128], AF.Copy, [P(b)], ['wsT'])

        NWB = 4
        wbuf = [SB(f"wbuf{i}", [128, 4096], BF16) for i in range(NWB)]
        wrr = [0]

        def wreq(src3, npart, a, b):
            i = wrr[0] % NWB
            wrr[0] += 1
            view = wbuf[i][:npart, :a * b].rearrange("p (a b) -> p a b", a=a)
            S.dma('sp', view, src3, reads=['WB'], writes=[('wbuf', i)])
            return view, ('wbuf', i)

        xT = SB("xT", [128, 8, 512])
        xn = SB("xn", [128, 8, 512], BF16)
        sq = SB("sq", [128, 2, 512], BF16)
        rstd = SB("rstd", [128, 512])
        gbuf = SB("gbuf", [128, 11, 512], BF16)
        hb = [SB(f"hb{i}", [128, 514]) for i in range(2)]
        ca = [SB(f"ca{i}", [128, 512]) for i in range(2)]
        print("SBUF remaining after hb/ca:", nc.sbuf_bytes_remaining)
        sgt = SB("sgt", [128, 512])
        tmpf = SB("tmpf", [128, 512]); sqh = tmpf; ssq = SB("ssq", [128, 8]); qf = SB("qf", [128, 512]); kf = SB("kf", [128, 512])
        vf = SB("vf", [128, 512]); qb = SB("qb", [128, 512], BF16); kb = SB("kb", [128, 512], BF16)
        lz = SB("lz", [128, 8]); logf = SB("logf", [128, 4, 8])
        vb = SB("vb", [128, 1, 8, 128], BF16)
        V('pool', 'memset', vb[:], 0.0, reads=[], writes=['vb'])
        V('pool', 'memset', vb[:, :, :, 64:65], 1.0, reads=[], writes=['vb'])
        qT = SB("qT", [128, 8, 512], BF16)
        V('pool', 'memset', qT[:], 1.0, reads=[], writes=['qT'])
        kT = SB("kT", [64, 8, 128], BF16)
        cc = SB("cc", [8, 512]); cr = SB("cr", [8, 512]); caug = SB("caug", [8, 3, 512], BF16); ncaug = SB("ncaug", [8, 3, 512], BF16)
        KB = 1024
        kbuf = [SB(f"kbuf{i}", [128, KB + 512], BF16) for i in range(2)]
        vbuf = [SB(f"vbuf{i}", [128, KB // 128 + 4, 128], BF16) for i in range(2)]
        for i in range(2):
            V('pool', 'memset', kbuf[i][:], 0.0, reads=[], writes=[('kbuf', i)])
        pT = [SB(f"pT{i}", [128, 512], BF16) for i in range(3)]
        osb = SB("osb", [65, 512]); rec = osb; bcs = SB("bcs", [64, 512])
        oa = SB("oa", [64, 8, 512], BF16)
        ub = SB("ub", [128, 4, 544])
        tmpg = SB("tmpg", [128, 512])
        oc = SB("oc", [128, 4, 512], BF16)
        ug = SB("ug", [128, 4, 512], BF16)
        zf = qf; vcb = kb
        rs1 = SB("rs1", [128, 8])
        ostage = tmpg

        kvrr = [0]
        ptrr = [0]

        def make_stream(sid, T, ntok_scr):
            st = Stream()
            st.sid = sid
            st.T = T
            st.TS = min(T, 128)
            st.NS = T // st.TS
            st.KT = [dscr(f"KT_{sid}_{i}", [8, 70, ntok_scr]) for i in range(2)]
            st.Vs = [dscr(f"V_{sid}_{i}", [8, 128, ntok_scr // 128, 128]) for i in range(2)]
            st.histF = SB(f"histF_{sid}", [128, 44, 8])
            st.histB = SB(f"histB_{sid}", [128, 4, 4])
            st.histD = SB(f"histD_{sid}", [128, 4, 60])
            st.carry = SB(f"carry_{sid}", [8, 2])
            st.kF, st.kB, st.kD, st.kC = f"histF_{sid}", f"histB_{sid}", f"histD_{sid}", f"carry_{sid}"
            st.o0 = 0
            return st

        def rmsnorm_x(st, gt, l):
            T = st.T
            for kc in range(8):
                b = kc % 2
                act(sq[:, b, :T], xT[:, kc, :T], AF.Square, ['xT'], [('sq', b)])
                mm(ps[7][:, :T], ones_b[:], sq[:, b, :T], kc == 0, kc == 7, [('sq', b), 'ones_b'], [P(7)], sig=True)
            V('dve', 'tensor_scalar', rstd[:, :T], ps[7][:, :T], 1.0 / D, EPS, reads=[P(7)], writes=['rstd'], op0=ALU.mult, op1=ALU.add)
            act(rstd[:, :T], rstd[:, :T], AF.Sqrt, ['rstd'], ['rstd'])
            V('dve', 'reciprocal', rstd[:, :T], rstd[:, :T], reads=['rstd'], writes=['rstd'])
            for kc in range(8):
                V('dve', 'scalar_tensor_tensor', xn[:, kc, :T], xT[:, kc, :T], gt[:, kc, l:l + 1], rstd[:, :T],
                  reads=['xT', 'rstd'], writes=['xn'], op0=ALU.mult, op1=ALU.mult)

        def head_norm(st, b, gbc, outf, okey, fin=None, finkey=None):
            TS = st.TS
            act(sqh[:TS, :], ps[b][:TS, :], AF.Square, [P(b)], ['tmpf'])
            V('dve', 'tensor_reduce', ssq[:TS, :], sqh[:TS, :].rearrange("p (h d) -> p h d", h=8), AX.X, ALU.add, reads=['tmpf'], writes=['ssq'])
            V('dve', 'tensor_scalar', ssq[:TS, :], ssq[:TS, :], 1.0 / 64, EPS, reads=['ssq'], writes=['ssq'], op0=ALU.mult, op1=ALU.add)
            act(ssq[:TS, :], ssq[:TS, :], AF.Sqrt, ['ssq'], ['ssq'])
            V('dve', 'reciprocal', ssq[:TS, :], ssq[:TS, :], reads=['ssq'], writes=['ssq'])
            V('dve', 'tensor_tensor', outf[:TS, :].rearrange("p (h d) -> p h d", h=8), ps[b][:TS, :].rearrange("p (h d) -> p h d", h=8),
              ssq[:TS, :].unsqueeze(2).to_broadcast([TS, 8, 64]), ALU.mult, reads=[P(b), 'ssq'], writes=[okey])
            if fin is None:
                fin, finkey = outf, okey
            V('dve', 'tensor_tensor', fin[:TS, :], outf[:TS, :], gbc[:TS, :], ALU.mult, reads=[okey, 'gq_bc', 'gk_bc'], writes=[finkey])

        def transp_heads(st, src_b, dstT, s, key_src, key_dst):
            TS = st.TS
            for h0 in (0, 4):
                b = bank('gen')
                for hh in range(4):
                    h = h0 + hh
                    mm(ps[b][:64, hh * TS:(hh + 1) * TS], src_b[:TS, h * 64:(h + 1) * 64], ident_b[:TS, :TS], True, True,
                       [key_src, 'ident_b'], [P(b)])
                act(dstT[:64, h0:h0 + 4, s * TS:(s + 1) * TS], ps[b][:64, :4 * TS].rearrange("p (h t) -> p h t", h=4), AF.Copy, [P(b)], [key_dst])

        def kv_finish(st, i2, t0):
            T, TS, NS = st.T, st.TS, st.NS
            for s in range(NS):
                mm(ps[7][:8, s * TS:(s + 1) * TS], logf[:TS, s, :], utri_f[:TS, :TS], True, True, ['logf', 'utri_f'], [P(7)])
            for s in range(NS):
                V('dve', 'tensor_scalar', cc[:8, s * TS:(s + 1) * TS], ps[7][:8, s * TS:(s + 1) * TS], st.carry[:8, i2:i2 + 1], None,
                  reads=[P(7), st.kC], writes=['cc'], op0=ALU.add)
                V('dve', 'tensor_copy', st.carry[:8, i2:i2 + 1], cc[:8, (s + 1) * TS - 1:(s + 1) * TS], reads=['cc'], writes=[st.kC])
            V('dve', 'tensor_copy', caug[:, 0, :T], cc[:, :T], reads=['cc'], writes=['caug'])
            V('dve', 'tensor_tensor', cr[:, :T], cc[:, :T], caug[:, 0, :T], ALU.subtract, reads=['cc', 'caug'], writes=['cr'])
            V('dve', 'tensor_copy', caug[:, 1, :T], cr[:, :T], reads=['cr'], writes=['caug'])
            V('dve', 'tensor_tensor', cr[:, :T], cr[:, :T], caug[:, 1, :T], ALU.subtract, reads=['cr', 'caug'], writes=['cr'])
            V('dve', 'tensor_copy', caug[:, 2, :T], cr[:, :T], reads=['cr'], writes=['caug'])
            V('dve', 'tensor_scalar', ncaug[:, :, :T], caug[:, :, :T], -1.0, None, reads=['caug'], writes=['ncaug'], op0=ALU.mult)
            kk = ('KV', st.sid, i2)
            S.dma('act', st.KT[i2][:, 64:67, t0:t0 + T], ones3[:, :, :T], reads=['ones3'], writes=[kk])
            S.dma('act', st.KT[i2][:, 67:70, t0:t0 + T], ncaug[:, :, :T], reads=['ncaug'], writes=[kk])

        def attention(st, i2, t0):
            T, TS, NS = st.T, st.TS, st.NS
            kk = ('KV', st.sid, i2)
            S.dma('act', cq_scr[:, :, :T], caug[:, :, :T], reads=['caug'], writes=['cq_scr'])
            S.dma('act', qT[64:67, :, :T], cq_scr[:, :, :T].rearrange("h a t -> a h t"), reads=['cq_scr'], writes=['qT'])
            ntot = t0 + T
            chunks = []
            c0 = 0
            while c0 < ntot:
                c1 = min(c0 + KB, ntot)
                if ntot - c1 <= 512 and ntot - c1 > 0:
                    c1 = ntot
                chunks.append((c0, c1))
                c0 = c1
            for h in range(8):
                obk = bank('O')
                first = True
                for (c0, c1) in chunks:
                    i = kvrr[0] % 2
                    kvrr[0] += 1
                    n = c1 - c0
                    nkt = (n + 127) // 128
                    S.dma('sp', kbuf[i][:70, :n], st.KT[i2][h, :, c0:c1], reads=[kk], writes=[('kbuf', i)])
                    S.dma('sp', vbuf[i][:, :nkt, :], st.Vs[i2][h, :, c0 // 128:c0 // 128 + nkt, :], reads=[kk], writes=[('vbuf', i)])
                    col = 0
                    while col < n:
                        gpos = c0 + col
                        if gpos < t0:
                            ksz, r = 128, None
                        else:
                            ksz, r = TS, (gpos - t0) // TS
                        last = (gpos + ksz >= ntot)
                        sb_ = bank('S')
                        mm(ps[sb_][:ksz, :T], kbuf[i][:, col:col + ksz], qT[:, h, :T], True, r is None, [('kbuf', i), 'qT'], [P(sb_)])
                        if r is not None:
                            mm(ps[sb_][:ksz, :T], ident_b[:ksz, :ksz], dmask[:ksz, r, :T], False, True, ['ident_b', 'dmask'], [P(sb_)])
                        pi = ptrr[0] % 3
                        ptrr[0] += 1
                        act(pT[pi][:ksz, :T], ps[sb_][:ksz, :T], AF.Exp, [P(sb_)], [('pT', pi)])
                        mm(ps[obk][:, :T], vbuf[i][:ksz, col // 128, :], pT[pi][:ksz, :T], first, last, [('vbuf', i), ('pT', pi)], [P(obk)])
                        first = False
                        col += ksz
                act(osb[:65, :T], ps[obk][:65, :T], AF.Copy, [P(obk)], ['osb'])
                V('dve', 'reciprocal', osb[64:65, :T], osb[64:65, :T], reads=['osb'], writes=['osb'])
                mm(ps[7][:64, :T], ones_f[64:65, :64], osb[64:65, :T], True, True, ['osb', 'ones_f'], [P(7)])
                act(bcs[:64, :T], ps[7][:64, :T], AF.Copy, [P(7)], ['bcs'])
                V('dve', 'tensor_tensor', oa[:64, h, :T], osb[:64, :T], bcs[:64, :T], ALU.mult, reads=['osb', 'bcs'], writes=['oa'])

        def residual_add(st, b, dc):
            T = st.T
            V('dve', 'tensor_tensor', xT[:, dc, :T], xT[:, dc, :T], ps[b][:, :T], ALU.add, reads=['xT', P(b)], writes=['xT'])

        def even_layer(st, l, t0, O_k, O_v, O_lf):
            T, TS, NS = st.T, st.TS, st.NS
            i2 = l // 2
            wie3 = WB['wie'][i2].rearrange("(kc p) n -> p kc n", p=128)
            rmsnorm_x(st, gmix, l)
            load_gqk(i2)
            kk = ('KV', st.sid, i2)
            kt0 = t0 // 128
            o0 = st.o0
            Wq, kq = wreq(wie3[:, :, 0:512], 128, 8, 512)
            Wk, kk_ = wreq(wie3[:, :, 512:1024], 128, 8, 512)
            Wv, kv_ = wreq(wie3[:, :, 1024:1536], 128, 8, 512)
            for s in range(NS):
                ts = slice(s * TS, (s + 1) * TS)
                b = bank('gen')
                for kc in range(8):
                    mm(ps[b][:TS, :], xn[:, kc, ts], Wq[:, kc, :], kc == 0, kc == 7, ['xn', kq], [P(b)])
                head_norm(st, b, gq_bc, qf, 'qf', qb, 'qb')
                transp_heads(st, qb, qT, s, 'qb', 'qT')
                if DBG < 0.5:
                    continue
                b = bank('gen')
                for kc in range(8):
                    mm(ps[b][:TS, :], xn[:, kc, ts], Wk[:, kc, :], kc == 0, kc == 7, ['xn', kk_], [P(b)])
                head_norm(st, b, gk_bc, kf, 'kf')
                S.dma('act', O_k[i2, o0 + s * TS:o0 + (s + 1) * TS, :], kf[:TS, :], reads=['kf'], writes=['O_k'])
                act(kb[:TS, :], kf[:TS, :], AF.Copy, ['kf'], ['kb'])
                transp_heads(st, kb, kT, 0, 'kb', 'kT')
                S.dma('act', st.KT[i2][:, 0:64, t0 + s * TS:t0 + (s + 1) * TS].rearrange("h d t -> d h t"), kT[:64, :, :TS], reads=['kT'], writes=[kk])
                if DBG < 0.7:
                    continue
                b = bank('gen')
                for kc in range(8):
                    mm(ps[b][:TS, :], xn[:, kc, ts], Wv[:, kc, :], kc == 0, kc == 7, ['xn', kv_], [P(b)])
                act(vf[:TS, :], ps[b][:TS, :], AF.Copy, [P(b)], ['vf'])
                V('dve', 'tensor_copy', vb[:TS, 0, :, 0:64], vf[:TS, :].rearrange("p (h d) -> p h d", h=8), reads=['vf'], writes=['vb'])
                S.dma('act', O_v[i2, o0 + s * TS:o0 + (s + 1) * TS, :], vf[:TS, :], reads=['vf'], writes=['O_v'])
                S.dma('act', st.Vs[i2][:, 0:TS, kt0 + s, :].rearrange("h p d -> p h d"), vb[:TS, 0, :, :], reads=['vb'], writes=[kk])
                if DBG < 0.9:
                    continue
                for kc in range(8):
                    mm(ps[7][:TS, 0:8], xn[:, kc, ts], wf_b[:, i2, kc, :], kc == 0, kc == 7, ['xn', 'wf_b'], [P(7)])
                V('dve', 'tensor_tensor', lz[:TS, :], ps[7][:TS, 0:8], bf_bc[:TS, i2, :], ALU.add, reads=[P(7), 'bf_bc'], writes=['lz'])
                act(lz[:TS, :], lz[:TS, :], AF.Exp, ['lz'], ['lz'], scale=-1.0)
                act(lz[:TS, :], lz[:TS, :], AF.Ln, ['lz'], ['lz'], bias=1.0)
                V('dve', 'tensor_scalar', logf[:TS, s, :], lz[:TS, :], -1.0, None, reads=['lz'], writes=['logf'], op0=ALU.mult)
                S.dma('act', O_lf[i2, o0 + s * TS:o0 + (s + 1) * TS, :], logf[:TS, s, :], reads=['logf'], writes=['O_lf'])
            if DBG < 2:
                return
            kv_finish(st, i2, t0)
            if DBG < 3:
                return
            attention(st, i2, t0)
            if DBG < 4:
                return
            Wbg, kbg = wreq(wie3[:, :, OFF_BG:OFF_BG + 512], 128, 8, 512)
            Wcg, kcg = wreq(wie3[:, :, OFF_BG + 512:OFF_BG + 1024], 128, 8, 512)
            Wxi, kxi = wreq(wie3[:, :, OFF_BG + 1024:OFF_BG + 1536], 128, 8, 512)
            V('dve', 'tensor_copy', ub[:, :, 0:2], st.histB[:, :, i2 * 2:i2 * 2 + 2], reads=[st.kB], writes=['ub'])
            for c in range(4):
                cs = slice(c * 128, (c + 1) * 128)
                b = bank('gen')
                for kc in range(8):
                    mm(ps[b][:, :T], Wcg[:, kc, cs], xn[:, kc, :T], kc == 0, kc == 7, ['xn', kcg], [P(b)])
                act(tmpf[:, :T], ps[b][:, :T], AF.Copy, [P(b)], ['tmpf'])
                b = bank('gen')
                for kc in range(8):
                    mm(ps[b][:, :T], Wxi[:, kc, cs], xn[:, kc, :T], kc == 0, kc == 7, ['xn', kxi], [P(b)])
                V('dve', 'tensor_tensor', ub[:, c, 2:2 + T], tmpf[:, :T], ps[b][:, :T], ALU.mult, reads=['tmpf', P(b)], writes=['ub'])
                V('dve', 'tensor_scalar', cx[:, c, :T], ub[:, c, 2:2 + T], wB[:, c, i2 * 3 + 2:i2 * 3 + 3], None, reads=['ub'], writes=[('cx', c)], op0=ALU.mult)
                for j in (1, 0):
                    fma('dve', cx[:, c, :T], ub[:, c, j:j + T], wB[:, c, i2 * 3 + j:i2 * 3 + j + 1], ['ub'], ('cx', c))
                b = bank('gen')
                for kc in range(8):
                    mm(ps[b][:, :T], Wbg[:, kc, cs], xn[:, kc, :T], kc == 0, kc == 7, ['xn', kbg], [P(b)])
                V('dve', 'tensor_tensor', ob[:, c, :T], ps[b][:, :T], cx[:, c, :T], ALU.mult, reads=[P(b), ('cx', c)], writes=['ob'])
            V('dve', 'tensor_copy', st.histB[:, :, i2 * 2:i2 * 2 + 2], ub[:, :, T:T + 2], reads=['ub'], writes=[st.kB])
            if DBG < 5:
                return
            woA = WB['woe'][i2][0:512, :].rearrange("(h d) n -> d h n", d=64)
            woB = WB['woe'][i2][512:1024, :].rearrange("(c p) n -> p c n", p=128)
            WA0, kA0 = wreq(woA[:, :, 0:512], 64, 8, 512)
            WA1, kA1 = wreq(woA[:, :, 512:1024], 64, 8, 512)
            WBo, kBo = wreq(woB, 128, 4, 1024)
            for dc in range(8):
                b = bank('gen')
                WA, kA = (WA0, kA0) if dc < 4 else (WA1, kA1)
                dsl = slice((dc % 4) * 128, (dc % 4 + 1) * 128)
                for h in range(8):
                    mm(ps[b][:, :T], WA[:64, h, dsl], oa[:64, h, :T], h == 0, False, ['oa', kA], [P(b)])
                for c in range(4):
                    mm(ps[b][:, :T], WBo[:, c, dc * 128:(dc + 1) * 128], ob[:, c, :T], False, c == 3, ['ob', kBo], [P(b)])
                residual_add(st, b, dc)

        def gelu_from_psum(b, npart, T, outf, key):
            act(tmpf[:npart, :T], ps[b][:npart, :T], AF.Square, [P(b)], ['tmpf'])
            V('dve', 'tensor_scalar', tmpf[:npart, :T], tmpf[:npart, :T], 0.044715, 1.0, reads=['tmpf'], writes=['tmpf'], op0=ALU.mult, op1=ALU.add)
            V('dve', 'tensor_tensor', tmpf[:npart, :T], tmpf[:npart, :T], ps[b][:npart, :T], ALU.mult, reads=['tmpf', P(b)], writes=['tmpf'])
            act(tmpf[:npart, :T], tmpf[:npart, :T], AF.Sigmoid, ['tmpf'], ['tmpf'], scale=1.5957691216057308)
            V('dve', 'tensor_tensor', outf, tmpf[:npart, :T], ps[b][:npart, :T], ALU.mult, reads=['tmpf', P(b)], writes=[key])

        def odd_layer(st, l, t0, O_vc):
            T, TS, NS = st.T, st.TS, st.NS
            i2 = l // 2
            wio3 = WB['wio'][i2].rearrange("(kc p) n -> p kc n", p=128)
            rmsnorm_x(st, gmix, l)
            S.dma('sp', gvc_bc[:, :], I['g_vc'][i2].partition_broadcast(128), reads=[], writes=['gvc_bc'])
            o0 = st.o0
            Wu, ku = wreq(wio3[:, :, 0:512], 128, 8, 512)
            Wvc, kvc = wreq(wio3[:, :, 512:1024], 128, 8, 512)
            for c in range(4):
                b = bank('gen')
                for kc in range(8):
                    mm(ps[b][:, :T], Wu[:, kc, c * 128:(c + 1) * 128], xn[:, kc, :T], kc == 0, kc == 7, ['xn', ku], [P(b)])
                gelu_from_psum(b, 128, T, ug[:, c, :T], 'ug')
            for s in range(NS):
                ts = slice(s * TS, (s + 1) * TS)
                b = bank('gen')
                for kc in range(8):
                    mm(ps[b][:TS, :], xn[:, kc, ts], Wvc[:, kc, :], kc == 0, kc == 7, ['xn', kvc], [P(b)])
                gelu_from_psum(b, TS, 512, zf[:TS, :], 'qf')
                act(sqh[:TS, :], zf[:TS, :], AF.Square, ['qf'], ['tmpf'])
                V('dve', 'tensor_reduce', rs1[:TS, 0:1], sqh[:TS, :], AX.X, ALU.add, reads=['tmpf'], writes=['rs1'])
                V('dve', 'tensor_scalar', rs1[:TS, 0:1], rs1[:TS, 0:1], 1.0 / 512, EPS, reads=['rs1'], writes=['rs1'], op0=ALU.mult, op1=ALU.add)
                act(rs1[:TS, 0:1], rs1[:TS, 0:1], AF.Sqrt, ['rs1'], ['rs1'])
                V('dve', 'reciprocal', rs1[:TS, 0:1], rs1[:TS, 0:1], reads=['rs1'], writes=['rs1'])
                V('dve', 'scalar_tensor_tensor', zf[:TS, :], zf[:TS, :], rs1[:TS, 0:1], gvc_bc[:TS, :], reads=['qf', 'rs1', 'gvc_bc'], writes=['qf'],
                  op0=ALU.mult, op1=ALU.mult)
                if O_vc is not None:
                    S.dma('act', O_vc[i2, o0 + s * TS:o0 + (s + 1) * TS, :], zf[:TS, :], reads=['qf'], writes=['O_vc'])
                act(vcb[:TS, :], zf[:TS, :], AF.Copy, ['qf'], ['kb'])
                b = bank('gen')
                for g in range(4):
                    mm(ps[b][:, g * TS:(g + 1) * TS], vcb[:TS, g * 128:(g + 1) * 128], wsT[:TS, i2, g, :TS], True, True, ['kb', 'wsT'], [P(b)])
                V('dve', 'tensor_tensor', tmpg[:, :4 * TS].rearrange("p (g t) -> p g t", g=4), ps[b][:, :4 * TS].rearrange("p (g t) -> p g t", g=4),
                  bs_bc[:, i2, :, :TS], ALU.add, reads=[P(b), 'bs_bc'], writes=['tmpg'])
                V('dve', 'tensor_tensor', oc[:, :, ts], tmpg[:, :4 * TS].rearrange("p (g t) -> p g t", g=4), ug[:, :, ts], ALU.mult,
                  reads=['tmpg', 'ug'], writes=['oc'])
            Wad, kad = wreq(wio3[:, :, 1024:1536], 128, 8, 512)
            Wgd, kgd = wreq(wio3[:, :, 1536:2048], 128, 8, 512)
            V('dve', 'tensor_copy', ub[:, :, 0:30], st.histD[:, :, i2 * 30:i2 * 30 + 30], reads=[st.kD], writes=['ub'])
            for c in range(4):
                cs = slice(c * 128, (c + 1) * 128)
                b = bank('gen')
                for kc in range(8):
                    mm(ps[b][:, :T], Wad[:, kc, cs], xn[:, kc, :T], kc == 0, kc == 7, ['xn', kad], [P(b)])
                act(tmpf[:, :T], ps[b][:, :T], AF.Copy, [P(b)], ['tmpf'])
                b = bank('gen')
                for kc in range(8):
                    mm(ps[b][:, :T], Wgd[:, kc, cs], xn[:, kc, :T], kc == 0, kc == 7, ['xn', kgd], [P(b)])
                act(tmpg[:, :T], ps[b][:, :T], AF.Sigmoid, [P(b)], ['tmpg'])
                V('dve', 'tensor_tensor', ub[:, c, 30:30 + T], tmpf[:, :T], tmpg[:, :T], ALU.mult, reads=['tmpf', 'tmpg'], writes=['ub'])
                e = 'dve'
                V(e, 'tensor_scalar', cx[:, c, :T], ub[:, c, 30:30 + T], wD[:, c, i2 * 31 + 30:i2 * 31 + 31], None, reads=['ub'], writes=[('cx', c)], op0=ALU.mult)
                for j in range(30):
                    fma(e, cx[:, c, :T], ub[:, c, j:j + T], wD[:, c, i2 * 31 + j:i2 * 31 + j + 1], ['ub'], ('cx', c))
            V('dve', 'tensor_copy', st.histD[:, :, i2 * 30:i2 * 30 + 30], ub[:, :, T:T + 30], reads=['ub'], writes=[st.kD])
            for c in range(4):
                b2 = c % 2
                act(sq[:, b2, :T], cx[:, c, :T], AF.Square, [('cx', c)], [('sq', b2)])
                mm(ps[7][:, :T], ones_b[:], sq[:, b2, :T], c == 0, c == 3, [('sq', b2), 'ones_b'], [P(7)], sig=True)
            V('dve', 'tensor_scalar', rstd[:, :T], ps[7][:, :T], 1.0 / 512, EPS, reads=[P(7)], writes=['rstd'], op0=ALU.mult, op1=ALU.add)
            act(rstd[:, :T], rstd[:, :T], AF.Sqrt, ['rstd'], ['rstd'])
            V('dve', 'reciprocal', rstd[:, :T], rstd[:, :T], reads=['rstd'], writes=['rstd'])
            for c in range(4):
                V('dve', 'scalar_tensor_tensor', tmpf[:, :T], cx[:, c, :T], gd[:, c, i2:i2 + 1], rstd[:, :T], reads=[('cx', c), 'rstd'], writes=['tmpf'],
                  op0=ALU.mult, op1=ALU.mult)
                act(ob[:, c, :T], tmpf[:, :T], AF.Silu, ['tmpf'], ['ob'])
            woo3 = WB['woo'][i2].rearrange("(c p) n -> p c n", p=128)
            W0, k0 = wreq(woo3[:, 0:4, :], 128, 4, 1024)
            W1, k1 = wreq(woo3[:, 4:8, :], 128, 4, 1024)
            for dc in range(8):
                b = bank('gen')
                for c in range(4):
                    mm(ps[b][:, :T], W0[:, c, dc * 128:(dc + 1) * 128], oc[:, c, :T], c == 0, False, ['oc', k0], [P(b)])
                for c in range(4):
                    mm(ps[b][:, :T], W1[:, c, dc * 128:(dc + 1) * 128], ob[:, c, :T], False, c == 3, ['ob', k1], [P(b)])
                residual_add(st, b, dc)

        def conv3_ffn(st, l, j, b, hbi, e, outap, outkey):
            T = st.T
            h = hb[hbi]
            hk = ('hb', hbi)
            hh = ('hbh', hbi)
            V('dve', 'tensor_copy', h[:, 0:2], st.histF[:, j, l * 2:l * 2 + 2], reads=[st.kF], writes=[hh])
            act(h[:, 2:2 + T], ps[b][:, :T], AF.Copy, [P(b)], [hk])
            act(outap, ps[b][:, :T], AF.Copy, [P(b)], [outkey], scale=wF[:, j, l * 3 + 2:l * 3 + 3])
            for jj in (1, 0):
                fma('dve', outap, h[:, jj:jj + T], wF[:, j, l * 3 + jj:l * 3 + jj + 1], [hk, hh], outkey)
            act(st.histF[:, j, l * 2:l * 2 + 2], h[:, T:T + 2], AF.Copy, [hk], [st.kF])

        def ffn(st, l):
            T = st.T
            rmsnorm_x(st, gffn, l)
            wup3 = WB['wup'][l].rearrange("(kc p) n -> p kc n", p=128)
            wdn3 = WB['wdn'][l].rearrange("(fc p) n -> p fc n", p=128)
            for half in range(2):
                f0 = half * 11
                groups = [(f0, 4), (f0 + 4, 4), (f0 + 8, 3)]
                for (g0, nch) in groups:
                    Wg, kg = wreq(wup3[:, :, g0 * 128:(g0 + nch) * 128], 128, 8, nch * 128)
                    Wv2, kv2 = wreq(wup3[:, :, DFF + g0 * 128:DFF + (g0 + nch) * 128], 128, 8, nch * 128)
                    for jj in range(nch):
                        j = g0 + jj
                        cs = slice(jj * 128, (jj + 1) * 128)
                        bg_ = bank('gen')
                        for kc in range(8):
                            mm(ps[bg_][:, :T], Wg[:, kc, cs], xn[:, kc, :T], kc == 0, kc == 7, ['xn', kg], [P(bg_)])
                        bv_ = bank('gen')
                        for kc in range(8):
                            mm(ps[bv_][:, :T], Wv2[:, kc, cs], xn[:, kc, :T], kc == 0, kc == 7, ['xn', kv2], [P(bv_)])
                        conv3_ffn(st, l, j, bg_, 0, 'dve', ca[0][:, :T], ('ca', 0))
                        conv3_ffn(st, l, 22 + j, bv_, 1, 'pool', ca[1][:, :T], ('ca', 1))
                        act(sgt[:, :T], ca[0][:, :T], AF.Silu, [('ca', 0)], ['sgt'])
                        V('dve', 'tensor_tensor', gbuf[:, j - f0, :T], sgt[:, :T], ca[1][:, :T], ALU.mult, reads=['sgt', ('ca', 1)], writes=[('gbuf', j - f0)])
                Wd = [wreq(wdn3[:, g0:g0 + n, :], 128, n, 1024) for (g0, n) in groups]
                for dc in range(8):
                    b = bank('gen')
                    idx = 0
                    for gi, (g0, n) in enumerate(groups):
                        for ff in range(n):
                            fc = g0 + ff - f0
                            mm(ps[b][:, :T], Wd[gi][0][:, ff, dc * 128:(dc + 1) * 128], gbuf[:, fc, :T], idx == 0, idx == 10,
                               [('gbuf', fc), Wd[gi][1]], [P(b)])
                            idx += 1
                    residual_add(st, b, dc)

        def load_x(st, src, t0):
            T, TS, NS = st.T, st.TS, st.NS
            for s in range(NS):
                S.dma('sp', xstage[:TS, :], src[t0 + s * TS:t0 + (s + 1) * TS, :], reads=[], writes=['xstage'])
                for k0 in (0, 4):
                    b = bank('gen')
                    for kk2 in range(4):
                        kc = k0 + kk2
                        mm(ps[b][:, kk2 * TS:(kk2 + 1) * TS], xstage[:TS, kc * 128:(kc + 1) * 128], ident_f[:TS, :TS], True, True,
                           ['xstage', 'ident_f'], [P(b)])
                    act(xT[:, k0:k0 + 4, s * TS:(s + 1) * TS], ps[b][:, :4 * TS].rearrange("p (k t) -> p k t", k=4), AF.Copy, [P(b)], ['xT'])

        def store_x(st, dst, t0):
            T, TS, NS = st.T, st.TS, st.NS
            for s in range(NS):
                for k0 in (0, 4):
                    b = bank('gen')
                    for kk2 in range(4):
                        kc = k0 + kk2
                        mm(ps[b][:TS, kk2 * 128:(kk2 + 1) * 128], xT[:, kc, s * TS:(s + 1) * TS], ident_f[:, :], True, True, ['xT', 'ident_f'], [P(b)])
                    act(xstage[:TS, k0 * 128:(k0 + 4) * 128], ps[b][:TS, :], AF.Copy, [P(b)], ['xstage'])
                S.dma('act', dst[t0 + s * TS:t0 + (s + 1) * TS, :], xstage[:TS, :], reads=['xstage'], writes=['O_y'])

        def store_T(src3, key, ncn, R, r0, dst):
            for c0 in range(0, ncn, 4):
                n = min(4, ncn - c0)
                b = bank('gen')
                for c in range(n):
                    mm(ps[b][:R, c * 128:(c + 1) * 128], src3[:, c0 + c, r0:r0 + R], ident_f[:, :], True, True, [key, 'ident_f'], [P(b)])
                act(ostage[:R, :n * 128], ps[b][:R, :n * 128], AF.Copy, [P(b)], ['tmpg'])
                S.dma('act', dst[:, c0 * 128:(c0 + n) * 128], ostage[:R, :n * 128], reads=['tmpg'], writes=['O_st'])

        def run_layers(st, t0, O_k, O_v, O_lf, O_vc):
            for l in range(NL):
                if l % 2 == 0:
                    even_layer(st, l, t0, O_k, O_v, O_lf)
                else:
                    odd_layer(st, l, t0, O_vc)
                if DBG >= 6:
                    ffn(st, l)

        def store_states(st, O_cb, O_cd, O_cf):
            for i2 in range(2):
                store_T(st.histB, st.kB, 4, 2, i2 * 2, O_cb[i2 * 2:i2 * 2 + 2, :])
                store_T(st.histD, st.kD, 4, 30, i2 * 30, O_cd[i2 * 30:i2 * 30 + 30, :])
            for l in range(4):
                store_T(st.histF, st.kF, 44, 2, l * 2, O_cf[l * 2:l * 2 + 2, :])

        if do_sample:
            ss = make_stream('s', DEC, PAST + 512)
            load_T(I['state_conv_ffn'], 8, 5632, ss.histF, ss.kF)
            load_T(I['state_conv_b'], 4, 512, ss.histB, ss.kB)
            load_T(I['state_conv_d'], 60, 512, ss.histD, ss.kD)
            V('pool', 'memset', ss.carry[:], 0.0, reads=[], writes=[ss.kC])
            pre = Stream()
            pre.sid, pre.T, pre.TS, pre.NS = 's', 512, 128, 4
            pre.KT, pre.Vs, pre.carry = ss.KT, ss.Vs, ss.carry
            pre.kC = ss.kC
            for i2 in range(2):
                if 2 * i2 >= NL:
                    continue
                for t0 in range(0, PAST, 512):
                    for s in range(4):
                        r0 = t0 + s * 128
                        S.dma('sp', kf[:, :], I['cache_k'][i2, r0:r0 + 128, :], reads=[], writes=['kf'])
                        act(kb[:, :], kf[:, :], AF.Copy, ['kf'], ['kb'])
                        transp_heads(pre, kb, kT, 0, 'kb', 'kT')
                        S.dma('act', ss.KT[i2][:, 0:64, r0:r0 + 128].rearrange("h d t -> d h t"), kT[:64, :, :128], reads=['kT'], writes=[('KV', 's', i2)])
                        S.dma('sp', vf[:, :], I['cache_v'][i2, r0:r0 + 128, :], reads=[], writes=['vf'])
                        V('dve', 'tensor_copy', vb[:, 0, :, 0:64], vf[:, :].rearrange("p (h d) -> p h d", h=8), reads=['vf'], writes=['vb'])
                        S.dma('act', ss.Vs[i2][:, 0:128, r0 // 128, :].rearrange("h p d -> p h d"), vb[:, 0, :, :], reads=['vb'], writes=[('KV', 's', i2)])
                        S.dma('sp', logf[:, s, :], I['cache_logf'][i2, r0:r0 + 128, :], reads=[], writes=['logf'])
                    kv_finish(pre, i2, t0)
            load_x(ss, I['x_sample'], 0)
            ss.o0 = 0
            run_layers(ss, PAST, O['s_k'], O['s_v'], O['s_lf'], O['s_vc'])
            store_x(ss, O['y_s'], 0)
            store_states(ss, O['s_cb'], O['s_cd'], O['s_cf'])

        if NT > 0:
            sp_ = make_stream('p', 512, NTOK)
            for tname, tk in ((sp_.histF, sp_.kF), (sp_.histB, sp_.kB), (sp_.histD, sp_.kD), (sp_.carry, sp_.kC)):
                V('pool', 'memset', tname[:], 0.0, reads=[], writes=[tk])
            for ti in range(NT):
                t0 = ti * 512
                sp_.o0 = t0
                load_x(sp_, I['x_prompt'], t0)
                run_layers_prompt = run_layers
                run_layers_prompt(sp_, t0, O['p_k'], O['p_v'], O['p_lf'], None)
                store_x(sp_, O['y_p'], t0)
            store_states(sp_, O['p_cb'], O['p_cd'], O['p_cf'])

        S.finish('sp')
        print("SBUF remaining at end:", nc.sbuf_bytes_remaining)
        S.simulate()
        print("instructions:", S.nins, "counts:", {k: v for k, v in S.cnt.items() if v})
    return nc


_NC_CACHE = {}


def _prep_inputs(inputs, c, NT):
    f = lambda a: np.ascontiguousarray(a, dtype=np.float32)
    b = c % 2
    m = {
        'x_prompt': f(inputs['x_prompt'][b, :NT * 512]),
        'x_sample': f(inputs['x_sample'][c]),
        'cache_k': f(inputs['cache_k'][:, c].reshape(2, PAST, 512)),
        'cache_v': f(inputs['cache_v'][:, c].reshape(2, PAST, 512)),
        'cache_logf': f(inputs['cache_logf'][:, c]),
        'state_conv_b': f(inputs['state_conv_b'][:, c].reshape(4, 512)),
        'state_conv_d': f(inputs['state_conv_d'][:, c].reshape(60, 512)),
        'state_conv_ffn': f(inputs['state_conv_ffn'][:, c].reshape(8, 5632)),
        'conv_b': f(inputs['conv_b'].reshape(6, 512)),
        'conv_d': f(inputs['conv_d'].reshape(62, 512)),
        'conv_ffn': f(inputs['conv_ffn'].reshape(12, 5632)),
    }
    for k in ['g_mix', 'w_in_even', 'b_f', 'g_q', 'g_k', 'w_out_even', 'w_in_odd', 'g_vc', 'w_s', 'b_s', 'g_d',
              'w_out_odd', 'g_ffn', 'w_up', 'w_down']:
        m[k] = f(inputs[k])
    return m


def run(inputs, NT=32, NL=4):
    key = (NT, NL)
    if key not in _NC_CACHE:
        _NC_CACHE[key] = build(NT, NL)
    nc = _NC_CACHE[key]
    in_maps = [_prep_inputs(inputs, c, NT) for c in range(8)]
    res = run_bass_kernel_spmd(nc, in_maps, core_ids=list(range(8)))
    R = res.results
    B = 2
    T = NT * 512
    y_p = np.stack([R[b]['y_p'] for b in range(B)])
    y_s = np.stack([R[c]['y_s'] for c in range(8)])
    pk = np.stack([R[b]['p_k'] for b in range(B)], axis=1).reshape(2, B, T, 8, 64)
    pv = np.stack([R[b]['p_v'] for b in range(B)], axis=1).reshape(2, B, T, 8, 64)
    plf = np.stack([R[b]['p_lf'] for b in range(B)], axis=1)
    pcb = np.stack([R[b]['p_cb'].reshape(2, 2, 512) for b in range(B)], axis=1)
    pcd = np.stack([R[b]['p_cd'].reshape(2, 30, 512) for b in range(B)], axis=1)
    pcf = np.stack([R[b]['p_cf'].reshape(4, 2, 5632) for b in range(B)], axis=1)
    sk = np.stack([R[c]['s_k'] for c in range(8)], axis=1).reshape(2, 8, DEC, 8, 64)
    sv = np.stack([R[c]['s_v'] for c in range(8)], axis=1).reshape(2, 8, DEC, 8, 64)
    slf = np.stack([R[c]['s_lf'] for c in range(8)], axis=1)
    scb = np.stack([R[c]['s_cb'].reshape(2, 2, 512) for c in range(8)], axis=1)
    svc = np.stack([R[c]['s_vc'] for c in range(8)], axis=1)
    scd = np.stack([R[c]['s_cd'].reshape(2, 30, 512) for c in range(8)], axis=1)
    scf = np.stack([R[c]['s_cf'].reshape(4, 2, 5632) for c in range(8)], axis=1)
    return tuple(np.ascontiguousarray(a, dtype=np.float32) for a in
                 (y_p, y_s, pk, pv, plf, pcb, pcd, pcf, sk, sv, slf, scb, svc, scd, scf))


def kernel(**inputs):
    return run(inputs, NT=32, NL=4)
```

```python
import os
import numpy as np
from contextlib import ExitStack
DBG = float(os.environ.get("MK_DBG", "9"))
import concourse.bass as bass
import concourse.mybir as mybir
from concourse.bass_utils import run_bass_kernel_spmd

F32 = mybir.dt.float32
BF16 = mybir.dt.bfloat16
ALU = mybir.AluOpType
AF = mybir.ActivationFunctionType
AX = mybir.AxisListType

D = 1024
DFF = 2816
PAST = 4096
DEC = 64
EPS = 1e-6
OFF_BG = 1544
NEG = -30000.0


class Sched:
    NDS = 4

    def __init__(self, nc):
        self.nc = nc
        self.eng = {'pe': nc.tensor, 'act': nc.scalar, 'dve': nc.vector, 'pool': nc.gpsimd, 'sp': nc.sync}
        self.sem = {}
        self.cnt = {}
        for e in ['pe', 'act', 'dve', 'pool']:
            self.sem[e] = nc.alloc_semaphore(name=f"s_{e}")
            self.cnt[e] = 0
        self.dq = ['sp', 'pool', 'act']
        for q in self.dq:
            for j in range(self.NDS):
                self.sem[(q, j)] = nc.alloc_semaphore(name=f"d_{q}{j}")
                self.cnt[(q, j)] = 0
        self.dcount = {q: 0 for q in self.dq}
        self.seen = {e: {} for e in self.eng}
        self.lastw = {}
        self.readers = {}
        self.nins = 0
        self.ev = {e: [] for e in self.eng}

    def simulate(self):
        val = {k: 0 for k in self.sem}
        pc = {e: 0 for e in self.ev}
        prog = True
        while prog:
            prog = False
            for e, lst in self.ev.items():
                while pc[e] < len(lst):
                    kind, s, v, info = lst[pc[e]]
                    if kind == 'wait':
                        if val[s] >= v:
                            pc[e] += 1
                            prog = True
                        else:
                            break
                    else:
                        val[s] += v
                        pc[e] += 1
                        prog = True
        stuck = {e: (pc[e], len(l), l[pc[e]] if pc[e] < len(l) else None) for e, l in self.ev.items()}
        ok = all(pc[e] == len(l) for e, l in self.ev.items())
        print("SIM", "OK" if ok else "DEADLOCK", stuck if not ok else "")
        if not ok:
            print({k: v for k, v in val.items()})
        return ok

    def _deps(self, reads, writes):
        deps = set()
        for k in reads:
            if k in self.lastw:
                deps.add(self.lastw[k])
        for k in writes:
            if k in self.lastw:
                deps.add(self.lastw[k])
            for r in self.readers.get(k, ()):
                deps.add(r)
        return deps

    def _wait(self, e, deps):
        need = {}
        for (s, c) in deps:
            if c > need.get(s, 0):
                need[s] = c
        for s, c in need.items():
            if self.seen[e].get(s, 0) >= c:
                continue
            unit = 16 if isinstance(s, tuple) else 1
            self.eng[e].wait_ge(self.sem[s], c * unit)
            self.ev[e].append(('wait', s, c * unit, None))
            self.seen[e][s] = c

    def _record(self, tok, reads, writes):
        for k in reads:
            lst = self.readers.setdefault(k, [])
            lst.append(tok)
            if len(lst) > 64:
                best = {}
                for (s, c) in lst:
                    if c > best.get(s, 0):
                        best[s] = c
                self.readers[k] = [(s, c) for s, c in best.items()]
        for k in writes:
            self.lastw[k] = tok
            self.readers[k] = []

    def op(self, e, fn, reads=(), writes=(), signal=True):
        deps = self._deps(reads, writes)
        if e == 'pe':
            deps = {d for d in deps if d[0] != 'pe'}
        self._wait(e, deps)
        ins = fn()
        tok = (e, self.cnt[e] + 1)
        if signal:
            ins.then_inc(self.sem[e], 1)
            self.cnt[e] += 1
            self.ev[e].append(('inc', e, 1, self.nins))
        self._record(tok, reads, writes)
        self.nins += 1
        return ins

    def dma(self, q, out, in_, reads=(), writes=(), **kw):
        if q == 'act' and os.environ.get("MK_ACTQ", "sp") != "act":
            q = os.environ.get("MK_ACTQ", "sp")
        deps = self._deps(reads, writes)
        j = self.dcount[q] % self.NDS
        self.dcount[q] += 1
        s = (q, j)
        if self.cnt[s] > 0:
            deps = set(deps)
            deps.add((s, self.cnt[s]))
        self._wait(q, deps)
        ins = self.eng[q].dma_start(out=out, in_=in_, **kw)
        ins.then_inc(self.sem[s], 16)
        self.cnt[s] += 1
        self.ev[q].append(('inc', s, 16, self.nins))
        tok = (s, self.cnt[s])
        self._record(tok, reads, writes)
        self.nins += 1
        return ins

    def barrier(self):
        deps = set()
        for s in self.cnt:
            if self.cnt[s] > 0:
                deps.add((s, self.cnt[s]))
        for e in ['pe', 'act', 'dve', 'pool', 'sp']:
            self._wait(e, deps)

    def finish(self, e='sp'):
        deps = set()
        for k, t in self.lastw.items():
            deps.add(t)
        for s in self.cnt:
            if self.cnt[s] > 0:
                deps.add((s, self.cnt[s]))
        self._wait(e, deps)


class Stream:
    pass


def build(NT=32, NL=4, do_sample=True):
    nc = bass.Bass("TRN2", target_bir_lowering=False)
    NTOK = NT * 512

    def din(name, shape):
        return nc.dram_tensor(name, list(shape), F32, kind="ExternalInput").ap()

    def dout(name, shape):
        return nc.dram_tensor(name, list(shape), F32, kind="ExternalOutput").ap()

    def dscr(name, shape, dt=BF16):
        return nc.dram_tensor(name, list(shape), dt, kind="Internal").ap()

    I = dict(
        x_prompt=din("x_prompt", [NTOK, D]), x_sample=din("x_sample", [DEC, D]),
        cache_k=din("cache_k", [2, PAST, 512]), cache_v=din("cache_v", [2, PAST, 512]),
        cache_logf=din("cache_logf", [2, PAST, 8]),
        state_conv_b=din("state_conv_b", [4, 512]), state_conv_d=din("state_conv_d", [60, 512]),
        state_conv_ffn=din("state_conv_ffn", [8, 5632]),
        g_mix=din("g_mix", [4, D]), w_in_even=din("w_in_even", [2, D, 3080]), b_f=din("b_f", [2, 8]),
        g_q=din("g_q", [2, 64]), g_k=din("g_k", [2, 64]), conv_b=din("conv_b", [6, 512]),
        w_out_even=din("w_out_even", [2, D, D]), w_in_odd=din("w_in_odd", [2, D, 2048]),
        g_vc=din("g_vc", [2, 512]), w_s=din("w_s", [2, 4, 128, 128]), b_s=din("b_s", [2, 4, 128]),
        conv_d=din("conv_d", [62, 512]), g_d=din("g_d", [2, 512]), w_out_odd=din("w_out_odd", [2, D, D]),
        g_ffn=din("g_ffn", [4, D]), w_up=din("w_up", [4, D, 5632]), conv_ffn=din("conv_ffn", [12, 5632]),
        w_down=din("w_down", [4, DFF, D]),
    )
    O = dict(
        y_p=dout("y_p", [NTOK, D]), y_s=dout("y_s", [DEC, D]),
        p_k=dout("p_k", [2, NTOK, 512]), p_v=dout("p_v", [2, NTOK, 512]), p_lf=dout("p_lf", [2, NTOK, 8]),
        p_cb=dout("p_cb", [4, 512]), p_cd=dout("p_cd", [60, 512]), p_cf=dout("p_cf", [8, 5632]),
        s_k=dout("s_k", [2, DEC, 512]), s_v=dout("s_v", [2, DEC, 512]), s_lf=dout("s_lf", [2, DEC, 8]),
        s_cb=dout("s_cb", [4, 512]), s_vc=dout("s_vc", [2, DEC, 512]), s_cd=dout("s_cd", [60, 512]),
        s_cf=dout("s_cf", [8, 5632]),
    )
    WB = dict(
        wie=dscr("wie_b", [2, D, 3080]), woe=dscr("woe_b", [2, D, D]), wio=dscr("wio_b", [2, D, 2048]),
        woo=dscr("woo_b", [2, D, D]), wup=dscr("wup_b", [4, D, 5632]), wdn=dscr("wdn_b", [4, DFF, D]),
    )
    cq_scr = dscr("cq_scr", [8, 3, 512])

    S = Sched(nc)
    with ExitStack() as es:
        def SB(name, shape, dt=F32):
            return es.enter_context(nc.sbuf_tensor(name, list(shape), dt))

        ps = [es.enter_context(nc.psum_tensor(f"ps{i}", [128, 512], F32)) for i in range(8)]
        rot = {'gen': [0, [0, 1, 2]], 'S': [0, [3, 4]], 'O': [0, [5, 6]]}

        def bank(kind):
            r = rot[kind]
            b = r[1][r[0] % len(r[1])]
            r[0] += 1
            return b

        def P(b):
            return ('ps', b)

        def mm(out, lhsT, rhs, start, stop, reads, writes, sig=None):
            S.op('pe', lambda: nc.tensor.matmul(out, lhsT=lhsT, rhs=rhs, start=start, stop=stop),
                 reads=reads, writes=writes, signal=(stop if sig is None else sig))

        def act(out, in_, func, reads, writes, **kw):
            S.op('act', lambda: nc.scalar.activation(out, in_, func, **kw), reads=reads, writes=writes)

        def V(e, name, *args, reads, writes, **kw):
            eng = nc.vector if e == 'dve' else nc.gpsimd
            S.op(e, lambda: getattr(eng, name)(*args, **kw), reads=reads, writes=writes)

        def fma(e, acc, src, wcol, rkeys, akey):
            V('dve', 'scalar_tensor_tensor', acc, src, wcol, acc, reads=rkeys + [akey], writes=[akey], op0=ALU.mult, op1=ALU.add)

        ones_f = SB("ones_f", [128, 128])
        ident_f = SB("ident_f", [128, 128])
        utri_f = SB("utri_f", [128, 128])
        ident_b = SB("ident_b", [128, 128], BF16)
        ones_b = SB("ones_b", [128, 128], BF16)
        zeros_b = SB("zeros_b", [128, 512], BF16)
        dmask = SB("dmask", [128, 4, 512], BF16)
        ones3 = SB("ones3", [8, 3, 512], BF16)
        V('pool', 'memset', ones_f[:], 1.0, reads=[], writes=['ones_f'])
        V('pool', 'memset', ones_b[:], 1.0, reads=[], writes=['ones_b'])
        V('pool', 'memset', zeros_b[:], 0.0, reads=[], writes=['zeros_b'])
        V('pool', 'memset', ones3[:], 1.0, reads=[], writes=['ones3'])
        S.op('pool', lambda: nc.gpsimd.affine_select(ident_f[:], ones_f[:], [[-1, 128]], ALU.is_equal, 0.0, base=0, channel_multiplier=1),
             reads=['ones_f'], writes=['ident_f'])
        S.op('pool', lambda: nc.gpsimd.affine_select(utri_f[:], ones_f[:], [[1, 128]], ALU.is_ge, 0.0, base=0, channel_multiplier=-1),
             reads=['ones_f'], writes=['utri_f'])
        V('pool', 'tensor_copy', ident_b[:], ident_f[:], reads=['ident_f'], writes=['ident_b'])
        for r in range(4):
            S.op('pool', lambda: nc.gpsimd.affine_select(dmask[:, r, :], zeros_b[:], [[1, 512]], ALU.is_ge, NEG, base=-128 * r, channel_multiplier=-1),
                 reads=['zeros_b'], writes=['dmask'])

        cx = SB("cx", [128, 4, 512])
        ob = SB("ob", [128, 4, 512], BF16)
        castf = cx[:, :, :].rearrange("p (a b) c -> p a (b c)", a=2)
        castb = ob[:, :, :].rearrange("p (a b) c -> p a (b c)", a=2)
        ci = [0]

        def cast_weight(src, dst, rows, cols):
            for r0 in range(0, rows, 128):
                for c0 in range(0, cols, 1024):
                    cw = min(1024, cols - c0)
                    b = ci[0] % 2
                    e = ['dve', 'pool'][ci[0] % 2]
                    ci[0] += 1
                    S.dma('sp', castf[:, b, :cw], src[r0:r0 + 128, c0:c0 + cw], reads=[], writes=[('castf', b)])
                    V(e, 'tensor_copy', castb[:, b, :cw], castf[:, b, :cw], reads=[('castf', b)], writes=[('castb', b)])
                    S.dma('act', dst[r0:r0 + 128, c0:c0 + cw], castb[:, b, :cw], reads=[('castb', b)], writes=['WB'])

        for i2 in range(2):
            cast_weight(I['w_in_even'][i2], WB['wie'][i2], D, 3080)
            cast_weight(I['w_out_even'][i2], WB['woe'][i2], D, D)
            cast_weight(I['w_in_odd'][i2], WB['wio'][i2], D, 2048)
            cast_weight(I['w_out_odd'][i2], WB['woo'][i2], D, D)
        for l in range(4):
            cast_weight(I['w_up'][l], WB['wup'][l], D, 5632)
            cast_weight(I['w_down'][l], WB['wdn'][l], DFF, D)

        S.barrier()
        xstage = SB("xstage", [128, 1024])
        stage = xstage

        def load_T(src, R, C, dst3, key):
            ncn = C // 128
            per = min(512 // R, 8)
            for c0 in range(0, ncn, per):
                n = min(per, ncn - c0)
                S.dma('sp', stage[:R, :n * 128], src[:, c0 * 128:(c0 + n) * 128], reads=[], writes=['xstage'])
                b = bank('gen')
                for c in range(n):
                    mm(ps[b][:, c * R:(c + 1) * R], stage[:R, c * 128:(c + 1) * 128], ident_f[:R, :R], True, True,
                       ['xstage', 'ident_f'], [P(b)])
                act(dst3[:, c0:c0 + n, :], ps[b][:, :n * R].rearrange("p (c r) -> p c r", r=R), AF.Copy, [P(b)], [key])

        gmix = SB("gmix", [128, 8, 4]); load_T(I['g_mix'], 4, D, gmix, 'gmix')
        gffn = SB("gffn", [128, 8, 4]); load_T(I['g_ffn'], 4, D, gffn, 'gffn')
        gd = SB("gd", [128, 4, 2]); load_T(I['g_d'], 2, 512, gd, 'gd')
        wB = SB("wB", [128, 4, 6]); load_T(I['conv_b'], 6, 512, wB, 'wB')
        wD = SB("wD", [128, 4, 62]); load_T(I['conv_d'], 62, 512, wD, 'wD')
        wF = SB("wF", [128, 44, 12]); load_T(I['conv_ffn'], 12, 5632, wF, 'wF')

        gq_t = SB("gq_t", [128, 2, 64]); gk_t = SB("gk_t", [128, 2, 64])
        gq_bc = SB("gq_bc", [128, 512]); gk_bc = SB("gk_bc", [128, 512])
        gvc_bc = SB("gvc_bc", [128, 512]); bs_bc = SB("bs_bc", [128, 2, 4, 128]); bf_bc = SB("bf_bc", [128, 2, 8])
        for i2 in range(2):
            S.dma('sp', gq_t[:, i2, :], I['g_q'][i2].partition_broadcast(128), reads=[], writes=['gq_t'])
            S.dma('sp', gk_t[:, i2, :], I['g_k'][i2].partition_broadcast(128), reads=[], writes=['gk_t'])
            S.dma('sp', bf_bc[:, i2, :], I['b_f'][i2].partition_broadcast(128), reads=[], writes=['bf_bc'])
            for g in range(4):
                S.dma('sp', bs_bc[:, i2, g, :], I['b_s'][i2, g].partition_broadcast(128), reads=[], writes=['bs_bc'])

        def load_gqk(i2):
            V('dve', 'tensor_scalar', gq_bc[:, :].rearrange("p (h d) -> p h d", h=8), gq_t[:, i2, :].unsqueeze(1).to_broadcast([128, 8, 64]),
              0.125, None, reads=['gq_t'], writes=['gq_bc'], op0=ALU.mult)
            V('dve', 'tensor_scalar', gk_bc[:, :].rearrange("p (h d) -> p h d", h=8), gk_t[:, i2, :].unsqueeze(1).to_broadcast([128, 8, 64]),
              1.0, None, reads=['gk_t'], writes=['gk_bc'], op0=ALU.mult)
        wf_f = SB("wf_f", [128, 2, 8, 8]); wf_b = SB("wf_b", [128, 2, 8, 8], BF16)
        for i2 in range(2):
            S.dma('sp', wf_f[:, i2, :, :], I['w_in_even'][i2].rearrange("(kc p) n -> p kc n", p=128)[:, :, 1536:1544], reads=[], writes=['wf_f'])
        V('dve', 'tensor_copy', wf_b[:], wf_f[:], reads=['wf_f'], writes=['wf_b'])
        wsT = SB("wsT", [128, 2, 4, 128], BF16)
        wsf = SB("wsf", [128, 128])
        for i2 in range(2):
            for g in range(4):
                S.dma('sp', wsf[:], I['w_s'][i2, g], reads=[], writes=['wsf'])
                S.op('pool', lambda: nc.gpsimd.affine_select(wsf[:], wsf[:], [[-1, 128]], ALU.is_ge, 0.0, base=0, channel_multiplier=1),
                     reads=['wsf'], writes=['wsf'])
                b = bank('gen')
                mm(ps[b][:, :128], wsf[:], ident_f[:], True, True, ['wsf', 'ident_f'], [P(b)])
                act(wsT[:, i2, g, :], ps[b][:, :128], AF.Copy, [P(b)], ['wsT'])

        NWB = 4
        wbuf = [SB(f"wbuf{i}", [128, 4096], BF16) for i in range(NWB)]
        wrr = [0]

        def wreq(src3, npart, a, b):
            i = wrr[0] % NWB
            wrr[0] += 1
            view = wbuf[i][:npart, :a * b].rearrange("p (a b) -> p a b", a=a)
            S.dma('sp', view, src3, reads=['WB'], writes=[('wbuf', i)])
            return view, ('wbuf', i)

        xT = SB("xT", [128, 8, 512])
        xn = SB("xn", [128, 8, 512], BF16)
        sq = SB("sq", [128, 2, 512], BF16)
        rstd = SB("rstd", [128, 512])
        gbuf = SB("gbuf", [128, 11, 512], BF16)
        hb = [SB(f"hb{i}", [128, 514]) for i in range(2)]
        ca = [SB(f"ca{i}", [128, 512]) for i in range(2)]
        print("SBUF remaining after hb/ca:", nc.sbuf_bytes_remaining)
        sgt = SB("sgt", [128, 512])
        tmpf = SB("tmpf", [128, 512]); sqh = tmpf; ssq = SB("ssq", [128, 8]); qf = SB("qf", [128, 512]); kf = SB("kf", [128, 512])
        vf = SB("vf", [128, 512]); qb = SB("qb", [128, 512], BF16); kb = SB("kb", [128, 512], BF16)
        lz = SB("lz", [128, 8]); logf = SB("logf", [128, 4, 8])
        vb = SB("vb", [128, 1, 8, 128], BF16)
        V('pool', 'memset', vb[:], 0.0, reads=[], writes=['vb'])
        V('pool', 'memset', vb[:, :, :, 64:65], 1.0, reads=[], writes=['vb'])
        qT = SB("qT", [128, 8, 512], BF16)
        V('pool', 'memset', qT[:], 1.0, reads=[], writes=['qT'])
        kT = SB("kT", [64, 8, 128], BF16)
        cc = SB("cc", [8, 512]); cr = SB("cr", [8, 512]); caug = SB("caug", [8, 3, 512], BF16); ncaug = SB("ncaug", [8, 3, 512], BF16)
        KB = 1024
        kbuf = [SB(f"kbuf{i}", [128, KB + 512], BF16) for i in range(2)]
        vbuf = [SB(f"vbuf{i}", [128, KB // 128 + 4, 128], BF16) for i in range(2)]
        for i in range(2):
            V('pool', 'memset', kbuf[i][:], 0.0, reads=[], writes=[('kbuf', i)])
        pT = [SB(f"pT{i}", [128, 512], BF16) for i in range(3)]
        osb = SB("osb", [65, 512]); rec = osb; bcs = SB("bcs", [64, 512])
        oa = SB("oa", [64, 8, 512], BF16)
        ub = SB("ub", [128, 4, 544])
        tmpg = SB("tmpg", [128, 512])
        oc = SB("oc", [128, 4, 512], BF16)
        ug = SB("ug", [128, 4, 512], BF16)
        zf = qf; vcb = kb
        rs1 = SB("rs1", [128, 8])
        ostage = tmpg

        kvrr = [0]
        ptrr = [0]

        def make_stream(sid, T, ntok_scr):
            st = Stream()
            st.sid = sid
            st.T = T
            st.TS = min(T, 128)
            st.NS = T // st.TS
            st.KT = [dscr(f"KT_{sid}_{i}", [8, 70, ntok_scr]) for i in range(2)]
            st.Vs = [dscr(f"V_{sid}_{i}", [8, 128, ntok_scr // 128, 128]) for i in range(2)]
            st.histF = SB(f"histF_{sid}", [128, 44, 8])
            st.histB = SB(f"histB_{sid}", [128, 4, 4])
            st.histD = SB(f"histD_{sid}", [128, 4, 60])
            st.carry = SB(f"carry_{sid}", [8, 2])
            st.kF, st.kB, st.kD, st.kC = f"histF_{sid}", f"histB_{sid}", f"histD_{sid}", f"carry_{sid}"
            st.o0 = 0
            return st

        def rmsnorm_x(st, gt, l):
            T = st.T
            for kc in range(8):
                b = kc % 2
                act(sq[:, b, :T], xT[:, kc, :T], AF.Square, ['xT'], [('sq', b)])
                mm(ps[7][:, :T], ones_b[:], sq[:, b, :T], kc == 0, kc == 7, [('sq', b), 'ones_b'], [P(7)], sig=True)
            V('dve', 'tensor_scalar', rstd[:, :T], ps[7][:, :T], 1.0 / D, EPS, reads=[P(7)], writes=['rstd'], op0=ALU.mult, op1=ALU.add)
            act(rstd[:, :T], rstd[:, :T], AF.Sqrt, ['rstd'], ['rstd'])
            V('dve', 'reciprocal', rstd[:, :T], rstd[:, :T], reads=['rstd'], writes=['rstd'])
            for kc in range(8):
                V('dve', 'scalar_tensor_tensor', xn[:, kc, :T], xT[:, kc, :T], gt[:, kc, l:l + 1], rstd[:, :T],
                  reads=['xT', 'rstd'], writes=['xn'], op0=ALU.mult, op1=ALU.mult)

        def head_norm(st, b, gbc, outf, okey, fin=None, finkey=None):
            TS = st.TS
            act(sqh[:TS, :], ps[b][:TS, :], AF.Square, [P(b)], ['tmpf'])
            V('dve', 'tensor_reduce', ssq[:TS, :], sqh[:TS, :].rearrange("p (h d) -> p h d", h=8), AX.X, ALU.add, reads=['tmpf'], writes=['ssq'])
            V('dve', 'tensor_scalar', ssq[:TS, :], ssq[:TS, :], 1.0 / 64, EPS, reads=['ssq'], writes=['ssq'], op0=ALU.mult, op1=ALU.add)
            act(ssq[:TS, :], ssq[:TS, :], AF.Sqrt, ['ssq'], ['ssq'])
            V('dve', 'reciprocal', ssq[:TS, :], ssq[:TS, :], reads=['ssq'], writes=['ssq'])
            V('dve', 'tensor_tensor', outf[:TS, :].rearrange("p (h d) -> p h d", h=8), ps[b][:TS, :].rearrange("p (h d) -> p h d", h=8),
              ssq[:TS, :].unsqueeze(2).to_broadcast([TS, 8, 64]), ALU.mult, reads=[P(b), 'ssq'], writes=[okey])
            if fin is None:
                fin, finkey = outf, okey
            V('dve', 'tensor_tensor', fin[:TS, :], outf[:TS, :], gbc[:TS, :], ALU.mult, reads=[okey, 'gq_bc', 'gk_bc'], writes=[finkey])

        def transp_heads(st, src_b, dstT, s, key_src, key_dst):
            TS = st.TS
            for h0 in (0, 4):
                b = bank('gen')
                for hh in range(4):
                    h = h0 + hh
                    mm(ps[b][:64, hh * TS:(hh + 1) * TS], src_b[:TS, h * 64:(h + 1) * 64], ident_b[:TS, :TS], True, True,
                       [key_src, 'ident_b'], [P(b)])
                act(dstT[:64, h0:h0 + 4, s * TS:(s + 1) * TS], ps[b][:64, :4 * TS].rearrange("p (h t) -> p h t", h=4), AF.Copy, [P(b)], [key_dst])

        def kv_finish(st, i2, t0):
            T, TS, NS = st.T, st.TS, st.NS
            for s in range(NS):
                mm(ps[7][:8, s * TS:(s + 1) * TS], logf[:TS, s, :], utri_f[:TS, :TS], True, True, ['logf', 'utri_f'], [P(7)])
            for s in range(NS):
                V('dve', 'tensor_scalar', cc[:8, s * TS:(s + 1) * TS], ps[7][:8, s * TS:(s + 1) * TS], st.carry[:8, i2:i2 + 1], None,
                  reads=[P(7), st.kC], writes=['cc'], op0=ALU.add)
                V('dve', 'tensor_copy', st.carry[:8, i2:i2 + 1], cc[:8, (s + 1) * TS - 1:(s + 1) * TS], reads=['cc'], writes=[st.kC])
            V('dve', 'tensor_copy', caug[:, 0, :T], cc[:, :T], reads=['cc'], writes=['caug'])
            V('dve', 'tensor_tensor', cr[:, :T], cc[:, :T], caug[:, 0, :T], ALU.subtract, reads=['cc', 'caug'], writes=['cr'])
            V('dve', 'tensor_copy', caug[:, 1, :T], cr[:, :T], reads=['cr'], writes=['caug'])
            V('dve', 'tensor_tensor', cr[:, :T], cr[:, :T], caug[:, 1, :T], ALU.subtract, reads=['cr', 'caug'], writes=['cr'])
            V('dve', 'tensor_copy', caug[:, 2, :T], cr[:, :T], reads=['cr'], writes=['caug'])
            V('dve', 'tensor_scalar', ncaug[:, :, :T], caug[:, :, :T], -1.0, None, reads=['caug'], writes=['ncaug'], op0=ALU.mult)
            kk = ('KV', st.sid, i2)
            S.dma('act', st.KT[i2][:, 64:67, t0:t0 + T], ones3[:, :, :T], reads=['ones3'], writes=[kk])
            S.dma('act', st.KT[i2][:, 67:70, t0:t0 + T], ncaug[:, :, :T], reads=['ncaug'], writes=[kk])

        def attention(st, i2, t0):
            T, TS, NS = st.T, st.TS, st.NS
            kk = ('KV', st.sid, i2)
            S.dma('act', cq_scr[:, :, :T], caug[:, :, :T], reads=['caug'], writes=['cq_scr'])
            S.dma('act', qT[64:67, :, :T], cq_scr[:, :, :T].rearrange("h a t -> a h t"), reads=['cq_scr'], writes=['qT'])
            ntot = t0 + T
            chunks = []
            c0 = 0
            while c0 < ntot:
                c1 = min(c0 + KB, ntot)
                if ntot - c1 <= 512 and ntot - c1 > 0:
                    c1 = ntot
                chunks.append((c0, c1))
                c0 = c1
            blocks = []
            for ci, (c0, c1) in enumerate(chunks):
                col = 0
                while col < c1 - c0:
                    gpos = c0 + col
                    if gpos < t0:
                        ksz, r = 128, None
                    else:
                        ksz, r = TS, (gpos - t0) // TS
                    blocks.append((ci, col, ksz, r))
                    col += ksz
            nb = len(blocks)
            for h in range(8):
                obk = bank('O')
                loaded = {}

                def ensure(ci):
                    if ci not in loaded:
                        i = kvrr[0] % 2
                        kvrr[0] += 1
                        c0, c1 = chunks[ci]
                        n = c1 - c0
                        nkt = (n + 127) // 128
                        S.dma('sp', kbuf[i][:70, :n], st.KT[i2][h, :, c0:c1], reads=[kk], writes=[('kbuf', i)])
                        S.dma('sp', vbuf[i][:, :nkt, :], st.Vs[i2][h, :, c0 // 128:c0 // 128 + nkt, :], reads=[kk], writes=[('vbuf', i)])
                        loaded[ci] = i
                    return loaded[ci]

                def emitS(bi):
                    ci, col, ksz, r = blocks[bi]
                    i = ensure(ci)
                    sb_ = bank('S')
                    mm(ps[sb_][:ksz, :T], kbuf[i][:, col:col + ksz], qT[:, h, :T], True, r is None, [('kbuf', i), 'qT'], [P(sb_)])
                    if r is not None:
                        mm(ps[sb_][:ksz, :T], ident_b[:ksz, :ksz], dmask[:ksz, r, :T], False, True, ['ident_b', 'dmask'], [P(sb_)])
                    return sb_

                sbk = {0: emitS(0)}
                for bi in range(nb):
                    if bi + 1 < nb:
                        sbk[bi + 1] = emitS(bi + 1)
                    ci, col, ksz, r = blocks[bi]
                    i = loaded[ci]
                    sb_ = sbk.pop(bi)
                    pi = ptrr[0] % 3
                    ptrr[0] += 1
                    act(pT[pi][:ksz, :T], ps[sb_][:ksz, :T], AF.Exp, [P(sb_)], [('pT', pi)])
                    mm(ps[obk][:, :T], vbuf[i][:ksz, col // 128, :], pT[pi][:ksz, :T], bi == 0, bi == nb - 1, [('vbuf', i), ('pT', pi)], [P(obk)])
                act(osb[:65, :T], ps[obk][:65, :T], AF.Copy, [P(obk)], ['osb'])
                V('dve', 'reciprocal', osb[64:65, :T], osb[64:65, :T], reads=['osb'], writes=['osb'])
                mm(ps[7][:64, :T], ones_f[64:65, :64], osb[64:65, :T], True, True, ['osb', 'ones_f'], [P(7)])
                act(bcs[:64, :T], ps[7][:64, :T], AF.Copy, [P(7)], ['bcs'])
                V('dve', 'tensor_tensor', oa[:64, h, :T], osb[:64, :T], bcs[:64, :T], ALU.mult, reads=['osb', 'bcs'], writes=['oa'])

        def residual_add(st, b, dc):
            T = st.T
            V('dve', 'tensor_tensor', xT[:, dc, :T], xT[:, dc, :T], ps[b][:, :T], ALU.add, reads=['xT', P(b)], writes=['xT'])

        def even_layer(st, l, t0, O_k, O_v, O_lf):
            T, TS, NS = st.T, st.TS, st.NS
            i2 = l // 2
            wie3 = WB['wie'][i2].rearrange("(kc p) n -> p kc n", p=128)
            rmsnorm_x(st, gmix, l)
            load_gqk(i2)
            kk = ('KV', st.sid, i2)
            kt0 = t0 // 128
            o0 = st.o0
            Wq, kq = wreq(wie3[:, :, 0:512], 128, 8, 512)
            Wk, kk_ = wreq(wie3[:, :, 512:1024], 128, 8, 512)
            Wv, kv_ = wreq(wie3[:, :, 1024:1536], 128, 8, 512)
            for s in range(NS):
                ts = slice(s * TS, (s + 1) * TS)
                b = bank('gen')
                for kc in range(8):
                    mm(ps[b][:TS, :], xn[:, kc, ts], Wq[:, kc, :], kc == 0, kc == 7, ['xn', kq], [P(b)])
                head_norm(st, b, gq_bc, qf, 'qf', qb, 'qb')
                transp_heads(st, qb, qT, s, 'qb', 'qT')
                if DBG < 0.5:
                    continue
                b = bank('gen')
                for kc in range(8):
                    mm(ps[b][:TS, :], xn[:, kc, ts], Wk[:, kc, :], kc == 0, kc == 7, ['xn', kk_], [P(b)])
                head_norm(st, b, gk_bc, kf, 'kf')
                S.dma('act', O_k[i2, o0 + s * TS:o0 + (s + 1) * TS, :], kf[:TS, :], reads=['kf'], writes=['O_k'])
                act(kb[:TS, :], kf[:TS, :], AF.Copy, ['kf'], ['kb'])
                transp_heads(st, kb, kT, 0, 'kb', 'kT')
                S.dma('act', st.KT[i2][:, 0:64, t0 + s * TS:t0 + (s + 1) * TS].rearrange("h d t -> d h t"), kT[:64, :, :TS], reads=['kT'], writes=[kk])
                if DBG < 0.7:
                    continue
                b = bank('gen')
                for kc in range(8):
                    mm(ps[b][:TS, :], xn[:, kc, ts], Wv[:, kc, :], kc == 0, kc == 7, ['xn', kv_], [P(b)])
                act(vf[:TS, :], ps[b][:TS, :], AF.Copy, [P(b)], ['vf'])
                V('dve', 'tensor_copy', vb[:TS, 0, :, 0:64], vf[:TS, :].rearrange("p (h d) -> p h d", h=8), reads=['vf'], writes=['vb'])
                S.dma('act', O_v[i2, o0 + s * TS:o0 + (s + 1) * TS, :], vf[:TS, :], reads=['vf'], writes=['O_v'])
                S.dma('act', st.Vs[i2][:, 0:TS, kt0 + s, :].rearrange("h p d -> p h d"), vb[:TS, 0, :, :], reads=['vb'], writes=[kk])
                if DBG < 0.9:
                    continue
                for kc in range(8):
                    mm(ps[7][:TS, 0:8], xn[:, kc, ts], wf_b[:, i2, kc, :], kc == 0, kc == 7, ['xn', 'wf_b'], [P(7)])
                V('dve', 'tensor_tensor', lz[:TS, :], ps[7][:TS, 0:8], bf_bc[:TS, i2, :], ALU.add, reads=[P(7), 'bf_bc'], writes=['lz'])
                act(lz[:TS, :], lz[:TS, :], AF.Exp, ['lz'], ['lz'], scale=-1.0)
                act(lz[:TS, :], lz[:TS, :], AF.Ln, ['lz'], ['lz'], bias=1.0)
                V('dve', 'tensor_scalar', logf[:TS, s, :], lz[:TS, :], -1.0, None, reads=['lz'], writes=['logf'], op0=ALU.mult)
                S.dma('act', O_lf[i2, o0 + s * TS:o0 + (s + 1) * TS, :], logf[:TS, s, :], reads=['logf'], writes=['O_lf'])
            if DBG < 2:
                return
            kv_finish(st, i2, t0)
            if DBG < 3:
                return
            attention(st, i2, t0)
            if DBG < 4:
                return
            Wbg, kbg = wreq(wie3[:, :, OFF_BG:OFF_BG + 512], 128, 8, 512)
            Wcg, kcg = wreq(wie3[:, :, OFF_BG + 512:OFF_BG + 1024], 128, 8, 512)
            Wxi, kxi = wreq(wie3[:, :, OFF_BG + 1024:OFF_BG + 1536], 128, 8, 512)
            V('dve', 'tensor_copy', ub[:, :, 0:2], st.histB[:, :, i2 * 2:i2 * 2 + 2], reads=[st.kB], writes=['ub'])
            for c in range(4):
                cs = slice(c * 128, (c + 1) * 128)
                b = bank('gen')
                for kc in range(8):
                    mm(ps[b][:, :T], Wcg[:, kc, cs], xn[:, kc, :T], kc == 0, kc == 7, ['xn', kcg], [P(b)])
                act(tmpf[:, :T], ps[b][:, :T], AF.Copy, [P(b)], ['tmpf'])
                b = bank('gen')
                for kc in range(8):
                    mm(ps[b][:, :T], Wxi[:, kc, cs], xn[:, kc, :T], kc == 0, kc == 7, ['xn', kxi], [P(b)])
                V('dve', 'tensor_tensor', ub[:, c, 2:2 + T], tmpf[:, :T], ps[b][:, :T], ALU.mult, reads=['tmpf', P(b)], writes=['ub'])
                V('dve', 'tensor_scalar', cx[:, c, :T], ub[:, c, 2:2 + T], wB[:, c, i2 * 3 + 2:i2 * 3 + 3], None, reads=['ub'], writes=[('cx', c)], op0=ALU.mult)
                for j in (1, 0):
                    fma('dve', cx[:, c, :T], ub[:, c, j:j + T], wB[:, c, i2 * 3 + j:i2 * 3 + j + 1], ['ub'], ('cx', c))
                b = bank('gen')
                for kc in range(8):
                    mm(ps[b][:, :T], Wbg[:, kc, cs], xn[:, kc, :T], kc == 0, kc == 7, ['xn', kbg], [P(b)])
                V('dve', 'tensor_tensor', ob[:, c, :T], ps[b][:, :T], cx[:, c, :T], ALU.mult, reads=[P(b), ('cx', c)], writes=['ob'])
            V('dve', 'tensor_copy', st.histB[:, :, i2 * 2:i2 * 2 + 2], ub[:, :, T:T + 2], reads=['ub'], writes=[st.kB])
            if DBG < 5:
                return
            woA = WB['woe'][i2][0:512, :].rearrange("(h d) n -> d h n", d=64)
            woB = WB['woe'][i2][512:1024, :].rearrange("(c p) n -> p c n", p=128)
            WA0, kA0 = wreq(woA[:, :, 0:512], 64, 8, 512)
            WA1, kA1 = wreq(woA[:, :, 512:1024], 64, 8, 512)
            WBo, kBo = wreq(woB, 128, 4, 1024)
            for dc in range(8):
                b = bank('gen')
                WA, kA = (WA0, kA0) if dc < 4 else (WA1, kA1)
                dsl = slice((dc % 4) * 128, (dc % 4 + 1) * 128)
                for h in range(8):
                    mm(ps[b][:, :T], WA[:64, h, dsl], oa[:64, h, :T], h == 0, False, ['oa', kA], [P(b)])
                for c in range(4):
                    mm(ps[b][:, :T], WBo[:, c, dc * 128:(dc + 1) * 128], ob[:, c, :T], False, c == 3, ['ob', kBo], [P(b)])
                residual_add(st, b, dc)

        def gelu_from_psum(b, npart, T, outf, key):
            act(tmpf[:npart, :T], ps[b][:npart, :T], AF.Square, [P(b)], ['tmpf'])
            V('dve', 'tensor_scalar', tmpf[:npart, :T], tmpf[:npart, :T], 0.044715, 1.0, reads=['tmpf'], writes=['tmpf'], op0=ALU.mult, op1=ALU.add)
            V('dve', 'tensor_tensor', tmpf[:npart, :T], tmpf[:npart, :T], ps[b][:npart, :T], ALU.mult, reads=['tmpf', P(b)], writes=['tmpf'])
            act(tmpf[:npart, :T], tmpf[:npart, :T], AF.Sigmoid, ['tmpf'], ['tmpf'], scale=1.5957691216057308)
            V('dve', 'tensor_tensor', outf, tmpf[:npart, :T], ps[b][:npart, :T], ALU.mult, reads=['tmpf', P(b)], writes=[key])

        def odd_layer(st, l, t0, O_vc):
            T, TS, NS = st.T, st.TS, st.NS
            i2 = l // 2
            wio3 = WB['wio'][i2].rearrange("(kc p) n -> p kc n", p=128)
            rmsnorm_x(st, gmix, l)
            S.dma('sp', gvc_bc[:, :], I['g_vc'][i2].partition_broadcast(128), reads=[], writes=['gvc_bc'])
            o0 = st.o0
            Wu, ku = wreq(wio3[:, :, 0:512], 128, 8, 512)
            Wvc, kvc = wreq(wio3[:, :, 512:1024], 128, 8, 512)
            for c in range(4):
                b = bank('gen')
                for kc in range(8):
                    mm(ps[b][:, :T], Wu[:, kc, c * 128:(c + 1) * 128], xn[:, kc, :T], kc == 0, kc == 7, ['xn', ku], [P(b)])
                gelu_from_psum(b, 128, T, ug[:, c, :T], 'ug')
            for s in range(NS):
                ts = slice(s * TS, (s + 1) * TS)
                b = bank('gen')
                for kc in range(8):
                    mm(ps[b][:TS, :], xn[:, kc, ts], Wvc[:, kc, :], kc == 0, kc == 7, ['xn', kvc], [P(b)])
                gelu_from_psum(b, TS, 512, zf[:TS, :], 'qf')
                act(sqh[:TS, :], zf[:TS, :], AF.Square, ['qf'], ['tmpf'])
                V('dve', 'tensor_reduce', rs1[:TS, 0:1], sqh[:TS, :], AX.X, ALU.add, reads=['tmpf'], writes=['rs1'])
                V('dve', 'tensor_scalar', rs1[:TS, 0:1], rs1[:TS, 0:1], 1.0 / 512, EPS, reads=['rs1'], writes=['rs1'], op0=ALU.mult, op1=ALU.add)
                act(rs1[:TS, 0:1], rs1[:TS, 0:1], AF.Sqrt, ['rs1'], ['rs1'])
                V('dve', 'reciprocal', rs1[:TS, 0:1], rs1[:TS, 0:1], reads=['rs1'], writes=['rs1'])
                V('dve', 'scalar_tensor_tensor', zf[:TS, :], zf[:TS, :], rs1[:TS, 0:1], gvc_bc[:TS, :], reads=['qf', 'rs1', 'gvc_bc'], writes=['qf'],
                  op0=ALU.mult, op1=ALU.mult)
                if O_vc is not None:
                    S.dma('act', O_vc[i2, o0 + s * TS:o0 + (s + 1) * TS, :], zf[:TS, :], reads=['qf'], writes=['O_vc'])
                act(vcb[:TS, :], zf[:TS, :], AF.Copy, ['qf'], ['kb'])
                b = bank('gen')
                for g in range(4):
                    mm(ps[b][:, g * TS:(g + 1) * TS], vcb[:TS, g * 128:(g + 1) * 128], wsT[:TS, i2, g, :TS], True, True, ['kb', 'wsT'], [P(b)])
                V('dve', 'tensor_tensor', tmpg[:, :4 * TS].rearrange("p (g t) -> p g t", g=4), ps[b][:, :4 * TS].rearrange("p (g t) -> p g t", g=4),
                  bs_bc[:, i2, :, :TS], ALU.add, reads=[P(b), 'bs_bc'], writes=['tmpg'])
                V('dve', 'tensor_tensor', oc[:, :, ts], tmpg[:, :4 * TS].rearrange("p (g t) -> p g t", g=4), ug[:, :, ts], ALU.mult,
                  reads=['tmpg', 'ug'], writes=['oc'])
            Wad, kad = wreq(wio3[:, :, 1024:1536], 128, 8, 512)
            Wgd, kgd = wreq(wio3[:, :, 1536:2048], 128, 8, 512)
            V('dve', 'tensor_copy', ub[:, :, 0:30], st.histD[:, :, i2 * 30:i2 * 30 + 30], reads=[st.kD], writes=['ub'])
            for c in range(4):
                cs = slice(c * 128, (c + 1) * 128)
                b = bank('gen')
                for kc in range(8):
                    mm(ps[b][:, :T], Wad[:, kc, cs], xn[:, kc, :T], kc == 0, kc == 7, ['xn', kad], [P(b)])
                act(tmpf[:, :T], ps[b][:, :T], AF.Copy, [P(b)], ['tmpf'])
                b = bank('gen')
                for kc in range(8):
                    mm(ps[b][:, :T], Wgd[:, kc, cs], xn[:, kc, :T], kc == 0, kc == 7, ['xn', kgd], [P(b)])
                act(tmpg[:, :T], ps[b][:, :T], AF.Sigmoid, [P(b)], ['tmpg'])
                V('dve', 'tensor_tensor', ub[:, c, 30:30 + T], tmpf[:, :T], tmpg[:, :T], ALU.mult, reads=['tmpf', 'tmpg'], writes=['ub'])
                e = 'dve'
                V(e, 'tensor_scalar', cx[:, c, :T], ub[:, c, 30:30 + T], wD[:, c, i2 * 31 + 30:i2 * 31 + 31], None, reads=['ub'], writes=[('cx', c)], op0=ALU.mult)
                for j in range(30):
                    fma(e, cx[:, c, :T], ub[:, c, j:j + T], wD[:, c, i2 * 31 + j:i2 * 31 + j + 1], ['ub'], ('cx', c))
            V('dve', 'tensor_copy', st.histD[:, :, i2 * 30:i2 * 30 + 30], ub[:, :, T:T + 30], reads=['ub'], writes=[st.kD])
            for c in range(4):
                b2 = c % 2
                act(sq[:, b2, :T], cx[:, c, :T], AF.Square, [('cx', c)], [('sq', b2)])
                mm(ps[7][:, :T], ones_b[:], sq[:, b2, :T], c == 0, c == 3, [('sq', b2), 'ones_b'], [P(7)], sig=True)
            V('dve', 'tensor_scalar', rstd[:, :T], ps[7][:, :T], 1.0 / 512, EPS, reads=[P(7)], writes=['rstd'], op0=ALU.mult, op1=ALU.add)
            act(rstd[:, :T], rstd[:, :T], AF.Sqrt, ['rstd'], ['rstd'])
            V('dve', 'reciprocal', rstd[:, :T], rstd[:, :T], reads=['rstd'], writes=['rstd'])
            for c in range(4):
                V('dve', 'scalar_tensor_tensor', tmpf[:, :T], cx[:, c, :T], gd[:, c, i2:i2 + 1], rstd[:, :T], reads=[('cx', c), 'rstd'], writes=['tmpf'],
                  op0=ALU.mult, op1=ALU.mult)
                act(ob[:, c, :T], tmpf[:, :T], AF.Silu, ['tmpf'], ['ob'])
            woo3 = WB['woo'][i2].rearrange("(c p) n -> p c n", p=128)
            W0, k0 = wreq(woo3[:, 0:4, :], 128, 4, 1024)
            W1, k1 = wreq(woo3[:, 4:8, :], 128, 4, 1024)
            for dc in range(8):
                b = bank('gen')
                for c in range(4):
                    mm(ps[b][:, :T], W0[:, c, dc * 128:(dc + 1) * 128], oc[:, c, :T], c == 0, False, ['oc', k0], [P(b)])
                for c in range(4):
                    mm(ps[b][:, :T], W1[:, c, dc * 128:(dc + 1) * 128], ob[:, c, :T], False, c == 3, ['ob', k1], [P(b)])
                residual_add(st, b, dc)

        def conv3_ffn(st, l, j, b, hbi, e, outap, outkey):
            T = st.T
            h = hb[hbi]
            hk = ('hb', hbi)
            hh = ('hbh', hbi)
            V('dve', 'tensor_copy', h[:, 0:2], st.histF[:, j, l * 2:l * 2 + 2], reads=[st.kF], writes=[hh])
            act(h[:, 2:2 + T], ps[b][:, :T], AF.Copy, [P(b)], [hk])
            act(outap, ps[b][:, :T], AF.Copy, [P(b)], [outkey], scale=wF[:, j, l * 3 + 2:l * 3 + 3])
            for jj in (1, 0):
                fma('dve', outap, h[:, jj:jj + T], wF[:, j, l * 3 + jj:l * 3 + jj + 1], [hk, hh], outkey)
            act(st.histF[:, j, l * 2:l * 2 + 2], h[:, T:T + 2], AF.Copy, [hk], [st.kF])

        def ffn(st, l):
            T = st.T
            rmsnorm_x(st, gffn, l)
            wup3 = WB['wup'][l].rearrange("(kc p) n -> p kc n", p=128)
            wdn3 = WB['wdn'][l].rearrange("(fc p) n -> p fc n", p=128)
            for half in range(2):
                f0 = half * 11
                groups = [(f0, 4), (f0 + 4, 4), (f0 + 8, 3)]
                for (g0, nch) in groups:
                    Wg, kg = wreq(wup3[:, :, g0 * 128:(g0 + nch) * 128], 128, 8, nch * 128)
                    Wv2, kv2 = wreq(wup3[:, :, DFF + g0 * 128:DFF + (g0 + nch) * 128], 128, 8, nch * 128)
                    for jj in range(nch):
                        j = g0 + jj
                        cs = slice(jj * 128, (jj + 1) * 128)
                        bg_ = bank('gen')
                        for kc in range(8):
                            mm(ps[bg_][:, :T], Wg[:, kc, cs], xn[:, kc, :T], kc == 0, kc == 7, ['xn', kg], [P(bg_)])
                        bv_ = bank('gen')
                        for kc in range(8):
                            mm(ps[bv_][:, :T], Wv2[:, kc, cs], xn[:, kc, :T], kc == 0, kc == 7, ['xn', kv2], [P(bv_)])
                        conv3_ffn(st, l, j, bg_, 0, 'dve', ca[0][:, :T], ('ca', 0))
                        conv3_ffn(st, l, 22 + j, bv_, 1, 'pool', ca[1][:, :T], ('ca', 1))
                        act(sgt[:, :T], ca[0][:, :T], AF.Silu, [('ca', 0)], ['sgt'])
                        V('dve', 'tensor_tensor', gbuf[:, j - f0, :T], sgt[:, :T], ca[1][:, :T], ALU.mult, reads=['sgt', ('ca', 1)], writes=[('gbuf', j - f0)])
                Wd = [wreq(wdn3[:, g0:g0 + n, :], 128, n, 1024) for (g0, n) in groups]
                for dc in range(8):
                    b = bank('gen')
                    idx = 0
                    for gi, (g0, n) in enumerate(groups):
                        for ff in range(n):
                            fc = g0 + ff - f0
                            mm(ps[b][:, :T], Wd[gi][0][:, ff, dc * 128:(dc + 1) * 128], gbuf[:, fc, :T], idx == 0, idx == 10,
                               [('gbuf', fc), Wd[gi][1]], [P(b)])
                            idx += 1
                    residual_add(st, b, dc)

        def load_x(st, src, t0):
            T, TS, NS = st.T, st.TS, st.NS
            for s in range(NS):
                S.dma('sp', xstage[:TS, :], src[t0 + s * TS:t0 + (s + 1) * TS, :], reads=[], writes=['xstage'])
                for k0 in (0, 4):
                    b = bank('gen')
                    for kk2 in range(4):
                        kc = k0 + kk2
                        mm(ps[b][:, kk2 * TS:(kk2 + 1) * TS], xstage[:TS, kc * 128:(kc + 1) * 128], ident_f[:TS, :TS], True, True,
                           ['xstage', 'ident_f'], [P(b)])
                    act(xT[:, k0:k0 + 4, s * TS:(s + 1) * TS], ps[b][:, :4 * TS].rearrange("p (k t) -> p k t", k=4), AF.Copy, [P(b)], ['xT'])

        def store_x(st, dst, t0):
            T, TS, NS = st.T, st.TS, st.NS
            for s in range(NS):
                for k0 in (0, 4):
                    b = bank('gen')
                    for kk2 in range(4):
                        kc = k0 + kk2
                        mm(ps[b][:TS, kk2 * 128:(kk2 + 1) * 128], xT[:, kc, s * TS:(s + 1) * TS], ident_f[:, :], True, True, ['xT', 'ident_f'], [P(b)])
                    act(xstage[:TS, k0 * 128:(k0 + 4) * 128], ps[b][:TS, :], AF.Copy, [P(b)], ['xstage'])
                S.dma('act', dst[t0 + s * TS:t0 + (s + 1) * TS, :], xstage[:TS, :], reads=['xstage'], writes=['O_y'])

        def store_T(src3, key, ncn, R, r0, dst):
            for c0 in range(0, ncn, 4):
                n = min(4, ncn - c0)
                b = bank('gen')
                for c in range(n):
                    mm(ps[b][:R, c * 128:(c + 1) * 128], src3[:, c0 + c, r0:r0 + R], ident_f[:, :], True, True, [key, 'ident_f'], [P(b)])
                act(ostage[:R, :n * 128], ps[b][:R, :n * 128], AF.Copy, [P(b)], ['tmpg'])
                S.dma('act', dst[:, c0 * 128:(c0 + n) * 128], ostage[:R, :n * 128], reads=['tmpg'], writes=['O_st'])

        def run_layers(st, t0, O_k, O_v, O_lf, O_vc):
            for l in range(NL):
                if l % 2 == 0:
                    even_layer(st, l, t0, O_k, O_v, O_lf)
                else:
                    odd_layer(st, l, t0, O_vc)
                if DBG >= 6:
                    ffn(st, l)

        def store_states(st, O_cb, O_cd, O_cf):
            for i2 in range(2):
                store_T(st.histB, st.kB, 4, 2, i2 * 2, O_cb[i2 * 2:i2 * 2 + 2, :])
                store_T(st.histD, st.kD, 4, 30, i2 * 30, O_cd[i2 * 30:i2 * 30 + 30, :])
            for l in range(4):
                store_T(st.histF, st.kF, 44, 2, l * 2, O_cf[l * 2:l * 2 + 2, :])

        if do_sample:
            ss = make_stream('s', DEC, PAST + 512)
            load_T(I['state_conv_ffn'], 8, 5632, ss.histF, ss.kF)
            load_T(I['state_conv_b'], 4, 512, ss.histB, ss.kB)
            load_T(I['state_conv_d'], 60, 512, ss.histD, ss.kD)
            V('pool', 'memset', ss.carry[:], 0.0, reads=[], writes=[ss.kC])
            pre = Stream()
            pre.sid, pre.T, pre.TS, pre.NS = 's', 512, 128, 4
            pre.KT, pre.Vs, pre.carry = ss.KT, ss.Vs, ss.carry
            pre.kC = ss.kC
            for i2 in range(2):
                if 2 * i2 >= NL:
                    continue
                for t0 in range(0, PAST, 512):
                    for s in range(4):
                        r0 = t0 + s * 128
                        S.dma('sp', kf[:, :], I['cache_k'][i2, r0:r0 + 128, :], reads=[], writes=['kf'])
                        act(kb[:, :], kf[:, :], AF.Copy, ['kf'], ['kb'])
                        transp_heads(pre, kb, kT, 0, 'kb', 'kT')
                        S.dma('act', ss.KT[i2][:, 0:64, r0:r0 + 128].rearrange("h d t -> d h t"), kT[:64, :, :128], reads=['kT'], writes=[('KV', 's', i2)])
                        S.dma('sp', vf[:, :], I['cache_v'][i2, r0:r0 + 128, :], reads=[], writes=['vf'])
                        V('dve', 'tensor_copy', vb[:, 0, :, 0:64], vf[:, :].rearrange("p (h d) -> p h d", h=8), reads=['vf'], writes=['vb'])
                        S.dma('act', ss.Vs[i2][:, 0:128, r0 // 128, :].rearrange("h p d -> p h d"), vb[:, 0, :, :], reads=['vb'], writes=[('KV', 's', i2)])
                        S.dma('sp', logf[:, s, :], I['cache_logf'][i2, r0:r0 + 128, :], reads=[], writes=['logf'])
                    kv_finish(pre, i2, t0)
            load_x(ss, I['x_sample'], 0)
            ss.o0 = 0
            run_layers(ss, PAST, O['s_k'], O['s_v'], O['s_lf'], O['s_vc'])
            store_x(ss, O['y_s'], 0)
            store_states(ss, O['s_cb'], O['s_cd'], O['s_cf'])

        if NT > 0:
            sp_ = make_stream('p', 512, NTOK)
            for tname, tk in ((sp_.histF, sp_.kF), (sp_.histB, sp_.kB), (sp_.histD, sp_.kD), (sp_.carry, sp_.kC)):
                V('pool', 'memset', tname[:], 0.0, reads=[], writes=[tk])
            for ti in range(NT):
                t0 = ti * 512
                sp_.o0 = t0
                load_x(sp_, I['x_prompt'], t0)
                run_layers_prompt = run_layers
                run_layers_prompt(sp_, t0, O['p_k'], O['p_v'], O['p_lf'], None)
                store_x(sp_, O['y_p'], t0)
            store_states(sp_, O['p_cb'], O['p_cd'], O['p_cf'])

        S.finish('sp')
        print("SBUF remaining at end:", nc.sbuf_bytes_remaining)
        S.simulate()
        print("instructions:", S.nins, "counts:", {k: v for k, v in S.cnt.items() if v})
    return nc


_NC_CACHE = {}


def _prep_inputs(inputs, c, NT):
    f = lambda a: np.ascontiguousarray(a, dtype=np.float32)
    b = c % 2
    m = {
        'x_prompt': f(inputs['x_prompt'][b, :NT * 512]),
        'x_sample': f(inputs['x_sample'][c]),
        'cache_k': f(inputs['cache_k'][:, c].reshape(2, PAST, 512)),
        'cache_v': f(inputs['cache_v'][:, c].reshape(2, PAST, 512)),
        'cache_logf': f(inputs['cache_logf'][:, c]),
        'state_conv_b': f(inputs['state_conv_b'][:, c].reshape(4, 512)),
        'state_conv_d': f(inputs['state_conv_d'][:, c].reshape(60, 512)),
        'state_conv_ffn': f(inputs['state_conv_ffn'][:, c].reshape(8, 5632)),
        'conv_b': f(inputs['conv_b'].reshape(6, 512)),
        'conv_d': f(inputs['conv_d'].reshape(62, 512)),
        'conv_ffn': f(inputs['conv_ffn'].reshape(12, 5632)),
    }
    for k in ['g_mix', 'w_in_even', 'b_f', 'g_q', 'g_k', 'w_out_even', 'w_in_odd', 'g_vc', 'w_s', 'b_s', 'g_d',
              'w_out_odd', 'g_ffn', 'w_up', 'w_down']:
        m[k] = f(inputs[k])
    return m


def run(inputs, NT=32, NL=4):
    key = (NT, NL)
    if key not in _NC_CACHE:
        _NC_CACHE[key] = build(NT, NL)
    nc = _NC_CACHE[key]
    in_maps = [_prep_inputs(inputs, c, NT) for c in range(8)]
    res = run_bass_kernel_spmd(nc, in_maps, core_ids=list(range(8)))
    R = res.results
    B = 2
    T = NT * 512
    y_p = np.stack([R[b]['y_p'] for b in range(B)])
    y_s = np.stack([R[c]['y_s'] for c in range(8)])
    pk = np.stack([R[b]['p_k'] for b in range(B)], axis=1).reshape(2, B, T, 8, 64)
    pv = np.stack([R[b]['p_v'] for b in range(B)], axis=1).reshape(2, B, T, 8, 64)
    plf = np.stack([R[b]['p_lf'] for b in range(B)], axis=1)
    pcb = np.stack([R[b]['p_cb'].reshape(2, 2, 512) for b in range(B)], axis=1)
    pcd = np.stack([R[b]['p_cd'].reshape(2, 30, 512) for b in range(B)], axis=1)
    pcf = np.stack([R[b]['p_cf'].reshape(4, 2, 5632) for b in range(B)], axis=1)
    sk = np.stack([R[c]['s_k'] for c in range(8)], axis=1).reshape(2, 8, DEC, 8, 64)
    sv = np.stack([R[c]['s_v'] for c in range(8)], axis=1).reshape(2, 8, DEC, 8, 64)
    slf = np.stack([R[c]['s_lf'] for c in range(8)], axis=1)
    scb = np.stack([R[c]['s_cb'].reshape(2, 2, 512) for c in range(8)], axis=1)
    svc = np.stack([R[c]['s_vc'] for c in range(8)], axis=1)
    scd = np.stack([R[c]['s_cd'].reshape(2, 30, 512) for c in range(8)], axis=1)
    scf = np.stack([R[c]['s_cf'].reshape(4, 2, 5632) for c in range(8)], axis=1)
    return tuple(np.ascontiguousarray(a, dtype=np.float32) for a in
                 (y_p, y_s, pk, pv, plf, pcb, pcd, pcf, sk, sv, slf, scb, svc, scd, scf))


def kernel(**inputs):
    return run(inputs, NT=32, NL=4)
```

```python
import os
import numpy as np
from contextlib import ExitStack
DBG = float(os.environ.get("MK_DBG", "9"))
import concourse.bass as bass
import concourse.mybir as mybir
from concourse.bass_utils import run_bass_kernel_spmd

F32 = mybir.dt.float32
BF16 = mybir.dt.bfloat16
ALU = mybir.AluOpType
AF = mybir.ActivationFunctionType
AX = mybir.AxisListType

D = 1024
DFF = 2816
PAST = 4096
DEC = 64
EPS = 1e-6
OFF_BG = 1544
NEG = -30000.0


class Sched:
    NDS = 4

    def __init__(self, nc):
        self.nc = nc
        self.eng = {'pe': nc.tensor, 'act': nc.scalar, 'dve': nc.vector, 'pool': nc.gpsimd, 'sp': nc.sync}
        self.sem = {}
        self.cnt = {}
        for e in ['pe', 'act', 'dve', 'pool']:
            self.sem[e] = nc.alloc_semaphore(name=f"s_{e}")
            self.cnt[e] = 0
        self.dq = ['sp', 'pool', 'act']
        for q in self.dq:
            for j in range(self.NDS):
                self.sem[(q, j)] = nc.alloc_semaphore(name=f"d_{q}{j}")
                self.cnt[(q, j)] = 0
        self.dcount = {q: 0 for q in self.dq}
        self.seen = {e: {} for e in self.eng}
        self.lastw = {}
        self.readers = {}
        self.nins = 0
        self.ev = {e: [] for e in self.eng}

    def simulate(self):
        val = {k: 0 for k in self.sem}
        pc = {e: 0 for e in self.ev}
        prog = True
        while prog:
            prog = False
            for e, lst in self.ev.items():
                while pc[e] < len(lst):
                    kind, s, v, info = lst[pc[e]]
                    if kind == 'wait':
                        if val[s] >= v:
                            pc[e] += 1
                            prog = True
                        else:
                            break
                    else:
                        val[s] += v
                        pc[e] += 1
                        prog = True
        stuck = {e: (pc[e], len(l), l[pc[e]] if pc[e] < len(l) else None) for e, l in self.ev.items()}
        ok = all(pc[e] == len(l) for e, l in self.ev.items())
        print("SIM", "OK" if ok else "DEADLOCK", stuck if not ok else "")
        if not ok:
            print({k: v for k, v in val.items()})
        return ok

    def _deps(self, reads, writes):
        deps = set()
        for k in reads:
            if k in self.lastw:
                deps.add(self.lastw[k])
        for k in writes:
            if k in self.lastw:
                deps.add(self.lastw[k])
            for r in self.readers.get(k, ()):
                deps.add(r)
        return deps

    def _wait(self, e, deps):
        need = {}
        for (s, c) in deps:
            if c > need.get(s, 0):
                need[s] = c
        for s, c in need.items():
            if self.seen[e].get(s, 0) >= c:
                continue
            unit = 16 if isinstance(s, tuple) else 1
            self.eng[e].wait_ge(self.sem[s], c * unit)
            self.ev[e].append(('wait', s, c * unit, None))
            self.seen[e][s] = c

    def _record(self, tok, reads, writes):
        for k in reads:
            lst = self.readers.setdefault(k, [])
            lst.append(tok)
            if len(lst) > 64:
                best = {}
                for (s, c) in lst:
                    if c > best.get(s, 0):
                        best[s] = c
                self.readers[k] = [(s, c) for s, c in best.items()]
        for k in writes:
            self.lastw[k] = tok
            self.readers[k] = []

    def op(self, e, fn, reads=(), writes=(), signal=True):
        deps = self._deps(reads, writes)
        if e == 'pe':
            deps = {d for d in deps if d[0] != 'pe'}
        self._wait(e, deps)
        ins = fn()
        tok = (e, self.cnt[e] + 1)
        if signal:
            ins.then_inc(self.sem[e], 1)
            self.cnt[e] += 1
            self.ev[e].append(('inc', e, 1, self.nins))
        self._record(tok, reads, writes)
        self.nins += 1
        return ins

    def dma(self, q, out, in_, reads=(), writes=(), **kw):
        if q == 'act' and os.environ.get("MK_ACTQ", "sp") != "act":
            q = os.environ.get("MK_ACTQ", "sp")
        deps = self._deps(reads, writes)
        j = self.dcount[q] % self.NDS
        self.dcount[q] += 1
        s = (q, j)
        if self.cnt[s] > 0:
            deps = set(deps)
            deps.add((s, self.cnt[s]))
        self._wait(q, deps)
        ins = self.eng[q].dma_start(out=out, in_=in_, **kw)
        ins.then_inc(self.sem[s], 16)
        self.cnt[s] += 1
        self.ev[q].append(('inc', s, 16, self.nins))
        tok = (s, self.cnt[s])
        self._record(tok, reads, writes)
        self.nins += 1
        return ins

    def barrier(self):
        deps = set()
        for s in self.cnt:
            if self.cnt[s] > 0:
                deps.add((s, self.cnt[s]))
        for e in ['pe', 'act', 'dve', 'pool', 'sp']:
            self._wait(e, deps)

    def finish(self, e='sp'):
        deps = set()
        for k, t in self.lastw.items():
            deps.add(t)
        for s in self.cnt:
            if self.cnt[s] > 0:
                deps.add((s, self.cnt[s]))
        self._wait(e, deps)


class Stream:
    pass


def build(NT=32, NL=4, do_sample=True):
    nc = bass.Bass("TRN2", target_bir_lowering=False)
    NTOK = NT * 512

    def din(name, shape):
        return nc.dram_tensor(name, list(shape), F32, kind="ExternalInput").ap()

    def dout(name, shape):
        return nc.dram_tensor(name, list(shape), F32, kind="ExternalOutput").ap()

    def dscr(name, shape, dt=BF16):
        return nc.dram_tensor(name, list(shape), dt, kind="Internal").ap()

    I = dict(
        x_prompt=din("x_prompt", [NTOK, D]), x_sample=din("x_sample", [DEC, D]),
        cache_k=din("cache_k", [2, PAST, 512]), cache_v=din("cache_v", [2, PAST, 512]),
        cache_logf=din("cache_logf", [2, PAST, 8]),
        state_conv_b=din("state_conv_b", [4, 512]), state_conv_d=din("state_conv_d", [60, 512]),
        state_conv_ffn=din("state_conv_ffn", [8, 5632]),
        g_mix=din("g_mix", [4, D]), w_in_even=din("w_in_even", [2, D, 3080]), b_f=din("b_f", [2, 8]),
        g_q=din("g_q", [2, 64]), g_k=din("g_k", [2, 64]), conv_b=din("conv_b", [6, 512]),
        w_out_even=din("w_out_even", [2, D, D]), w_in_odd=din("w_in_odd", [2, D, 2048]),
        g_vc=din("g_vc", [2, 512]), w_s=din("w_s", [2, 4, 128, 128]), b_s=din("b_s", [2, 4, 128]),
        conv_d=din("conv_d", [62, 512]), g_d=din("g_d", [2, 512]), w_out_odd=din("w_out_odd", [2, D, D]),
        g_ffn=din("g_ffn", [4, D]), w_up=din("w_up", [4, D, 5632]), conv_ffn=din("conv_ffn", [12, 5632]),
        w_down=din("w_down", [4, DFF, D]),
    )
    O = dict(
        y_p=dout("y_p", [NTOK, D]), y_s=dout("y_s", [DEC, D]),
        p_k=dout("p_k", [2, NTOK, 512]), p_v=dout("p_v", [2, NTOK, 512]), p_lf=dout("p_lf", [2, NTOK, 8]),
        p_cb=dout("p_cb", [4, 512]), p_cd=dout("p_cd", [60, 512]), p_cf=dout("p_cf", [8, 5632]),
        s_k=dout("s_k", [2, DEC, 512]), s_v=dout("s_v", [2, DEC, 512]), s_lf=dout("s_lf", [2, DEC, 8]),
        s_cb=dout("s_cb", [4, 512]), s_vc=dout("s_vc", [2, DEC, 512]), s_cd=dout("s_cd", [60, 512]),
        s_cf=dout("s_cf", [8, 5632]),
    )
    WB = dict(
        wie=dscr("wie_b", [2, D, 3080]), woe=dscr("woe_b", [2, D, D]), wio=dscr("wio_b", [2, D, 2048]),
        woo=dscr("woo_b", [2, D, D]), wup=dscr("wup_b", [4, D, 5632]), wdn=dscr("wdn_b", [4, DFF, D]),
    )
    cq_scr = dscr("cq_scr", [8, 3, 512])

    S = Sched(nc)
    with ExitStack() as es:
        def SB(name, shape, dt=F32):
            return es.enter_context(nc.sbuf_tensor(name, list(shape), dt))

        ps = [es.enter_context(nc.psum_tensor(f"ps{i}", [128, 512], F32)) for i in range(8)]
        rot = {'gen': [0, [0, 1, 2]], 'S': [0, [3, 4]], 'O': [0, [5, 6]]}

        def bank(kind):
            r = rot[kind]
            b = r[1][r[0] % len(r[1])]
            r[0] += 1
            return b

        def P(b):
            return ('ps', b)

        def mm(out, lhsT, rhs, start, stop, reads, writes, sig=None):
            S.op('pe', lambda: nc.tensor.matmul(out, lhsT=lhsT, rhs=rhs, start=start, stop=stop),
                 reads=reads, writes=writes, signal=(stop if sig is None else sig))

        def act(out, in_, func, reads, writes, **kw):
            S.op('act', lambda: nc.scalar.activation(out, in_, func, **kw), reads=reads, writes=writes)

        def V(e, name, *args, reads, writes, **kw):
            eng = nc.vector if e == 'dve' else nc.gpsimd
            S.op(e, lambda: getattr(eng, name)(*args, **kw), reads=reads, writes=writes)

        def fma(e, acc, src, wcol, rkeys, akey):
            V('dve', 'scalar_tensor_tensor', acc, src, wcol, acc, reads=rkeys + [akey], writes=[akey], op0=ALU.mult, op1=ALU.add)

        ones_f = SB("ones_f", [128, 128])
        ident_f = SB("ident_f", [128, 128])
        utri_f = SB("utri_f", [128, 128])
        ident_b = SB("ident_b", [128, 128], BF16)
        ones_b = SB("ones_b", [128, 128], BF16)
        zeros_b = SB("zeros_b", [128, 512], BF16)
        dmask = SB("dmask", [128, 4, 512], BF16)
        ones3 = SB("ones3", [8, 3, 512], BF16)
        V('pool', 'memset', ones_f[:], 1.0, reads=[], writes=['ones_f'])
        V('pool', 'memset', ones_b[:], 1.0, reads=[], writes=['ones_b'])
        V('pool', 'memset', zeros_b[:], 0.0, reads=[], writes=['zeros_b'])
        V('pool', 'memset', ones3[:], 1.0, reads=[], writes=['ones3'])
        S.op('pool', lambda: nc.gpsimd.affine_select(ident_f[:], ones_f[:], [[-1, 128]], ALU.is_equal, 0.0, base=0, channel_multiplier=1),
             reads=['ones_f'], writes=['ident_f'])
        S.op('pool', lambda: nc.gpsimd.affine_select(utri_f[:], ones_f[:], [[1, 128]], ALU.is_ge, 0.0, base=0, channel_multiplier=-1),
             reads=['ones_f'], writes=['utri_f'])
        V('pool', 'tensor_copy', ident_b[:], ident_f[:], reads=['ident_f'], writes=['ident_b'])
        for r in range(4):
            S.op('pool', lambda: nc.gpsimd.affine_select(dmask[:, r, :], zeros_b[:], [[1, 512]], ALU.is_ge, NEG, base=-128 * r, channel_multiplier=-1),
                 reads=['zeros_b'], writes=['dmask'])

        cx = SB("cx", [128, 4, 512])
        ob = SB("ob", [128, 4, 512], BF16)
        castf = cx[:, :, :].rearrange("p (a b) c -> p a (b c)", a=2)
        castb = ob[:, :, :].rearrange("p (a b) c -> p a (b c)", a=2)
        ci = [0]

        def cast_weight(src, dst, rows, cols):
            for r0 in range(0, rows, 128):
                for c0 in range(0, cols, 1024):
                    cw = min(1024, cols - c0)
                    b = ci[0] % 2
                    e = ['dve', 'pool'][ci[0] % 2]
                    ci[0] += 1
                    S.dma('sp', castf[:, b, :cw], src[r0:r0 + 128, c0:c0 + cw], reads=[], writes=[('castf', b)])
                    V(e, 'tensor_copy', castb[:, b, :cw], castf[:, b, :cw], reads=[('castf', b)], writes=[('castb', b)])
                    S.dma('act', dst[r0:r0 + 128, c0:c0 + cw], castb[:, b, :cw], reads=[('castb', b)], writes=['WB'])

        for i2 in range(2):
            cast_weight(I['w_in_even'][i2], WB['wie'][i2], D, 3080)
            cast_weight(I['w_out_even'][i2], WB['woe'][i2], D, D)
            cast_weight(I['w_in_odd'][i2], WB['wio'][i2], D, 2048)
            cast_weight(I['w_out_odd'][i2], WB['woo'][i2], D, D)
        for l in range(4):
            cast_weight(I['w_up'][l], WB['wup'][l], D, 5632)
            cast_weight(I['w_down'][l], WB['wdn'][l], DFF, D)

        S.barrier()
        xstage = SB("xstage", [128, 1024])
        stage = xstage

        def load_T(src, R, C, dst3, key):
            ncn = C // 128
            per = min(512 // R, 8)
            for c0 in range(0, ncn, per):
                n = min(per, ncn - c0)
                S.dma('sp', stage[:R, :n * 128], src[:, c0 * 128:(c0 + n) * 128], reads=[], writes=['xstage'])
                b = bank('gen')
                for c in range(n):
                    mm(ps[b][:, c * R:(c + 1) * R], stage[:R, c * 128:(c + 1) * 128], ident_f[:R, :R], True, True,
                       ['xstage', 'ident_f'], [P(b)])
                act(dst3[:, c0:c0 + n, :], ps[b][:, :n * R].rearrange("p (c r) -> p c r", r=R), AF.Copy, [P(b)], [key])

        gmix = SB("gmix", [128, 8, 4]); load_T(I['g_mix'], 4, D, gmix, 'gmix')
        gffn = SB("gffn", [128, 8, 4]); load_T(I['g_ffn'], 4, D, gffn, 'gffn')
        gd = SB("gd", [128, 4, 2]); load_T(I['g_d'], 2, 512, gd, 'gd')
        wB = SB("wB", [128, 4, 6]); load_T(I['conv_b'], 6, 512, wB, 'wB')
        wD = SB("wD", [128, 4, 62]); load_T(I['conv_d'], 62, 512, wD, 'wD')
        wF = SB("wF", [128, 44, 12]); load_T(I['conv_ffn'], 12, 5632, wF, 'wF')

        gq_t = SB("gq_t", [128, 2, 64]); gk_t = SB("gk_t", [128, 2, 64])
        gq_bc = SB("gq_bc", [128, 512]); gk_bc = SB("gk_bc", [128, 512])
        gvc_bc = SB("gvc_bc", [128, 512]); bs_bc = SB("bs_bc", [128, 2, 4, 128]); bf_bc = SB("bf_bc", [128, 2, 8])
        for i2 in range(2):
            S.dma('sp', gq_t[:, i2, :], I['g_q'][i2].partition_broadcast(128), reads=[], writes=['gq_t'])
            S.dma('sp', gk_t[:, i2, :], I['g_k'][i2].partition_broadcast(128), reads=[], writes=['gk_t'])
            S.dma('sp', bf_bc[:, i2, :], I['b_f'][i2].partition_broadcast(128), reads=[], writes=['bf_bc'])
            for g in range(4):
                S.dma('sp', bs_bc[:, i2, g, :], I['b_s'][i2, g].partition_broadcast(128), reads=[], writes=['bs_bc'])

        def load_gqk(i2):
            V('dve', 'tensor_scalar', gq_bc[:, :].rearrange("p (h d) -> p h d", h=8), gq_t[:, i2, :].unsqueeze(1).to_broadcast([128, 8, 64]),
              0.125, None, reads=['gq_t'], writes=['gq_bc'], op0=ALU.mult)
            V('dve', 'tensor_scalar', gk_bc[:, :].rearrange("p (h d) -> p h d", h=8), gk_t[:, i2, :].unsqueeze(1).to_broadcast([128, 8, 64]),
              1.0, None, reads=['gk_t'], writes=['gk_bc'], op0=ALU.mult)
        wf_f = SB("wf_f", [128, 2, 8, 8]); wf_b = SB("wf_b", [128, 2, 8, 8], BF16)
        for i2 in range(2):
            S.dma('sp', wf_f[:, i2, :, :], I['w_in_even'][i2].rearrange("(kc p) n -> p kc n", p=128)[:, :, 1536:1544], reads=[], writes=['wf_f'])
        V('dve', 'tensor_copy', wf_b[:], wf_f[:], reads=['wf_f'], writes=['wf_b'])
        wsT = SB("wsT", [128, 2, 4, 128], BF16)
        wsf = SB("wsf", [128, 128])
        for i2 in range(2):
            for g in range(4):
                S.dma('sp', wsf[:], I['w_s'][i2, g], reads=[], writes=['wsf'])
                S.op('pool', lambda: nc.gpsimd.affine_select(wsf[:], wsf[:], [[-1, 128]], ALU.is_ge, 0.0, base=0, channel_multiplier=1),
                     reads=['wsf'], writes=['wsf'])
                b = bank('gen')
                mm(ps[b][:, :128], wsf[:], ident_f[:], True, True, ['wsf', 'ident_f'], [P(b)])
                act(wsT[:, i2, g, :], ps[b][:, :128], AF.Copy, [P(b)], ['wsT'])

        NWB = 4
        wbuf = [SB(f"wbuf{i}", [128, 4096], BF16) for i in range(NWB)]
        wrr = [0]

        def wreq(src3, npart, a, b):
            i = wrr[0] % NWB
            wrr[0] += 1
            view = wbuf[i][:npart, :a * b].rearrange("p (a b) -> p a b", a=a)
            S.dma('sp', view, src3, reads=['WB'], writes=[('wbuf', i)])
            return view, ('wbuf', i)

        xT = SB("xT", [128, 8, 512])
        xn = SB("xn", [128, 8, 512], BF16)
        sq = SB("sq", [128, 2, 512], BF16)
        rstd = SB("rstd", [128, 512])
        gbuf = SB("gbuf", [128, 11, 512], BF16)
        hb = [SB(f"hb{i}", [128, 514]) for i in range(2)]
        ca = [SB(f"ca{i}", [128, 512]) for i in range(2)]
        print("SBUF remaining after hb/ca:", nc.sbuf_bytes_remaining)
        sgt = SB("sgt", [128, 512])
        tmpf = SB("tmpf", [128, 512]); sqh = tmpf; ssq = SB("ssq", [128, 8]); qf = SB("qf", [128, 512]); kf = SB("kf", [128, 512])
        vf = SB("vf", [128, 512]); qb = SB("qb", [128, 512], BF16); kb = SB("kb", [128, 512], BF16)
        lz = SB("lz", [128, 8]); logf = SB("logf", [128, 4, 8])
        vb = SB("vb", [128, 1, 8, 128], BF16)
        V('pool', 'memset', vb[:], 0.0, reads=[], writes=['vb'])
        V('pool', 'memset', vb[:, :, :, 64:65], 1.0, reads=[], writes=['vb'])
        qT = SB("qT", [128, 8, 512], BF16)
        V('pool', 'memset', qT[:], 1.0, reads=[], writes=['qT'])
        kT = SB("kT", [64, 8, 128], BF16)
        cc = SB("cc", [8, 512]); cr = SB("cr", [8, 512]); caug = SB("caug", [8, 3, 512], BF16); ncaug = SB("ncaug", [8, 3, 512], BF16)
        KB = 1024
        kbuf = [SB(f"kbuf{i}", [128, KB + 512], BF16) for i in range(2)]
        vbuf = [SB(f"vbuf{i}", [128, KB // 128 + 4, 128], BF16) for i in range(2)]
        for i in range(2):
            V('pool', 'memset', kbuf[i][:], 0.0, reads=[], writes=[('kbuf', i)])
        pT = [SB(f"pT{i}", [128, 512], BF16) for i in range(3)]
        osb = SB("osb", [65, 512]); rec = osb; bcs = SB("bcs", [64, 512])
        oa = SB("oa", [64, 8, 512], BF16)
        ub = SB("ub", [128, 4, 544])
        tmpg = SB("tmpg", [128, 512])
        oc = SB("oc", [128, 4, 512], BF16)
        ug = SB("ug", [128, 4, 512], BF16)
        zf = qf; vcb = kb
        rs1 = SB("rs1", [128, 8])
        ostage = tmpg

        kvrr = [0]
        ptrr = [0]

        def make_stream(sid, T, ntok_scr):
            st = Stream()
            st.sid = sid
            st.T = T
            st.TS = min(T, 128)
            st.NS = T // st.TS
            st.KT = [dscr(f"KT_{sid}_{i}", [8, 70, ntok_scr]) for i in range(2)]
            st.Vs = [dscr(f"V_{sid}_{i}", [8, 128, ntok_scr // 128, 128]) for i in range(2)]
            st.histF = SB(f"histF_{sid}", [128, 44, 8])
            st.histB = SB(f"histB_{sid}", [128, 4, 4])
            st.histD = SB(f"histD_{sid}", [128, 4, 60])
            st.carry = SB(f"carry_{sid}", [8, 2])
            st.kF, st.kB, st.kD, st.kC = f"histF_{sid}", f"histB_{sid}", f"histD_{sid}", f"carry_{sid}"
            st.o0 = 0
            return st

        def rmsnorm_x(st, gt, l):
            T = st.T
            for kc in range(8):
                b = kc % 2
                act(sq[:, b, :T], xT[:, kc, :T], AF.Square, ['xT'], [('sq', b)])
                mm(ps[7][:, :T], ones_b[:], sq[:, b, :T], kc == 0, kc == 7, [('sq', b), 'ones_b'], [P(7)], sig=True)
            V('dve', 'tensor_scalar', rstd[:, :T], ps[7][:, :T], 1.0 / D, EPS, reads=[P(7)], writes=['rstd'], op0=ALU.mult, op1=ALU.add)
            act(rstd[:, :T], rstd[:, :T], AF.Sqrt, ['rstd'], ['rstd'])
            V('dve', 'reciprocal', rstd[:, :T], rstd[:, :T], reads=['rstd'], writes=['rstd'])
            for kc in range(8):
                V('dve', 'scalar_tensor_tensor', xn[:, kc, :T], xT[:, kc, :T], gt[:, kc, l:l + 1], rstd[:, :T],
                  reads=['xT', 'rstd'], writes=['xn'], op0=ALU.mult, op1=ALU.mult)

        def head_norm(st, b, gbc, outf, okey, fin=None, finkey=None):
            TS = st.TS
            act(sqh[:TS, :], ps[b][:TS, :], AF.Square, [P(b)], ['tmpf'])
            V('dve', 'tensor_reduce', ssq[:TS, :], sqh[:TS, :].rearrange("p (h d) -> p h d", h=8), AX.X, ALU.add, reads=['tmpf'], writes=['ssq'])
            V('dve', 'tensor_scalar', ssq[:TS, :], ssq[:TS, :], 1.0 / 64, EPS, reads=['ssq'], writes=['ssq'], op0=ALU.mult, op1=ALU.add)
            act(ssq[:TS, :], ssq[:TS, :], AF.Sqrt, ['ssq'], ['ssq'])
            V('dve', 'reciprocal', ssq[:TS, :], ssq[:TS, :], reads=['ssq'], writes=['ssq'])
            V('dve', 'tensor_tensor', outf[:TS, :].rearrange("p (h d) -> p h d", h=8), ps[b][:TS, :].rearrange("p (h d) -> p h d", h=8),
              ssq[:TS, :].unsqueeze(2).to_broadcast([TS, 8, 64]), ALU.mult, reads=[P(b), 'ssq'], writes=[okey])
            if fin is None:
                fin, finkey = outf, okey
            V('dve', 'tensor_tensor', fin[:TS, :], outf[:TS, :], gbc[:TS, :], ALU.mult, reads=[okey, 'gq_bc', 'gk_bc'], writes=[finkey])

        def transp_heads(st, src_b, dstT, s, key_src, key_dst):
            TS = st.TS
            for h0 in (0, 4):
                b = bank('gen')
                for hh in range(4):
                    h = h0 + hh
                    mm(ps[b][:64, hh * TS:(hh + 1) * TS], src_b[:TS, h * 64:(h + 1) * 64], ident_b[:TS, :TS], True, True,
                       [key_src, 'ident_b'], [P(b)])
                act(dstT[:64, h0:h0 + 4, s * TS:(s + 1) * TS], ps[b][:64, :4 * TS].rearrange("p (h t) -> p h t", h=4), AF.Copy, [P(b)], [key_dst])

        def kv_finish(st, i2, t0):
            T, TS, NS = st.T, st.TS, st.NS
            for s in range(NS):
                mm(ps[7][:8, s * TS:(s + 1) * TS], logf[:TS, s, :], utri_f[:TS, :TS], True, True, ['logf', 'utri_f'], [P(7)])
            for s in range(NS):
                V('dve', 'tensor_scalar', cc[:8, s * TS:(s + 1) * TS], ps[7][:8, s * TS:(s + 1) * TS], st.carry[:8, i2:i2 + 1], None,
                  reads=[P(7), st.kC], writes=['cc'], op0=ALU.add)
                V('dve', 'tensor_copy', st.carry[:8, i2:i2 + 1], cc[:8, (s + 1) * TS - 1:(s + 1) * TS], reads=['cc'], writes=[st.kC])
            V('dve', 'tensor_copy', caug[:, 0, :T], cc[:, :T], reads=['cc'], writes=['caug'])
            V('dve', 'tensor_tensor', cr[:, :T], cc[:, :T], caug[:, 0, :T], ALU.subtract, reads=['cc', 'caug'], writes=['cr'])
            V('dve', 'tensor_copy', caug[:, 1, :T], cr[:, :T], reads=['cr'], writes=['caug'])
            V('dve', 'tensor_tensor', cr[:, :T], cr[:, :T], caug[:, 1, :T], ALU.subtract, reads=['cr', 'caug'], writes=['cr'])
            V('dve', 'tensor_copy', caug[:, 2, :T], cr[:, :T], reads=['cr'], writes=['caug'])
            V('dve', 'tensor_scalar', ncaug[:, :, :T], caug[:, :, :T], -1.0, None, reads=['caug'], writes=['ncaug'], op0=ALU.mult)
            kk = ('KV', st.sid, i2)
            S.dma('act', st.KT[i2][:, 64:67, t0:t0 + T], ones3[:, :, :T], reads=['ones3'], writes=[kk])
            S.dma('act', st.KT[i2][:, 67:70, t0:t0 + T], ncaug[:, :, :T], reads=['ncaug'], writes=[kk])

        def attention(st, i2, t0):
            T, TS, NS = st.T, st.TS, st.NS
            kk = ('KV', st.sid, i2)
            S.dma('act', cq_scr[:, :, :T], caug[:, :, :T], reads=['caug'], writes=['cq_scr'])
            S.dma('act', qT[64:67, :, :T], cq_scr[:, :, :T].rearrange("h a t -> a h t"), reads=['cq_scr'], writes=['qT'])
            ntot = t0 + T
            chunks = []
            c0 = 0
            while c0 < ntot:
                c1 = min(c0 + KB, ntot)
                if ntot - c1 <= 512 and ntot - c1 > 0:
                    c1 = ntot
                chunks.append((c0, c1))
                c0 = c1
            blocks = []
            for ci, (c0, c1) in enumerate(chunks):
                col = 0
                while col < c1 - c0:
                    gpos = c0 + col
                    if gpos < t0:
                        ksz, r = 128, None
                    else:
                        ksz, r = TS, (gpos - t0) // TS
                    blocks.append((ci, col, ksz, r))
                    col += ksz
            nb = len(blocks)
            for h in range(8):
                obk = bank('O')
                loaded = {}

                def ensure(ci):
                    if ci not in loaded:
                        i = kvrr[0] % 2
                        kvrr[0] += 1
                        c0, c1 = chunks[ci]
                        n = c1 - c0
                        nkt = (n + 127) // 128
                        S.dma('sp', kbuf[i][:70, :n], st.KT[i2][h, :, c0:c1], reads=[kk], writes=[('kbuf', i)])
                        S.dma('sp', vbuf[i][:, :nkt, :], st.Vs[i2][h, :, c0 // 128:c0 // 128 + nkt, :], reads=[kk], writes=[('vbuf', i)])
                        loaded[ci] = i
                    return loaded[ci]

                def emitS(bi):
                    ci, col, ksz, r = blocks[bi]
                    i = ensure(ci)
                    sb_ = bank('S')
                    mm(ps[sb_][:ksz, :T], kbuf[i][:, col:col + ksz], qT[:, h, :T], True, r is None, [('kbuf', i), 'qT'], [P(sb_)])
                    if r is not None:
                        mm(ps[sb_][:ksz, :T], ident_b[:ksz, :ksz], dmask[:ksz, r, :T], False, True, ['ident_b', 'dmask'], [P(sb_)])
                    return sb_

                sbk = {0: emitS(0)}
                for bi in range(nb):
                    if bi + 1 < nb:
                        sbk[bi + 1] = emitS(bi + 1)
                    ci, col, ksz, r = blocks[bi]
                    i = loaded[ci]
                    sb_ = sbk.pop(bi)
                    pi = ptrr[0] % 3
                    ptrr[0] += 1
                    act(pT[pi][:ksz, :T], ps[sb_][:ksz, :T], AF.Exp, [P(sb_)], [('pT', pi)])
                    mm(ps[obk][:, :T], vbuf[i][:ksz, col // 128, :], pT[pi][:ksz, :T], bi == 0, bi == nb - 1, [('vbuf', i), ('pT', pi)], [P(obk)])
                act(osb[:65, :T], ps[obk][:65, :T], AF.Copy, [P(obk)], ['osb'])
                V('dve', 'reciprocal', osb[64:65, :T], osb[64:65, :T], reads=['osb'], writes=['osb'])
                mm(ps[7][:64, :T], ones_f[64:65, :64], osb[64:65, :T], True, True, ['osb', 'ones_f'], [P(7)])
                act(bcs[:64, :T], ps[7][:64, :T], AF.Copy, [P(7)], ['bcs'])
                V('dve', 'tensor_tensor', oa[:64, h, :T], osb[:64, :T], bcs[:64, :T], ALU.mult, reads=['osb', 'bcs'], writes=['oa'])

        def residual_add(st, b, dc):
            T = st.T
            V('dve', 'tensor_tensor', xT[:, dc, :T], xT[:, dc, :T], ps[b][:, :T], ALU.add, reads=['xT', P(b)], writes=['xT'])

        def even_layer(st, l, t0, O_k, O_v, O_lf):
            T, TS, NS = st.T, st.TS, st.NS
            i2 = l // 2
            wie3 = WB['wie'][i2].rearrange("(kc p) n -> p kc n", p=128)
            rmsnorm_x(st, gmix, l)
            load_gqk(i2)
            kk = ('KV', st.sid, i2)
            kt0 = t0 // 128
            o0 = st.o0
            Wq, kq = wreq(wie3[:, :, 0:512], 128, 8, 512)
            Wk, kk_ = wreq(wie3[:, :, 512:1024], 128, 8, 512)
            Wv, kv_ = wreq(wie3[:, :, 1024:1536], 128, 8, 512)
            for s in range(NS):
                ts = slice(s * TS, (s + 1) * TS)
                b = bank('gen')
                for kc in range(8):
                    mm(ps[b][:TS, :], xn[:, kc, ts], Wq[:, kc, :], kc == 0, kc == 7, ['xn', kq], [P(b)])
                head_norm(st, b, gq_bc, qf, 'qf', qb, 'qb')
                transp_heads(st, qb, qT, s, 'qb', 'qT')
                if DBG < 0.5:
                    continue
                b = bank('gen')
                for kc in range(8):
                    mm(ps[b][:TS, :], xn[:, kc, ts], Wk[:, kc, :], kc == 0, kc == 7, ['xn', kk_], [P(b)])
                head_norm(st, b, gk_bc, kf, 'kf')
                S.dma('act', O_k[i2, o0 + s * TS:o0 + (s + 1) * TS, :], kf[:TS, :], reads=['kf'], writes=['O_k'])
                act(kb[:TS, :], kf[:TS, :], AF.Copy, ['kf'], ['kb'])
                transp_heads(st, kb, kT, 0, 'kb', 'kT')
                S.dma('act', st.KT[i2][:, 0:64, t0 + s * TS:t0 + (s + 1) * TS].rearrange("h d t -> d h t"), kT[:64, :, :TS], reads=['kT'], writes=[kk])
                if DBG < 0.7:
                    continue
                b = bank('gen')
                for kc in range(8):
                    mm(ps[b][:TS, :], xn[:, kc, ts], Wv[:, kc, :], kc == 0, kc == 7, ['xn', kv_], [P(b)])
                act(vf[:TS, :], ps[b][:TS, :], AF.Copy, [P(b)], ['vf'])
                V('dve', 'tensor_copy', vb[:TS, 0, :, 0:64], vf[:TS, :].rearrange("p (h d) -> p h d", h=8), reads=['vf'], writes=['vb'])
                S.dma('act', O_v[i2, o0 + s * TS:o0 + (s + 1) * TS, :], vf[:TS, :], reads=['vf'], writes=['O_v'])
                S.dma('act', st.Vs[i2][:, 0:TS, kt0 + s, :].rearrange("h p d -> p h d"), vb[:TS, 0, :, :], reads=['vb'], writes=[kk])
                if DBG < 0.9:
                    continue
                for kc in range(8):
                    mm(ps[7][:TS, 0:8], xn[:, kc, ts], wf_b[:, i2, kc, :], kc == 0, kc == 7, ['xn', 'wf_b'], [P(7)])
                V('dve', 'tensor_tensor', lz[:TS, :], ps[7][:TS, 0:8], bf_bc[:TS, i2, :], ALU.add, reads=[P(7), 'bf_bc'], writes=['lz'])
                act(lz[:TS, :], lz[:TS, :], AF.Exp, ['lz'], ['lz'], scale=-1.0)
                act(lz[:TS, :], lz[:TS, :], AF.Ln, ['lz'], ['lz'], bias=1.0)
                V('dve', 'tensor_scalar', logf[:TS, s, :], lz[:TS, :], -1.0, None, reads=['lz'], writes=['logf'], op0=ALU.mult)
                S.dma('act', O_lf[i2, o0 + s * TS:o0 + (s + 1) * TS, :], logf[:TS, s, :], reads=['logf'], writes=['O_lf'])
            if DBG < 2:
                return
            kv_finish(st, i2, t0)
            if DBG < 3:
                return
            attention(st, i2, t0)
            if DBG < 4:
                return
            Wbg, kbg = wreq(wie3[:, :, OFF_BG:OFF_BG + 512], 128, 8, 512)
            Wcg, kcg = wreq(wie3[:, :, OFF_BG + 512:OFF_BG + 1024], 128, 8, 512)
            Wxi, kxi = wreq(wie3[:, :, OFF_BG + 1024:OFF_BG + 1536], 128, 8, 512)
            V('dve', 'tensor_copy', ub[:, :, 0:2], st.histB[:, :, i2 * 2:i2 * 2 + 2], reads=[st.kB], writes=['ub'])
            for c in range(4):
                cs = slice(c * 128, (c + 1) * 128)
                b = bank('gen')
                for kc in range(8):
                    mm(ps[b][:, :T], Wcg[:, kc, cs], xn[:, kc, :T], kc == 0, kc == 7, ['xn', kcg], [P(b)])
                act(tmpf[:, :T], ps[b][:, :T], AF.Copy, [P(b)], ['tmpf'])
                b = bank('gen')
                for kc in range(8):
                    mm(ps[b][:, :T], Wxi[:, kc, cs], xn[:, kc, :T], kc == 0, kc == 7, ['xn', kxi], [P(b)])
                V('dve', 'tensor_tensor', ub[:, c, 2:2 + T], tmpf[:, :T], ps[b][:, :T], ALU.mult, reads=['tmpf', P(b)], writes=['ub'])
                V('dve', 'tensor_scalar', cx[:, c, :T], ub[:, c, 2:2 + T], wB[:, c, i2 * 3 + 2:i2 * 3 + 3], None, reads=['ub'], writes=[('cx', c)], op0=ALU.mult)
                for j in (1, 0):
                    fma('dve', cx[:, c, :T], ub[:, c, j:j + T], wB[:, c, i2 * 3 + j:i2 * 3 + j + 1], ['ub'], ('cx', c))
                b = bank('gen')
                for kc in range(8):
                    mm(ps[b][:, :T], Wbg[:, kc, cs], xn[:, kc, :T], kc == 0, kc == 7, ['xn', kbg], [P(b)])
                V('dve', 'tensor_tensor', ob[:, c, :T], ps[b][:, :T], cx[:, c, :T], ALU.mult, reads=[P(b), ('cx', c)], writes=['ob'])
            V('dve', 'tensor_copy', st.histB[:, :, i2 * 2:i2 * 2 + 2], ub[:, :, T:T + 2], reads=['ub'], writes=[st.kB])
            if DBG < 5:
                return
            woA = WB['woe'][i2][0:512, :].rearrange("(h d) n -> d h n", d=64)
            woB = WB['woe'][i2][512:1024, :].rearrange("(c p) n -> p c n", p=128)
            WA0, kA0 = wreq(woA[:, :, 0:512], 64, 8, 512)
            WA1, kA1 = wreq(woA[:, :, 512:1024], 64, 8, 512)
            WBo, kBo = wreq(woB, 128, 4, 1024)
            for dc in range(8):
                b = bank('gen')
                WA, kA = (WA0, kA0) if dc < 4 else (WA1, kA1)
                dsl = slice((dc % 4) * 128, (dc % 4 + 1) * 128)
                for h in range(8):
                    mm(ps[b][:, :T], WA[:64, h, dsl], oa[:64, h, :T], h == 0, False, ['oa', kA], [P(b)])
                for c in range(4):
                    mm(ps[b][:, :T], WBo[:, c, dc * 128:(dc + 1) * 128], ob[:, c, :T], False, c == 3, ['ob', kBo], [P(b)])
                residual_add(st, b, dc)

        def gelu_from_psum(b, npart, T, outf, key):
            act(tmpf[:npart, :T], ps[b][:npart, :T], AF.Square, [P(b)], ['tmpf'])
            V('dve', 'tensor_scalar', tmpf[:npart, :T], tmpf[:npart, :T], 0.044715, 1.0, reads=['tmpf'], writes=['tmpf'], op0=ALU.mult, op1=ALU.add)
            V('dve', 'tensor_tensor', tmpf[:npart, :T], tmpf[:npart, :T], ps[b][:npart, :T], ALU.mult, reads=['tmpf', P(b)], writes=['tmpf'])
            act(tmpf[:npart, :T], tmpf[:npart, :T], AF.Sigmoid, ['tmpf'], ['tmpf'], scale=1.5957691216057308)
            V('dve', 'tensor_tensor', outf, tmpf[:npart, :T], ps[b][:npart, :T], ALU.mult, reads=['tmpf', P(b)], writes=[key])

        def odd_layer(st, l, t0, O_vc):
            T, TS, NS = st.T, st.TS, st.NS
            i2 = l // 2
            wio3 = WB['wio'][i2].rearrange("(kc p) n -> p kc n", p=128)
            rmsnorm_x(st, gmix, l)
            S.dma('sp', gvc_bc[:, :], I['g_vc'][i2].partition_broadcast(128), reads=[], writes=['gvc_bc'])
            o0 = st.o0
            Wad, kad = wreq(wio3[:, :, 1024:1536], 128, 8, 512)
            Wgd, kgd = wreq(wio3[:, :, 1536:2048], 128, 8, 512)
            V('dve', 'tensor_copy', ub[:, :, 0:30], st.histD[:, :, i2 * 30:i2 * 30 + 30], reads=[st.kD], writes=['ub'])
            taps = []
            for c in range(4):
                cs = slice(c * 128, (c + 1) * 128)
                b = bank('gen')
                for kc in range(8):
                    mm(ps[b][:, :T], Wad[:, kc, cs], xn[:, kc, :T], kc == 0, kc == 7, ['xn', kad], [P(b)])
                act(tmpf[:, :T], ps[b][:, :T], AF.Copy, [P(b)], ['tmpf'])
                b = bank('gen')
                for kc in range(8):
                    mm(ps[b][:, :T], Wgd[:, kc, cs], xn[:, kc, :T], kc == 0, kc == 7, ['xn', kgd], [P(b)])
                act(tmpg[:, :T], ps[b][:, :T], AF.Sigmoid, [P(b)], ['tmpg'])
                V('dve', 'tensor_tensor', ub[:, c, 30:30 + T], tmpf[:, :T], tmpg[:, :T], ALU.mult, reads=['tmpf', 'tmpg'], writes=['ub'])

                def first_tap(c=c):
                    V('dve', 'tensor_scalar', cx[:, c, :T], ub[:, c, 30:30 + T], wD[:, c, i2 * 31 + 30:i2 * 31 + 31], None, reads=['ub'], writes=[('cx', c)], op0=ALU.mult)
                taps.append(first_tap)
                for j in range(30):
                    def tap(c=c, j=j):
                        fma('dve', cx[:, c, :T], ub[:, c, j:j + T], wD[:, c, i2 * 31 + j:i2 * 31 + j + 1], ['ub'], ('cx', c))
                    taps.append(tap)
            tpos = [0]

            def emit_taps(n):
                for _ in range(n):
                    if tpos[0] < len(taps):
                        taps[tpos[0]]()
                        tpos[0] += 1

            per = (len(taps) + 4 + NS - 1) // (4 + NS)
            Wu, ku = wreq(wio3[:, :, 0:512], 128, 8, 512)
            Wvc, kvc = wreq(wio3[:, :, 512:1024], 128, 8, 512)
            for c in range(4):
                b = bank('gen')
                for kc in range(8):
                    mm(ps[b][:, :T], Wu[:, kc, c * 128:(c + 1) * 128], xn[:, kc, :T], kc == 0, kc == 7, ['xn', ku], [P(b)])
                gelu_from_psum(b, 128, T, ug[:, c, :T], 'ug')
                emit_taps(per)
            for s in range(NS):
                ts = slice(s * TS, (s + 1) * TS)
                b = bank('gen')
                for kc in range(8):
                    mm(ps[b][:TS, :], xn[:, kc, ts], Wvc[:, kc, :], kc == 0, kc == 7, ['xn', kvc], [P(b)])
                gelu_from_psum(b, TS, 512, zf[:TS, :], 'qf')
                act(sqh[:TS, :], zf[:TS, :], AF.Square, ['qf'], ['tmpf'])
                V('dve', 'tensor_reduce', rs1[:TS, 0:1], sqh[:TS, :], AX.X, ALU.add, reads=['tmpf'], writes=['rs1'])
                V('dve', 'tensor_scalar', rs1[:TS, 0:1], rs1[:TS, 0:1], 1.0 / 512, EPS, reads=['rs1'], writes=['rs1'], op0=ALU.mult, op1=ALU.add)
                act(rs1[:TS, 0:1], rs1[:TS, 0:1], AF.Sqrt, ['rs1'], ['rs1'])
                V('dve', 'reciprocal', rs1[:TS, 0:1], rs1[:TS, 0:1], reads=['rs1'], writes=['rs1'])
                V('dve', 'scalar_tensor_tensor', zf[:TS, :], zf[:TS, :], rs1[:TS, 0:1], gvc_bc[:TS, :], reads=['qf', 'rs1', 'gvc_bc'], writes=['qf'],
                  op0=ALU.mult, op1=ALU.mult)
                if O_vc is not None:
                    S.dma('act', O_vc[i2, o0 + s * TS:o0 + (s + 1) * TS, :], zf[:TS, :], reads=['qf'], writes=['O_vc'])
                act(vcb[:TS, :], zf[:TS, :], AF.Copy, ['qf'], ['kb'])
                b = bank('gen')
                for g in range(4):
                    mm(ps[b][:, g * TS:(g + 1) * TS], vcb[:TS, g * 128:(g + 1) * 128], wsT[:TS, i2, g, :TS], True, True, ['kb', 'wsT'], [P(b)])
                V('dve', 'tensor_tensor', tmpg[:, :4 * TS].rearrange("p (g t) -> p g t", g=4), ps[b][:, :4 * TS].rearrange("p (g t) -> p g t", g=4),
                  bs_bc[:, i2, :, :TS], ALU.add, reads=[P(b), 'bs_bc'], writes=['tmpg'])
                V('dve', 'tensor_tensor', oc[:, :, ts], tmpg[:, :4 * TS].rearrange("p (g t) -> p g t", g=4), ug[:, :, ts], ALU.mult,
                  reads=['tmpg', 'ug'], writes=['oc'])
                emit_taps(per)
            emit_taps(len(taps))
            V('dve', 'tensor_copy', st.histD[:, :, i2 * 30:i2 * 30 + 30], ub[:, :, T:T + 30], reads=['ub'], writes=[st.kD])
            for c in range(4):
                b2 = c % 2
                act(sq[:, b2, :T], cx[:, c, :T], AF.Square, [('cx', c)], [('sq', b2)])
                mm(ps[7][:, :T], ones_b[:], sq[:, b2, :T], c == 0, c == 3, [('sq', b2), 'ones_b'], [P(7)], sig=True)
            V('dve', 'tensor_scalar', rstd[:, :T], ps[7][:, :T], 1.0 / 512, EPS, reads=[P(7)], writes=['rstd'], op0=ALU.mult, op1=ALU.add)
            act(rstd[:, :T], rstd[:, :T], AF.Sqrt, ['rstd'], ['rstd'])
            V('dve', 'reciprocal', rstd[:, :T], rstd[:, :T], reads=['rstd'], writes=['rstd'])
            for c in range(4):
                V('dve', 'scalar_tensor_tensor', tmpf[:, :T], cx[:, c, :T], gd[:, c, i2:i2 + 1], rstd[:, :T], reads=[('cx', c), 'rstd'], writes=['tmpf'],
                  op0=ALU.mult, op1=ALU.mult)
                act(ob[:, c, :T], tmpf[:, :T], AF.Silu, ['tmpf'], ['ob'])
            woo3 = WB['woo'][i2].rearrange("(c p) n -> p c n", p=128)
            W0, k0 = wreq(woo3[:, 0:4, :], 128, 4, 1024)
            W1, k1 = wreq(woo3[:, 4:8, :], 128, 4, 1024)
            for dc in range(8):
                b = bank('gen')
                for c in range(4):
                    mm(ps[b][:, :T], W0[:, c, dc * 128:(dc + 1) * 128], oc[:, c, :T], c == 0, False, ['oc', k0], [P(b)])
                for c in range(4):
                    mm(ps[b][:, :T], W1[:, c, dc * 128:(dc + 1) * 128], ob[:, c, :T], False, c == 3, ['ob', k1], [P(b)])
                residual_add(st, b, dc)

        def conv3_ffn(st, l, j, b, hbi, e, outap, outkey):
            T = st.T
            h = hb[hbi]
            hk = ('hb', hbi)
            hh = ('hbh', hbi)
            V('dve', 'tensor_copy', h[:, 0:2], st.histF[:, j, l * 2:l * 2 + 2], reads=[st.kF], writes=[hh])
            act(h[:, 2:2 + T], ps[b][:, :T], AF.Copy, [P(b)], [hk])
            act(outap, ps[b][:, :T], AF.Copy, [P(b)], [outkey], scale=wF[:, j, l * 3 + 2:l * 3 + 3])
            for jj in (1, 0):
                fma('dve', outap, h[:, jj:jj + T], wF[:, j, l * 3 + jj:l * 3 + jj + 1], [hk, hh], outkey)
            act(st.histF[:, j, l * 2:l * 2 + 2], h[:, T:T + 2], AF.Copy, [hk], [st.kF])

        def ffn(st, l):
            T = st.T
            rmsnorm_x(st, gffn, l)
            wup3 = WB['wup'][l].rearrange("(kc p) n -> p kc n", p=128)
            wdn3 = WB['wdn'][l].rearrange("(fc p) n -> p fc n", p=128)
            for half in range(2):
                f0 = half * 11
                groups = [(f0, 4), (f0 + 4, 4), (f0 + 8, 3)]
                for (g0, nch) in groups:
                    Wg, kg = wreq(wup3[:, :, g0 * 128:(g0 + nch) * 128], 128, 8, nch * 128)
                    Wv2, kv2 = wreq(wup3[:, :, DFF + g0 * 128:DFF + (g0 + nch) * 128], 128, 8, nch * 128)
                    for jj in range(nch):
                        j = g0 + jj
                        cs = slice(jj * 128, (jj + 1) * 128)
                        bg_ = bank('gen')
                        for kc in range(8):
                            mm(ps[bg_][:, :T], Wg[:, kc, cs], xn[:, kc, :T], kc == 0, kc == 7, ['xn', kg], [P(bg_)])
                        bv_ = bank('gen')
                        for kc in range(8):
                            mm(ps[bv_][:, :T], Wv2[:, kc, cs], xn[:, kc, :T], kc == 0, kc == 7, ['xn', kv2], [P(bv_)])
                        conv3_ffn(st, l, j, bg_, 0, 'dve', ca[0][:, :T], ('ca', 0))
                        conv3_ffn(st, l, 22 + j, bv_, 1, 'pool', ca[1][:, :T], ('ca', 1))
                        act(sgt[:, :T], ca[0][:, :T], AF.Silu, [('ca', 0)], ['sgt'])
                        V('dve', 'tensor_tensor', gbuf[:, j - f0, :T], sgt[:, :T], ca[1][:, :T], ALU.mult, reads=['sgt', ('ca', 1)], writes=[('gbuf', j - f0)])
                Wd = [wreq(wdn3[:, g0:g0 + n, :], 128, n, 1024) for (g0, n) in groups]
                for dc in range(8):
                    b = bank('gen')
                    idx = 0
                    for gi, (g0, n) in enumerate(groups):
                        for ff in range(n):
                            fc = g0 + ff - f0
                            mm(ps[b][:, :T], Wd[gi][0][:, ff, dc * 128:(dc + 1) * 128], gbuf[:, fc, :T], idx == 0, idx == 10,
                               [('gbuf', fc), Wd[gi][1]], [P(b)])
                            idx += 1
                    residual_add(st, b, dc)

        def load_x(st, src, t0):
            T, TS, NS = st.T, st.TS, st.NS
            for s in range(NS):
                S.dma('sp', xstage[:TS, :], src[t0 + s * TS:t0 + (s + 1) * TS, :], reads=[], writes=['xstage'])
                for k0 in (0, 4):
                    b = bank('gen')
                    for kk2 in range(4):
                        kc = k0 + kk2
                        mm(ps[b][:, kk2 * TS:(kk2 + 1) * TS], xstage[:TS, kc * 128:(kc + 1) * 128], ident_f[:TS, :TS], True, True,
                           ['xstage', 'ident_f'], [P(b)])
                    act(xT[:, k0:k0 + 4, s * TS:(s + 1) * TS], ps[b][:, :4 * TS].rearrange("p (k t) -> p k t", k=4), AF.Copy, [P(b)], ['xT'])

        def store_x(st, dst, t0):
            T, TS, NS = st.T, st.TS, st.NS
            for s in range(NS):
                for k0 in (0, 4):
                    b = bank('gen')
                    for kk2 in range(4):
                        kc = k0 + kk2
                        mm(ps[b][:TS, kk2 * 128:(kk2 + 1) * 128], xT[:, kc, s * TS:(s + 1) * TS], ident_f[:, :], True, True, ['xT', 'ident_f'], [P(b)])
                    act(xstage[:TS, k0 * 128:(k0 + 4) * 128], ps[b][:TS, :], AF.Copy, [P(b)], ['xstage'])
                S.dma('act', dst[t0 + s * TS:t0 + (s + 1) * TS, :], xstage[:TS, :], reads=['xstage'], writes=['O_y'])

        def store_T(src3, key, ncn, R, r0, dst):
            for c0 in range(0, ncn, 4):
                n = min(4, ncn - c0)
                b = bank('gen')
                for c in range(n):
                    mm(ps[b][:R, c * 128:(c + 1) * 128], src3[:, c0 + c, r0:r0 + R], ident_f[:, :], True, True, [key, 'ident_f'], [P(b)])
                act(ostage[:R, :n * 128], ps[b][:R, :n * 128], AF.Copy, [P(b)], ['tmpg'])
                S.dma('act', dst[:, c0 * 128:(c0 + n) * 128], ostage[:R, :n * 128], reads=['tmpg'], writes=['O_st'])

        def run_layers(st, t0, O_k, O_v, O_lf, O_vc):
            for l in range(NL):
                if l % 2 == 0:
                    even_layer(st, l, t0, O_k, O_v, O_lf)
                else:
                    odd_layer(st, l, t0, O_vc)
                if DBG >= 6:
                    ffn(st, l)

        def store_states(st, O_cb, O_cd, O_cf):
            for i2 in range(2):
                store_T(st.histB, st.kB, 4, 2, i2 * 2, O_cb[i2 * 2:i2 * 2 + 2, :])
                store_T(st.histD, st.kD, 4, 30, i2 * 30, O_cd[i2 * 30:i2 * 30 + 30, :])
            for l in range(4):
                store_T(st.histF, st.kF, 44, 2, l * 2, O_cf[l * 2:l * 2 + 2, :])

        if do_sample:
            ss = make_stream('s', DEC, PAST + 512)
            load_T(I['state_conv_ffn'], 8, 5632, ss.histF, ss.kF)
            load_T(I['state_conv_b'], 4, 512, ss.histB, ss.kB)
            load_T(I['state_conv_d'], 60, 512, ss.histD, ss.kD)
            V('pool', 'memset', ss.carry[:], 0.0, reads=[], writes=[ss.kC])
            pre = Stream()
            pre.sid, pre.T, pre.TS, pre.NS = 's', 512, 128, 4
            pre.KT, pre.Vs, pre.carry = ss.KT, ss.Vs, ss.carry
            pre.kC = ss.kC
            for i2 in range(2):
                if 2 * i2 >= NL:
                    continue
                for t0 in range(0, PAST, 512):
                    for s in range(4):
                        r0 = t0 + s * 128
                        S.dma('sp', kf[:, :], I['cache_k'][i2, r0:r0 + 128, :], reads=[], writes=['kf'])
                        act(kb[:, :], kf[:, :], AF.Copy, ['kf'], ['kb'])
                        transp_heads(pre, kb, kT, 0, 'kb', 'kT')
                        S.dma('act', ss.KT[i2][:, 0:64, r0:r0 + 128].rearrange("h d t -> d h t"), kT[:64, :, :128], reads=['kT'], writes=[('KV', 's', i2)])
                        S.dma('sp', vf[:, :], I['cache_v'][i2, r0:r0 + 128, :], reads=[], writes=['vf'])
                        V('dve', 'tensor_copy', vb[:, 0, :, 0:64], vf[:, :].rearrange("p (h d) -> p h d", h=8), reads=['vf'], writes=['vb'])
                        S.dma('act', ss.Vs[i2][:, 0:128, r0 // 128, :].rearrange("h p d -> p h d"), vb[:, 0, :, :], reads=['vb'], writes=[('KV', 's', i2)])
                        S.dma('sp', logf[:, s, :], I['cache_logf'][i2, r0:r0 + 128, :], reads=[], writes=['logf'])
                    kv_finish(pre, i2, t0)
            load_x(ss, I['x_sample'], 0)
            ss.o0 = 0
            run_layers(ss, PAST, O['s_k'], O['s_v'], O['s_lf'], O['s_vc'])
            store_x(ss, O['y_s'], 0)
            store_states(ss, O['s_cb'], O['s_cd'], O['s_cf'])

        if NT > 0:
            sp_ = make_stream('p', 512, NTOK)
            for tname, tk in ((sp_.histF, sp_.kF), (sp_.histB, sp_.kB), (sp_.histD, sp_.kD), (sp_.carry, sp_.kC)):
                V('pool', 'memset', tname[:], 0.0, reads=[], writes=[tk])
            for ti in range(NT):
                t0 = ti * 512
                sp_.o0 = t0
                load_x(sp_, I['x_prompt'], t0)
                run_layers_prompt = run_layers
                run_layers_prompt(sp_, t0, O['p_k'], O['p_v'], O['p_lf'], None)
                store_x(sp_, O['y_p'], t0)
            store_states(sp_, O['p_cb'], O['p_cd'], O['p_cf'])

        S.finish('sp')
        print("SBUF remaining at end:", nc.sbuf_bytes_remaining)
        S.simulate()
        print("instructions:", S.nins, "counts:", {k: v for k, v in S.cnt.items() if v})
    return nc


_NC_CACHE = {}


def _prep_inputs(inputs, c, NT):
    f = lambda a: np.ascontiguousarray(a, dtype=np.float32)
    b = c % 2
    m = {
        'x_prompt': f(inputs['x_prompt'][b, :NT * 512]),
        'x_sample': f(inputs['x_sample'][c]),
        'cache_k': f(inputs['cache_k'][:, c].reshape(2, PAST, 512)),
        'cache_v': f(inputs['cache_v'][:, c].reshape(2, PAST, 512)),
        'cache_logf': f(inputs['cache_logf'][:, c]),
        'state_conv_b': f(inputs['state_conv_b'][:, c].reshape(4, 512)),
        'state_conv_d': f(inputs['state_conv_d'][:, c].reshape(60, 512)),
        'state_conv_ffn': f(inputs['state_conv_ffn'][:, c].reshape(8, 5632)),
        'conv_b': f(inputs['conv_b'].reshape(6, 512)),
        'conv_d': f(inputs['conv_d'].reshape(62, 512)),
        'conv_ffn': f(inputs['conv_ffn'].reshape(12, 5632)),
    }
    for k in ['g_mix', 'w_in_even', 'b_f', 'g_q', 'g_k', 'w_out_even', 'w_in_odd', 'g_vc', 'w_s', 'b_s', 'g_d',
              'w_out_odd', 'g_ffn', 'w_up', 'w_down']:
        m[k] = f(inputs[k])
    return m


def run(inputs, NT=32, NL=4):
    key = (NT, NL)
    if key not in _NC_CACHE:
        _NC_CACHE[key] = build(NT, NL)
    nc = _NC_CACHE[key]
    in_maps = [_prep_inputs(inputs, c, NT) for c in range(8)]
    res = run_bass_kernel_spmd(nc, in_maps, core_ids=list(range(8)))
    R = res.results
    B = 2
    T = NT * 512
    y_p = np.stack([R[b]['y_p'] for b in range(B)])
    y_s = np.stack([R[c]['y_s'] for c in range(8)])
    pk = np.stack([R[b]['p_k'] for b in range(B)], axis=1).reshape(2, B, T, 8, 64)
    pv = np.stack([R[b]['p_v'] for b in range(B)], axis=1).reshape(2, B, T, 8, 64)
    plf = np.stack([R[b]['p_lf'] for b in range(B)], axis=1)
    pcb = np.stack([R[b]['p_cb'].reshape(2, 2, 512) for b in range(B)], axis=1)
    pcd = np.stack([R[b]['p_cd'].reshape(2, 30, 512) for b in range(B)], axis=1)
    pcf = np.stack([R[b]['p_cf'].reshape(4, 2, 5632) for b in range(B)], axis=1)
    sk = np.stack([R[c]['s_k'] for c in range(8)], axis=1).reshape(2, 8, DEC, 8, 64)
    sv = np.stack([R[c]['s_v'] for c in range(8)], axis=1).reshape(2, 8, DEC, 8, 64)
    slf = np.stack([R[c]['s_lf'] for c in range(8)], axis=1)
    scb = np.stack([R[c]['s_cb'].reshape(2, 2, 512) for c in range(8)], axis=1)
    svc = np.stack([R[c]['s_vc'] for c in range(8)], axis=1)
    scd = np.stack([R[c]['s_cd'].reshape(2, 30, 512) for c in range(8)], axis=1)
    scf = np.stack([R[c]['s_cf'].reshape(4, 2, 5632) for c in range(8)], axis=1)
    return tuple(np.ascontiguousarray(a, dtype=np.float32) for a in
                 (y_p, y_s, pk, pv, plf, pcb, pcd, pcf, sk, sv, slf, scb, svc, scd, scf))


def kernel(**inputs):
    return run(inputs, NT=32, NL=4)
```

```python
import os
import numpy as np
from contextlib import ExitStack
DBG = float(os.environ.get("MK_DBG", "9"))
import concourse.bass as bass
import concourse.mybir as mybir
from concourse.bass_utils import run_bass_kernel_spmd

F32 = mybir.dt.float32
BF16 = mybir.dt.bfloat16
ALU = mybir.AluOpType
AF = mybir.ActivationFunctionType
AX = mybir.AxisListType

D = 1024
DFF = 2816
PAST = 4096
DEC = 64
EPS = 1e-6
OFF_BG = 1544
NEG = -30000.0


class Sched:
    NDS = 4

    def __init__(self, nc):
        self.nc = nc
        self.eng = {'pe': nc.tensor, 'act': nc.scalar, 'dve': nc.vector, 'pool': nc.gpsimd, 'sp': nc.sync}
        self.sem = {}
        self.cnt = {}
        for e in ['pe', 'act', 'dve', 'pool']:
            self.sem[e] = nc.alloc_semaphore(name=f"s_{e}")
            self.cnt[e] = 0
        self.dq = ['sp', 'pool', 'act']
        for q in self.dq:
            for j in range(self.NDS):
                self.sem[(q, j)] = nc.alloc_semaphore(name=f"d_{q}{j}")
                self.cnt[(q, j)] = 0
        self.dcount = {q: 0 for q in self.dq}
        self.seen = {e: {} for e in self.eng}
        self.lastw = {}
        self.readers = {}
        self.nins = 0
        self.ev = {e: [] for e in self.eng}

    def simulate(self):
        val = {k: 0 for k in self.sem}
        pc = {e: 0 for e in self.ev}
        prog = True
        while prog:
            prog = False
            for e, lst in self.ev.items():
                while pc[e] < len(lst):
                    kind, s, v, info = lst[pc[e]]
                    if kind == 'wait':
                        if val[s] >= v:
                            pc[e] += 1
                            prog = True
                        else:
                            break
                    else:
                        val[s] += v
                        pc[e] += 1
                        prog = True
        stuck = {e: (pc[e], len(l), l[pc[e]] if pc[e] < len(l) else None) for e, l in self.ev.items()}
        ok = all(pc[e] == len(l) for e, l in self.ev.items())
        print("SIM", "OK" if ok else "DEADLOCK", stuck if not ok else "")
        if not ok:
            print({k: v for k, v in val.items()})
        return ok

    def _deps(self, reads, writes):
        deps = set()
        for k in reads:
            if k in self.lastw:
                deps.add(self.lastw[k])
        for k in writes:
            if k in self.lastw:
                deps.add(self.lastw[k])
            for r in self.readers.get(k, ()):
                deps.add(r)
        return deps

    def _wait(self, e, deps):
        need = {}
        for (s, c) in deps:
            if c > need.get(s, 0):
                need[s] = c
        for s, c in need.items():
            if self.seen[e].get(s, 0) >= c:
                continue
            unit = 16 if isinstance(s, tuple) else 1
            self.eng[e].wait_ge(self.sem[s], c * unit)
            self.ev[e].append(('wait', s, c * unit, None))
            self.seen[e][s] = c

    def _record(self, tok, reads, writes):
        for k in reads:
            lst = self.readers.setdefault(k, [])
            lst.append(tok)
            if len(lst) > 64:
                best = {}
                for (s, c) in lst:
                    if c > best.get(s, 0):
                        best[s] = c
                self.readers[k] = [(s, c) for s, c in best.items()]
        for k in writes:
            self.lastw[k] = tok
            self.readers[k] = []

    def op(self, e, fn, reads=(), writes=(), signal=True):
        deps = self._deps(reads, writes)
        if e == 'pe':
            deps = {d for d in deps if d[0] != 'pe'}
        self._wait(e, deps)
        ins = fn()
        tok = (e, self.cnt[e] + 1)
        if signal:
            ins.then_inc(self.sem[e], 1)
            self.cnt[e] += 1
            self.ev[e].append(('inc', e, 1, self.nins))
        self._record(tok, reads, writes)
        self.nins += 1
        return ins

    def dma(self, q, out, in_, reads=(), writes=(), **kw):
        if q == 'act' and os.environ.get("MK_ACTQ", "act") != "act":
            q = os.environ.get("MK_ACTQ", "act")
        deps = self._deps(reads, writes)
        j = self.dcount[q] % self.NDS
        self.dcount[q] += 1
        s = (q, j)
        if self.cnt[s] > 0:
            deps = set(deps)
            deps.add((s, self.cnt[s]))
        self._wait(q, deps)
        ins = self.eng[q].dma_start(out=out, in_=in_, **kw)
        ins.then_inc(self.sem[s], 16)
        self.cnt[s] += 1
        self.ev[q].append(('inc', s, 16, self.nins))
        tok = (s, self.cnt[s])
        self._record(tok, reads, writes)
        self.nins += 1
        return ins

    def barrier(self):
        deps = set()
        for s in self.cnt:
            if self.cnt[s] > 0:
                deps.add((s, self.cnt[s]))
        for e in ['pe', 'act', 'dve', 'pool', 'sp']:
            self._wait(e, deps)

    def finish(self, e='sp'):
        deps = set()
        for k, t in self.lastw.items():
            deps.add(t)
        for s in self.cnt:
            if self.cnt[s] > 0:
                deps.add((s, self.cnt[s]))
        self._wait(e, deps)


class Stream:
    pass


def build(NT=32, NL=4, do_sample=True):
    nc = bass.Bass("TRN2", target_bir_lowering=False)
    NTOK = NT * 512

    def din(name, shape):
        return nc.dram_tensor(name, list(shape), F32, kind="ExternalInput").ap()

    def dout(name, shape):
        return nc.dram_tensor(name, list(shape), F32, kind="ExternalOutput").ap()

    def dscr(name, shape, dt=BF16):
        return nc.dram_tensor(name, list(shape), dt, kind="Internal").ap()

    I = dict(
        x_prompt=din("x_prompt", [NTOK, D]), x_sample=din("x_sample", [DEC, D]),
        cache_k=din("cache_k", [2, PAST, 512]), cache_v=din("cache_v", [2, PAST, 512]),
        cache_logf=din("cache_logf", [2, PAST, 8]),
        state_conv_b=din("state_conv_b", [4, 512]), state_conv_d=din("state_conv_d", [60, 512]),
        state_conv_ffn=din("state_conv_ffn", [8, 5632]),
        g_mix=din("g_mix", [4, D]), w_in_even=din("w_in_even", [2, D, 3080]), b_f=din("b_f", [2, 8]),
        g_q=din("g_q", [2, 64]), g_k=din("g_k", [2, 64]), conv_b=din("conv_b", [6, 512]),
        w_out_even=din("w_out_even", [2, D, D]), w_in_odd=din("w_in_odd", [2, D, 2048]),
        g_vc=din("g_vc", [2, 512]), w_s=din("w_s", [2, 4, 128, 128]), b_s=din("b_s", [2, 4, 128]),
        conv_d=din("conv_d", [62, 512]), g_d=din("g_d", [2, 512]), w_out_odd=din("w_out_odd", [2, D, D]),
        g_ffn=din("g_ffn", [4, D]), w_up=din("w_up", [4, D, 5632]), conv_ffn=din("conv_ffn", [12, 5632]),
        w_down=din("w_down", [4, DFF, D]),
    )
    O = dict(
        y_p=dout("y_p", [NTOK, D]), y_s=dout("y_s", [DEC, D]),
        p_k=dout("p_k", [2, NTOK, 512]), p_v=dout("p_v", [2, NTOK, 512]), p_lf=dout("p_lf", [2, NTOK, 8]),
        p_cb=dout("p_cb", [4, 512]), p_cd=dout("p_cd", [60, 512]), p_cf=dout("p_cf", [8, 5632]),
        s_k=dout("s_k", [2, DEC, 512]), s_v=dout("s_v", [2, DEC, 512]), s_lf=dout("s_lf", [2, DEC, 8]),
        s_cb=dout("s_cb", [4, 512]), s_vc=dout("s_vc", [2, DEC, 512]), s_cd=dout("s_cd", [60, 512]),
        s_cf=dout("s_cf", [8, 5632]),
    )
    WB = dict(
        wie=dscr("wie_b", [2, D, 3080]), woe=dscr("woe_b", [2, D, D]), wio=dscr("wio_b", [2, D, 2048]),
        woo=dscr("woo_b", [2, D, D]), wup=dscr("wup_b", [4, D, 5632]), wdn=dscr("wdn_b", [4, DFF, D]),
    )
    cq_scr = dscr("cq_scr", [8, 3, 512])

    S = Sched(nc)
    with ExitStack() as es:
        def SB(name, shape, dt=F32):
            return es.enter_context(nc.sbuf_tensor(name, list(shape), dt))

        ps = [es.enter_context(nc.psum_tensor(f"ps{i}", [128, 512], F32)) for i in range(8)]
        rot = {'gen': [0, [0, 1, 2]], 'S': [0, [3, 4]], 'O': [0, [5, 6]]}

        def bank(kind):
            r = rot[kind]
            b = r[1][r[0] % len(r[1])]
            r[0] += 1
            return b

        def P(b):
            return ('ps', b)

        def mm(out, lhsT, rhs, start, stop, reads, writes, sig=None):
            S.op('pe', lambda: nc.tensor.matmul(out, lhsT=lhsT, rhs=rhs, start=start, stop=stop),
                 reads=reads, writes=writes, signal=(stop if sig is None else sig))

        def act(out, in_, func, reads, writes, **kw):
            S.op('act', lambda: nc.scalar.activation(out, in_, func, **kw), reads=reads, writes=writes)

        def V(e, name, *args, reads, writes, **kw):
            eng = nc.vector if e == 'dve' else nc.gpsimd
            S.op(e, lambda: getattr(eng, name)(*args, **kw), reads=reads, writes=writes)

        def fma(e, acc, src, wcol, rkeys, akey):
            V('dve', 'scalar_tensor_tensor', acc, src, wcol, acc, reads=rkeys + [akey], writes=[akey], op0=ALU.mult, op1=ALU.add)

        ones_f = SB("ones_f", [128, 128])
        ident_f = SB("ident_f", [128, 128])
        utri_f = SB("utri_f", [128, 128])
        ident_b = SB("ident_b", [128, 128], BF16)
        ones_b = SB("ones_b", [128, 128], BF16)
        zeros_b = SB("zeros_b", [128, 512], BF16)
        dmask = SB("dmask", [128, 4, 512], BF16)
        ones3 = SB("ones3", [8, 3, 512], BF16)
        V('pool', 'memset', ones_f[:], 1.0, reads=[], writes=['ones_f'])
        V('pool', 'memset', ones_b[:], 1.0, reads=[], writes=['ones_b'])
        V('pool', 'memset', zeros_b[:], 0.0, reads=[], writes=['zeros_b'])
        V('pool', 'memset', ones3[:], 1.0, reads=[], writes=['ones3'])
        S.op('pool', lambda: nc.gpsimd.affine_select(ident_f[:], ones_f[:], [[-1, 128]], ALU.is_equal, 0.0, base=0, channel_multiplier=1),
             reads=['ones_f'], writes=['ident_f'])
        S.op('pool', lambda: nc.gpsimd.affine_select(utri_f[:], ones_f[:], [[1, 128]], ALU.is_ge, 0.0, base=0, channel_multiplier=-1),
             reads=['ones_f'], writes=['utri_f'])
        V('pool', 'tensor_copy', ident_b[:], ident_f[:], reads=['ident_f'], writes=['ident_b'])
        for r in range(4):
            S.op('pool', lambda: nc.gpsimd.affine_select(dmask[:, r, :], zeros_b[:], [[1, 512]], ALU.is_ge, NEG, base=-128 * r, channel_multiplier=-1),
                 reads=['zeros_b'], writes=['dmask'])

        cx = SB("cx", [128, 4, 512])
        ob = SB("ob", [128, 4, 512], BF16)
        castf = cx[:, :, :].rearrange("p (a b) c -> p a (b c)", a=2)
        castb = ob[:, :, :].rearrange("p (a b) c -> p a (b c)", a=2)
        ci = [0]

        def cast_weight(src, dst, rows, cols):
            for r0 in range(0, rows, 128):
                for c0 in range(0, cols, 1024):
                    cw = min(1024, cols - c0)
                    b = ci[0] % 2
                    e = ['dve', 'pool'][ci[0] % 2]
                    ci[0] += 1
                    S.dma('sp', castf[:, b, :cw], src[r0:r0 + 128, c0:c0 + cw], reads=[], writes=[('castf', b)])
                    V(e, 'tensor_copy', castb[:, b, :cw], castf[:, b, :cw], reads=[('castf', b)], writes=[('castb', b)])
                    S.dma('act', dst[r0:r0 + 128, c0:c0 + cw], castb[:, b, :cw], reads=[('castb', b)], writes=['WB'])

        for i2 in range(2):
            cast_weight(I['w_in_even'][i2], WB['wie'][i2], D, 3080)
            cast_weight(I['w_out_even'][i2], WB['woe'][i2], D, D)
            cast_weight(I['w_in_odd'][i2], WB['wio'][i2], D, 2048)
            cast_weight(I['w_out_odd'][i2], WB['woo'][i2], D, D)
        for l in range(4):
            cast_weight(I['w_up'][l], WB['wup'][l], D, 5632)
            cast_weight(I['w_down'][l], WB['wdn'][l], DFF, D)

        S.barrier()
        xstage = SB("xstage", [128, 1024])
        stage = xstage

        def load_T(src, R, C, dst3, key):
            ncn = C // 128
            per = min(512 // R, 8)
            for c0 in range(0, ncn, per):
                n = min(per, ncn - c0)
                S.dma('sp', stage[:R, :n * 128], src[:, c0 * 128:(c0 + n) * 128], reads=[], writes=['xstage'])
                b = bank('gen')
                for c in range(n):
                    mm(ps[b][:, c * R:(c + 1) * R], stage[:R, c * 128:(c + 1) * 128], ident_f[:R, :R], True, True,
                       ['xstage', 'ident_f'], [P(b)])
                act(dst3[:, c0:c0 + n, :], ps[b][:, :n * R].rearrange("p (c r) -> p c r", r=R), AF.Copy, [P(b)], [key])

        gmix = SB("gmix", [128, 8, 4]); load_T(I['g_mix'], 4, D, gmix, 'gmix')
        gffn = SB("gffn", [128, 8, 4]); load_T(I['g_ffn'], 4, D, gffn, 'gffn')
        gd = SB("gd", [128, 4, 2]); load_T(I['g_d'], 2, 512, gd, 'gd')
        wB = SB("wB", [128, 4, 6]); load_T(I['conv_b'], 6, 512, wB, 'wB')
        wD = SB("wD", [128, 4, 62]); load_T(I['conv_d'], 62, 512, wD, 'wD')
        wF = SB("wF", [128, 44, 12]); load_T(I['conv_ffn'], 12, 5632, wF, 'wF')

        gq_t = SB("gq_t", [128, 2, 64]); gk_t = SB("gk_t", [128, 2, 64])
        gq_bc = SB("gq_bc", [128, 512]); gk_bc = SB("gk_bc", [128, 512])
        gvc_bc = SB("gvc_bc", [128, 512]); bs_bc = SB("bs_bc", [128, 2, 4, 128]); bf_bc = SB("bf_bc", [128, 2, 8])
        for i2 in range(2):
            S.dma('sp', gq_t[:, i2, :], I['g_q'][i2].partition_broadcast(128), reads=[], writes=['gq_t'])
            S.dma('sp', gk_t[:, i2, :], I['g_k'][i2].partition_broadcast(128), reads=[], writes=['gk_t'])
            S.dma('sp', bf_bc[:, i2, :], I['b_f'][i2].partition_broadcast(128), reads=[], writes=['bf_bc'])
            for g in range(4):
                S.dma('sp', bs_bc[:, i2, g, :], I['b_s'][i2, g].partition_broadcast(128), reads=[], writes=['bs_bc'])

        def load_gqk(i2):
            V('dve', 'tensor_scalar', gq_bc[:, :].rearrange("p (h d) -> p h d", h=8), gq_t[:, i2, :].unsqueeze(1).to_broadcast([128, 8, 64]),
              0.125, None, reads=['gq_t'], writes=['gq_bc'], op0=ALU.mult)
            V('dve', 'tensor_scalar', gk_bc[:, :].rearrange("p (h d) -> p h d", h=8), gk_t[:, i2, :].unsqueeze(1).to_broadcast([128, 8, 64]),
              1.0, None, reads=['gk_t'], writes=['gk_bc'], op0=ALU.mult)
        wf_f = SB("wf_f", [128, 2, 8, 8]); wf_b = SB("wf_b", [128, 2, 8, 8], BF16)
        for i2 in range(2):
            S.dma('sp', wf_f[:, i2, :, :], I['w_in_even'][i2].rearrange("(kc p) n -> p kc n", p=128)[:, :, 1536:1544], reads=[], writes=['wf_f'])
        V('dve', 'tensor_copy', wf_b[:], wf_f[:], reads=['wf_f'], writes=['wf_b'])
        wsT = SB("wsT", [128, 2, 4, 128], BF16)
        wsf = SB("wsf", [128, 128])
        for i2 in range(2):
            for g in range(4):
                S.dma('sp', wsf[:], I['w_s'][i2, g], reads=[], writes=['wsf'])
                S.op('pool', lambda: nc.gpsimd.affine_select(wsf[:], wsf[:], [[-1, 128]], ALU.is_ge, 0.0, base=0, channel_multiplier=1),
                     reads=['wsf'], writes=['wsf'])
                b = bank('gen')
                mm(ps[b][:, :128], wsf[:], ident_f[:], True, True, ['wsf', 'ident_f'], [P(b)])
                act(wsT[:, i2, g, :], ps[b][:, :128], AF.Copy, [P(b)], ['wsT'])

        NWB = 4
        wbuf = [SB(f"wbuf{i}", [128, 4096], BF16) for i in range(NWB)]
        wrr = [0]

        def wreq(src3, npart, a, b):
            i = wrr[0] % NWB
            wrr[0] += 1
            view = wbuf[i][:npart, :a * b].rearrange("p (a b) -> p a b", a=a)
            S.dma('sp', view, src3, reads=['WB'], writes=[('wbuf', i)])
            return view, ('wbuf', i)

        xT = SB("xT", [128, 8, 512])
        xn = SB("xn", [128, 8, 512], BF16)
        sq = SB("sq", [128, 2, 512], BF16)
        rstd = SB("rstd", [128, 512])
        gbuf = SB("gbuf", [128, 11, 512], BF16)
        hb = [SB(f"hb{i}", [128, 514]) for i in range(2)]
        ca = [SB(f"ca{i}", [128, 512]) for i in range(2)]
        print("SBUF remaining after hb/ca:", nc.sbuf_bytes_remaining)
        sgt = SB("sgt", [128, 512])
        tmpf = SB("tmpf", [128, 512]); sqh = tmpf; ssq = SB("ssq", [128, 8]); qf = SB("qf", [128, 512]); kf = SB("kf", [128, 512])
        vf = SB("vf", [128, 512]); qb = SB("qb", [128, 512], BF16); kb = SB("kb", [128, 512], BF16)
        lz = SB("lz", [128, 8]); logf = SB("logf", [128, 4, 8])
        vb = SB("vb", [128, 1, 8, 128], BF16)
        V('pool', 'memset', vb[:], 0.0, reads=[], writes=['vb'])
        V('pool', 'memset', vb[:, :, :, 64:65], 1.0, reads=[], writes=['vb'])
        qT = SB("qT", [128, 8, 512], BF16)
        V('pool', 'memset', qT[:], 1.0, reads=[], writes=['qT'])
        kT = SB("kT", [64, 8, 128], BF16)
        cc = SB("cc", [8, 512]); cr = SB("cr", [8, 512]); caug = SB("caug", [8, 3, 512], BF16); ncaug = SB("ncaug", [8, 3, 512], BF16)
        KB = 1024
        kbuf = [SB(f"kbuf{i}", [128, KB + 512], BF16) for i in range(2)]
        vbuf = [SB(f"vbuf{i}", [128, KB // 128 + 4, 128], BF16) for i in range(2)]
        for i in range(2):
            V('pool', 'memset', kbuf[i][:], 0.0, reads=[], writes=[('kbuf', i)])
        pT = [SB(f"pT{i}", [128, 512], BF16) for i in range(3)]
        osb = SB("osb", [65, 512]); rec = osb; bcs = SB("bcs", [64, 512])
        oa = SB("oa", [64, 8, 512], BF16)
        ub = SB("ub", [128, 4, 544])
        tmpg = SB("tmpg", [128, 512])
        oc = SB("oc", [128, 4, 512], BF16)
        ug = SB("ug", [128, 4, 512], BF16)
        zf = qf; vcb = kb
        rs1 = SB("rs1", [128, 8])
        ostage = tmpg

        kvrr = [0]
        ptrr = [0]

        def make_stream(sid, T, ntok_scr):
            st = Stream()
            st.sid = sid
            st.T = T
            st.TS = min(T, 128)
            st.NS = T // st.TS
            st.KT = [dscr(f"KT_{sid}_{i}", [8, 70, ntok_scr]) for i in range(2)]
            st.Vs = [dscr(f"V_{sid}_{i}", [8, 128, ntok_scr // 128, 128]) for i in range(2)]
            st.histF = SB(f"histF_{sid}", [128, 44, 8])
            st.histB = SB(f"histB_{sid}", [128, 4, 4])
            st.histD = SB(f"histD_{sid}", [128, 4, 60])
            st.carry = SB(f"carry_{sid}", [8, 2])
            st.kF, st.kB, st.kD, st.kC = f"histF_{sid}", f"histB_{sid}", f"histD_{sid}", f"carry_{sid}"
            st.o0 = 0
            return st

        def rmsnorm_x(st, gt, l):
            T = st.T
            for kc in range(8):
                b = kc % 2
                act(sq[:, b, :T], xT[:, kc, :T], AF.Square, ['xT'], [('sq', b)])
                mm(ps[7][:, :T], ones_b[:], sq[:, b, :T], kc == 0, kc == 7, [('sq', b), 'ones_b'], [P(7)], sig=True)
            V('dve', 'tensor_scalar', rstd[:, :T], ps[7][:, :T], 1.0 / D, EPS, reads=[P(7)], writes=['rstd'], op0=ALU.mult, op1=ALU.add)
            act(rstd[:, :T], rstd[:, :T], AF.Sqrt, ['rstd'], ['rstd'])
            V('dve', 'reciprocal', rstd[:, :T], rstd[:, :T], reads=['rstd'], writes=['rstd'])
            for kc in range(8):
                V('dve', 'scalar_tensor_tensor', xn[:, kc, :T], xT[:, kc, :T], gt[:, kc, l:l + 1], rstd[:, :T],
                  reads=['xT', 'rstd'], writes=['xn'], op0=ALU.mult, op1=ALU.mult)

        def head_norm(st, b, gbc, outf, okey, fin=None, finkey=None):
            TS = st.TS
            act(sqh[:TS, :], ps[b][:TS, :], AF.Square, [P(b)], ['tmpf'])
            V('dve', 'tensor_reduce', ssq[:TS, :], sqh[:TS, :].rearrange("p (h d) -> p h d", h=8), AX.X, ALU.add, reads=['tmpf'], writes=['ssq'])
            V('dve', 'tensor_scalar', ssq[:TS, :], ssq[:TS, :], 1.0 / 64, EPS, reads=['ssq'], writes=['ssq'], op0=ALU.mult, op1=ALU.add)
            act(ssq[:TS, :], ssq[:TS, :], AF.Sqrt, ['ssq'], ['ssq'])
            V('dve', 'reciprocal', ssq[:TS, :], ssq[:TS, :], reads=['ssq'], writes=['ssq'])
            V('dve', 'tensor_tensor', outf[:TS, :].rearrange("p (h d) -> p h d", h=8), ps[b][:TS, :].rearrange("p (h d) -> p h d", h=8),
              ssq[:TS, :].unsqueeze(2).to_broadcast([TS, 8, 64]), ALU.mult, reads=[P(b), 'ssq'], writes=[okey])
            if fin is None:
                fin, finkey = outf, okey
            V('dve', 'tensor_tensor', fin[:TS, :], outf[:TS, :], gbc[:TS, :], ALU.mult, reads=[okey, 'gq_bc', 'gk_bc'], writes=[finkey])

        def transp_heads(st, src_b, dstT, s, key_src, key_dst):
            TS = st.TS
            for h0 in (0, 4):
                b = bank('gen')
                for hh in range(4):
                    h = h0 + hh
                    mm(ps[b][:64, hh * TS:(hh + 1) * TS], src_b[:TS, h * 64:(h + 1) * 64], ident_b[:TS, :TS], True, True,
                       [key_src, 'ident_b'], [P(b)])
                act(dstT[:64, h0:h0 + 4, s * TS:(s + 1) * TS], ps[b][:64, :4 * TS].rearrange("p (h t) -> p h t", h=4), AF.Copy, [P(b)], [key_dst])

        def kv_finish(st, i2, t0):
            T, TS, NS = st.T, st.TS, st.NS
            for s in range(NS):
                mm(ps[7][:8, s * TS:(s + 1) * TS], logf[:TS, s, :], utri_f[:TS, :TS], True, True, ['logf', 'utri_f'], [P(7)])
            for s in range(NS):
                V('dve', 'tensor_scalar', cc[:8, s * TS:(s + 1) * TS], ps[7][:8, s * TS:(s + 1) * TS], st.carry[:8, i2:i2 + 1], None,
                  reads=[P(7), st.kC], writes=['cc'], op0=ALU.add)
                V('dve', 'tensor_copy', st.carry[:8, i2:i2 + 1], cc[:8, (s + 1) * TS - 1:(s + 1) * TS], reads=['cc'], writes=[st.kC])
            V('dve', 'tensor_copy', caug[:, 0, :T], cc[:, :T], reads=['cc'], writes=['caug'])
            V('dve', 'tensor_tensor', cr[:, :T], cc[:, :T], caug[:, 0, :T], ALU.subtract, reads=['cc', 'caug'], writes=['cr'])
            V('dve', 'tensor_copy', caug[:, 1, :T], cr[:, :T], reads=['cr'], writes=['caug'])
            V('dve', 'tensor_tensor', cr[:, :T], cr[:, :T], caug[:, 1, :T], ALU.subtract, reads=['cr', 'caug'], writes=['cr'])
            V('dve', 'tensor_copy', caug[:, 2, :T], cr[:, :T], reads=['cr'], writes=['caug'])
            V('dve', 'tensor_scalar', ncaug[:, :, :T], caug[:, :, :T], -1.0, None, reads=['caug'], writes=['ncaug'], op0=ALU.mult)
            kk = ('KV', st.sid, i2)
            S.dma('act', st.KT[i2][:, 64:67, t0:t0 + T], ones3[:, :, :T], reads=['ones3'], writes=[kk])
            S.dma('act', st.KT[i2][:, 67:70, t0:t0 + T], ncaug[:, :, :T], reads=['ncaug'], writes=[kk])

        def attention(st, i2, t0):
            T, TS, NS = st.T, st.TS, st.NS
            kk = ('KV', st.sid, i2)
            S.dma('act', cq_scr[:, :, :T], caug[:, :, :T], reads=['caug'], writes=['cq_scr'])
            S.dma('act', qT[64:67, :, :T], cq_scr[:, :, :T].rearrange("h a t -> a h t"), reads=['cq_scr'], writes=['qT'])
            ntot = t0 + T
            chunks = []
            c0 = 0
            while c0 < ntot:
                c1 = min(c0 + KB, ntot)
                if ntot - c1 <= 512 and ntot - c1 > 0:
                    c1 = ntot
                chunks.append((c0, c1))
                c0 = c1
            blocks = []
            for ci, (c0, c1) in enumerate(chunks):
                col = 0
                while col < c1 - c0:
                    gpos = c0 + col
                    if gpos < t0:
                        ksz, r = 128, None
                    else:
                        ksz, r = TS, (gpos - t0) // TS
                    blocks.append((ci, col, ksz, r))
                    col += ksz
            nb = len(blocks)
            for h in range(8):
                obk = bank('O')
                loaded = {}

                def ensure(ci):
                    if ci not in loaded:
                        i = kvrr[0] % 2
                        kvrr[0] += 1
                        c0, c1 = chunks[ci]
                        n = c1 - c0
                        nkt = (n + 127) // 128
                        S.dma('sp', kbuf[i][:70, :n], st.KT[i2][h, :, c0:c1], reads=[kk], writes=[('kbuf', i)])
                        S.dma('sp', vbuf[i][:, :nkt, :], st.Vs[i2][h, :, c0 // 128:c0 // 128 + nkt, :], reads=[kk], writes=[('vbuf', i)])
                        loaded[ci] = i
                    return loaded[ci]

                def emitS(bi):
                    ci, col, ksz, r = blocks[bi]
                    i = ensure(ci)
                    sb_ = bank('S')
                    mm(ps[sb_][:ksz, :T], kbuf[i][:, col:col + ksz], qT[:, h, :T], True, r is None, [('kbuf', i), 'qT'], [P(sb_)])
                    if r is not None:
                        mm(ps[sb_][:ksz, :T], ident_b[:ksz, :ksz], dmask[:ksz, r, :T], False, True, ['ident_b', 'dmask'], [P(sb_)])
                    return sb_

                sbk = {0: emitS(0)}
                for bi in range(nb):
                    if bi + 1 < nb:
                        sbk[bi + 1] = emitS(bi + 1)
                    ci, col, ksz, r = blocks[bi]
                    i = loaded[ci]
                    sb_ = sbk.pop(bi)
                    pi = ptrr[0] % 3
                    ptrr[0] += 1
                    act(pT[pi][:ksz, :T], ps[sb_][:ksz, :T], AF.Exp, [P(sb_)], [('pT', pi)])
                    mm(ps[obk][:, :T], vbuf[i][:ksz, col // 128, :], pT[pi][:ksz, :T], bi == 0, bi == nb - 1, [('vbuf', i), ('pT', pi)], [P(obk)])
                act(osb[:65, :T], ps[obk][:65, :T], AF.Copy, [P(obk)], ['osb'])
                V('dve', 'reciprocal', osb[64:65, :T], osb[64:65, :T], reads=['osb'], writes=['osb'])
                mm(ps[7][:64, :T], ones_f[64:65, :64], osb[64:65, :T], True, True, ['osb', 'ones_f'], [P(7)])
                act(bcs[:64, :T], ps[7][:64, :T], AF.Copy, [P(7)], ['bcs'])
                V('dve', 'tensor_tensor', oa[:64, h, :T], osb[:64, :T], bcs[:64, :T], ALU.mult, reads=['osb', 'bcs'], writes=['oa'])

        def residual_add(st, b, dc):
            T = st.T
            V('dve', 'tensor_tensor', xT[:, dc, :T], xT[:, dc, :T], ps[b][:, :T], ALU.add, reads=['xT', P(b)], writes=['xT'])

        def even_layer(st, l, t0, O_k, O_v, O_lf):
            T, TS, NS = st.T, st.TS, st.NS
            i2 = l // 2
            wie3 = WB['wie'][i2].rearrange("(kc p) n -> p kc n", p=128)
            rmsnorm_x(st, gmix, l)
            load_gqk(i2)
            kk = ('KV', st.sid, i2)
            kt0 = t0 // 128
            o0 = st.o0
            Wq, kq = wreq(wie3[:, :, 0:512], 128, 8, 512)
            Wk, kk_ = wreq(wie3[:, :, 512:1024], 128, 8, 512)
            Wv, kv_ = wreq(wie3[:, :, 1024:1536], 128, 8, 512)
            for s in range(NS):
                ts = slice(s * TS, (s + 1) * TS)
                b = bank('gen')
                for kc in range(8):
                    mm(ps[b][:TS, :], xn[:, kc, ts], Wq[:, kc, :], kc == 0, kc == 7, ['xn', kq], [P(b)])
                head_norm(st, b, gq_bc, qf, 'qf', qb, 'qb')
                transp_heads(st, qb, qT, s, 'qb', 'qT')
                if DBG < 0.5:
                    continue
                b = bank('gen')
                for kc in range(8):
                    mm(ps[b][:TS, :], xn[:, kc, ts], Wk[:, kc, :], kc == 0, kc == 7, ['xn', kk_], [P(b)])
                head_norm(st, b, gk_bc, kf, 'kf')
                S.dma('act', O_k[i2, o0 + s * TS:o0 + (s + 1) * TS, :], kf[:TS, :], reads=['kf'], writes=['O_k'])
                act(kb[:TS, :], kf[:TS, :], AF.Copy, ['kf'], ['kb'])
                transp_heads(st, kb, kT, 0, 'kb', 'kT')
                S.dma('act', st.KT[i2][:, 0:64, t0 + s * TS:t0 + (s + 1) * TS].rearrange("h d t -> d h t"), kT[:64, :, :TS], reads=['kT'], writes=[kk])
                if DBG < 0.7:
                    continue
                b = bank('gen')
                for kc in range(8):
                    mm(ps[b][:TS, :], xn[:, kc, ts], Wv[:, kc, :], kc == 0, kc == 7, ['xn', kv_], [P(b)])
                act(vf[:TS, :], ps[b][:TS, :], AF.Copy, [P(b)], ['vf'])
                V('dve', 'tensor_copy', vb[:TS, 0, :, 0:64], vf[:TS, :].rearrange("p (h d) -> p h d", h=8), reads=['vf'], writes=['vb'])
                S.dma('act', O_v[i2, o0 + s * TS:o0 + (s + 1) * TS, :], vf[:TS, :], reads=['vf'], writes=['O_v'])
                S.dma('act', st.Vs[i2][:, 0:TS, kt0 + s, :].rearrange("h p d -> p h d"), vb[:TS, 0, :, :], reads=['vb'], writes=[kk])
                if DBG < 0.9:
                    continue
                for kc in range(8):
                    mm(ps[7][:TS, 0:8], xn[:, kc, ts], wf_b[:, i2, kc, :], kc == 0, kc == 7, ['xn', 'wf_b'], [P(7)])
                V('dve', 'tensor_tensor', lz[:TS, :], ps[7][:TS, 0:8], bf_bc[:TS, i2, :], ALU.add, reads=[P(7), 'bf_bc'], writes=['lz'])
                act(lz[:TS, :], lz[:TS, :], AF.Exp, ['lz'], ['lz'], scale=-1.0)
                act(lz[:TS, :], lz[:TS, :], AF.Ln, ['lz'], ['lz'], bias=1.0)
                V('dve', 'tensor_scalar', logf[:TS, s, :], lz[:TS, :], -1.0, None, reads=['lz'], writes=['logf'], op0=ALU.mult)
                S.dma('act', O_lf[i2, o0 + s * TS:o0 + (s + 1) * TS, :], logf[:TS, s, :], reads=['logf'], writes=['O_lf'])
            if DBG < 2:
                return
            kv_finish(st, i2, t0)
            if DBG < 3:
                return
            attention(st, i2, t0)
            if DBG < 4:
                return
            Wbg, kbg = wreq(wie3[:, :, OFF_BG:OFF_BG + 512], 128, 8, 512)
            Wcg, kcg = wreq(wie3[:, :, OFF_BG + 512:OFF_BG + 1024], 128, 8, 512)
            Wxi, kxi = wreq(wie3[:, :, OFF_BG + 1024:OFF_BG + 1536], 128, 8, 512)
            V('dve', 'tensor_copy', ub[:, :, 0:2], st.histB[:, :, i2 * 2:i2 * 2 + 2], reads=[st.kB], writes=['ub'])
            for c in range(4):
                cs = slice(c * 128, (c + 1) * 128)
                b = bank('gen')
                for kc in range(8):
                    mm(ps[b][:, :T], Wcg[:, kc, cs], xn[:, kc, :T], kc == 0, kc == 7, ['xn', kcg], [P(b)])
                act(tmpf[:, :T], ps[b][:, :T], AF.Copy, [P(b)], ['tmpf'])
                b = bank('gen')
                for kc in range(8):
                    mm(ps[b][:, :T], Wxi[:, kc, cs], xn[:, kc, :T], kc == 0, kc == 7, ['xn', kxi], [P(b)])
                V('dve', 'tensor_tensor', ub[:, c, 2:2 + T], tmpf[:, :T], ps[b][:, :T], ALU.mult, reads=['tmpf', P(b)], writes=['ub'])
                V('dve', 'tensor_scalar', cx[:, c, :T], ub[:, c, 2:2 + T], wB[:, c, i2 * 3 + 2:i2 * 3 + 3], None, reads=['ub'], writes=[('cx', c)], op0=ALU.mult)
                for j in (1, 0):
                    fma('dve', cx[:, c, :T], ub[:, c, j:j + T], wB[:, c, i2 * 3 + j:i2 * 3 + j + 1], ['ub'], ('cx', c))
                b = bank('gen')
                for kc in range(8):
                    mm(ps[b][:, :T], Wbg[:, kc, cs], xn[:, kc, :T], kc == 0, kc == 7, ['xn', kbg], [P(b)])
                V('dve', 'tensor_tensor', ob[:, c, :T], ps[b][:, :T], cx[:, c, :T], ALU.mult, reads=[P(b), ('cx', c)], writes=['ob'])
            V('dve', 'tensor_copy', st.histB[:, :, i2 * 2:i2 * 2 + 2], ub[:, :, T:T + 2], reads=['ub'], writes=[st.kB])
            if DBG < 5:
                return
            woA = WB['woe'][i2][0:512, :].rearrange("(h d) n -> d h n", d=64)
            woB = WB['woe'][i2][512:1024, :].rearrange("(c p) n -> p c n", p=128)
            WA0, kA0 = wreq(woA[:, :, 0:512], 64, 8, 512)
            WA1, kA1 = wreq(woA[:, :, 512:1024], 64, 8, 512)
            WBo, kBo = wreq(woB, 128, 4, 1024)
            for dc in range(8):
                b = bank('gen')
                WA, kA = (WA0, kA0) if dc < 4 else (WA1, kA1)
                dsl = slice((dc % 4) * 128, (dc % 4 + 1) * 128)
                for h in range(8):
                    mm(ps[b][:, :T], WA[:64, h, dsl], oa[:64, h, :T], h == 0, False, ['oa', kA], [P(b)])
                for c in range(4):
                    mm(ps[b][:, :T], WBo[:, c, dc * 128:(dc + 1) * 128], ob[:, c, :T], False, c == 3, ['ob', kBo], [P(b)])
                residual_add(st, b, dc)

        def gelu_from_psum(b, npart, T, outf, key):
            act(tmpf[:npart, :T], ps[b][:npart, :T], AF.Square, [P(b)], ['tmpf'])
            V('dve', 'tensor_scalar', tmpf[:npart, :T], tmpf[:npart, :T], 0.044715, 1.0, reads=['tmpf'], writes=['tmpf'], op0=ALU.mult, op1=ALU.add)
            V('dve', 'tensor_tensor', tmpf[:npart, :T], tmpf[:npart, :T], ps[b][:npart, :T], ALU.mult, reads=['tmpf', P(b)], writes=['tmpf'])
            act(tmpf[:npart, :T], tmpf[:npart, :T], AF.Sigmoid, ['tmpf'], ['tmpf'], scale=1.5957691216057308)
            V('dve', 'tensor_tensor', outf, tmpf[:npart, :T], ps[b][:npart, :T], ALU.mult, reads=['tmpf', P(b)], writes=[key])

        def odd_layer(st, l, t0, O_vc):
            T, TS, NS = st.T, st.TS, st.NS
            i2 = l // 2
            wio3 = WB['wio'][i2].rearrange("(kc p) n -> p kc n", p=128)
            rmsnorm_x(st, gmix, l)
            S.dma('sp', gvc_bc[:, :], I['g_vc'][i2].partition_broadcast(128), reads=[], writes=['gvc_bc'])
            o0 = st.o0
            Wad, kad = wreq(wio3[:, :, 1024:1536], 128, 8, 512)
            Wgd, kgd = wreq(wio3[:, :, 1536:2048], 128, 8, 512)
            V('dve', 'tensor_copy', ub[:, :, 0:30], st.histD[:, :, i2 * 30:i2 * 30 + 30], reads=[st.kD], writes=['ub'])
            taps = []
            for c in range(4):
                cs = slice(c * 128, (c + 1) * 128)
                b = bank('gen')
                for kc in range(8):
                    mm(ps[b][:, :T], Wad[:, kc, cs], xn[:, kc, :T], kc == 0, kc == 7, ['xn', kad], [P(b)])
                act(tmpf[:, :T], ps[b][:, :T], AF.Copy, [P(b)], ['tmpf'])
                b = bank('gen')
                for kc in range(8):
                    mm(ps[b][:, :T], Wgd[:, kc, cs], xn[:, kc, :T], kc == 0, kc == 7, ['xn', kgd], [P(b)])
                act(tmpg[:, :T], ps[b][:, :T], AF.Sigmoid, [P(b)], ['tmpg'])
                V('dve', 'tensor_tensor', ub[:, c, 30:30 + T], tmpf[:, :T], tmpg[:, :T], ALU.mult, reads=['tmpf', 'tmpg'], writes=['ub'])

                def first_tap(c=c):
                    V('dve', 'tensor_scalar', cx[:, c, :T], ub[:, c, 30:30 + T], wD[:, c, i2 * 31 + 30:i2 * 31 + 31], None, reads=['ub'], writes=[('cx', c)], op0=ALU.mult)
                taps.append(first_tap)
                for j in range(30):
                    def tap(c=c, j=j):
                        fma('dve', cx[:, c, :T], ub[:, c, j:j + T], wD[:, c, i2 * 31 + j:i2 * 31 + j + 1], ['ub'], ('cx', c))
                    taps.append(tap)
            tpos = [0]

            def emit_taps(n):
                for _ in range(n):
                    if tpos[0] < len(taps):
                        taps[tpos[0]]()
                        tpos[0] += 1

            per = (len(taps) + 4 + NS - 1) // (4 + NS)
            Wu, ku = wreq(wio3[:, :, 0:512], 128, 8, 512)
            Wvc, kvc = wreq(wio3[:, :, 512:1024], 128, 8, 512)
            for c in range(4):
                b = bank('gen')
                for kc in range(8):
                    mm(ps[b][:, :T], Wu[:, kc, c * 128:(c + 1) * 128], xn[:, kc, :T], kc == 0, kc == 7, ['xn', ku], [P(b)])
                gelu_from_psum(b, 128, T, ug[:, c, :T], 'ug')
                emit_taps(per)
            for s in range(NS):
                ts = slice(s * TS, (s + 1) * TS)
                b = bank('gen')
                for kc in range(8):
                    mm(ps[b][:TS, :], xn[:, kc, ts], Wvc[:, kc, :], kc == 0, kc == 7, ['xn', kvc], [P(b)])
                gelu_from_psum(b, TS, 512, zf[:TS, :], 'qf')
                act(sqh[:TS, :], zf[:TS, :], AF.Square, ['qf'], ['tmpf'])
                V('dve', 'tensor_reduce', rs1[:TS, 0:1], sqh[:TS, :], AX.X, ALU.add, reads=['tmpf'], writes=['rs1'])
                V('dve', 'tensor_scalar', rs1[:TS, 0:1], rs1[:TS, 0:1], 1.0 / 512, EPS, reads=['rs1'], writes=['rs1'], op0=ALU.mult, op1=ALU.add)
                act(rs1[:TS, 0:1], rs1[:TS, 0:1], AF.Sqrt, ['rs1'], ['rs1'])
                V('dve', 'reciprocal', rs1[:TS, 0:1], rs1[:TS, 0:1], reads=['rs1'], writes=['rs1'])
                V('dve', 'scalar_tensor_tensor', zf[:TS, :], zf[:TS, :], rs1[:TS, 0:1], gvc_bc[:TS, :], reads=['qf', 'rs1', 'gvc_bc'], writes=['qf'],
                  op0=ALU.mult, op1=ALU.mult)
                if O_vc is not None:
                    S.dma('act', O_vc[i2, o0 + s * TS:o0 + (s + 1) * TS, :], zf[:TS, :], reads=['qf'], writes=['O_vc'])
                act(vcb[:TS, :], zf[:TS, :], AF.Copy, ['qf'], ['kb'])
                b = bank('gen')
                for g in range(4):
                    mm(ps[b][:, g * TS:(g + 1) * TS], vcb[:TS, g * 128:(g + 1) * 128], wsT[:TS, i2, g, :TS], True, True, ['kb', 'wsT'], [P(b)])
                V('dve', 'tensor_tensor', tmpg[:, :4 * TS].rearrange("p (g t) -> p g t", g=4), ps[b][:, :4 * TS].rearrange("p (g t) -> p g t", g=4),
                  bs_bc[:, i2, :, :TS], ALU.add, reads=[P(b), 'bs_bc'], writes=['tmpg'])
                V('dve', 'tensor_tensor', oc[:, :, ts], tmpg[:, :4 * TS].rearrange("p (g t) -> p g t", g=4), ug[:, :, ts], ALU.mult,
                  reads=['tmpg', 'ug'], writes=['oc'])
                emit_taps(per)
            emit_taps(len(taps))
            V('dve', 'tensor_copy', st.histD[:, :, i2 * 30:i2 * 30 + 30], ub[:, :, T:T + 30], reads=['ub'], writes=[st.kD])
            for c in range(4):
                b2 = c % 2
                act(sq[:, b2, :T], cx[:, c, :T], AF.Square, [('cx', c)], [('sq', b2)])
                mm(ps[7][:, :T], ones_b[:], sq[:, b2, :T], c == 0, c == 3, [('sq', b2), 'ones_b'], [P(7)], sig=True)
            V('dve', 'tensor_scalar', rstd[:, :T], ps[7][:, :T], 1.0 / 512, EPS, reads=[P(7)], writes=['rstd'], op0=ALU.mult, op1=ALU.add)
            act(rstd[:, :T], rstd[:, :T], AF.Sqrt, ['rstd'], ['rstd'])
            V('dve', 'reciprocal', rstd[:, :T], rstd[:, :T], reads=['rstd'], writes=['rstd'])
            for c in range(4):
                V('dve', 'scalar_tensor_tensor', tmpf[:, :T], cx[:, c, :T], gd[:, c, i2:i2 + 1], rstd[:, :T], reads=[('cx', c), 'rstd'], writes=['tmpf'],
                  op0=ALU.mult, op1=ALU.mult)
                act(ob[:, c, :T], tmpf[:, :T], AF.Silu, ['tmpf'], ['ob'])
            woo3 = WB['woo'][i2].rearrange("(c p) n -> p c n", p=128)
            W0, k0 = wreq(woo3[:, 0:4, :], 128, 4, 1024)
            W1, k1 = wreq(woo3[:, 4:8, :], 128, 4, 1024)
            for dc in range(8):
                b = bank('gen')
                for c in range(4):
                    mm(ps[b][:, :T], W0[:, c, dc * 128:(dc + 1) * 128], oc[:, c, :T], c == 0, False, ['oc', k0], [P(b)])
                for c in range(4):
                    mm(ps[b][:, :T], W1[:, c, dc * 128:(dc + 1) * 128], ob[:, c, :T], False, c == 3, ['ob', k1], [P(b)])
                residual_add(st, b, dc)

        def conv3_ffn(st, l, j, b, hbi, e, outap, outkey):
            T = st.T
            h = hb[hbi]
            hk = ('hb', hbi)
            hh = ('hbh', hbi)
            V('dve', 'tensor_copy', h[:, 0:2], st.histF[:, j, l * 2:l * 2 + 2], reads=[st.kF], writes=[hh])
            act(h[:, 2:2 + T], ps[b][:, :T], AF.Copy, [P(b)], [hk])
            act(outap, ps[b][:, :T], AF.Copy, [P(b)], [outkey], scale=wF[:, j, l * 3 + 2:l * 3 + 3])
            for jj in (1, 0):
                fma('dve', outap, h[:, jj:jj + T], wF[:, j, l * 3 + jj:l * 3 + jj + 1], [hk, hh], outkey)
            act(st.histF[:, j, l * 2:l * 2 + 2], h[:, T:T + 2], AF.Copy, [hk], [st.kF])

        def ffn(st, l):
            T = st.T
            rmsnorm_x(st, gffn, l)
            wup3 = WB['wup'][l].rearrange("(kc p) n -> p kc n", p=128)
            wdn3 = WB['wdn'][l].rearrange("(fc p) n -> p fc n", p=128)
            for half in range(2):
                f0 = half * 11
                groups = [(f0, 4), (f0 + 4, 4), (f0 + 8, 3)]
                for (g0, nch) in groups:
                    Wg, kg = wreq(wup3[:, :, g0 * 128:(g0 + nch) * 128], 128, 8, nch * 128)
                    Wv2, kv2 = wreq(wup3[:, :, DFF + g0 * 128:DFF + (g0 + nch) * 128], 128, 8, nch * 128)
                    for jj in range(nch):
                        j = g0 + jj
                        cs = slice(jj * 128, (jj + 1) * 128)
                        bg_ = bank('gen')
                        for kc in range(8):
                            mm(ps[bg_][:, :T], Wg[:, kc, cs], xn[:, kc, :T], kc == 0, kc == 7, ['xn', kg], [P(bg_)])
                        bv_ = bank('gen')
                        for kc in range(8):
                            mm(ps[bv_][:, :T], Wv2[:, kc, cs], xn[:, kc, :T], kc == 0, kc == 7, ['xn', kv2], [P(bv_)])
                        conv3_ffn(st, l, j, bg_, 0, 'dve', ca[0][:, :T], ('ca', 0))
                        conv3_ffn(st, l, 22 + j, bv_, 1, 'pool', ca[1][:, :T], ('ca', 1))
                        act(sgt[:, :T], ca[0][:, :T], AF.Silu, [('ca', 0)], ['sgt'])
                        V('dve', 'tensor_tensor', gbuf[:, j - f0, :T], sgt[:, :T], ca[1][:, :T], ALU.mult, reads=['sgt', ('ca', 1)], writes=[('gbuf', j - f0)])
                Wd = [wreq(wdn3[:, g0:g0 + n, :], 128, n, 1024) for (g0, n) in groups]
                for dc in range(8):
                    b = bank('gen')
                    idx = 0
                    for gi, (g0, n) in enumerate(groups):
                        for ff in range(n):
                            fc = g0 + ff - f0
                            mm(ps[b][:, :T], Wd[gi][0][:, ff, dc * 128:(dc + 1) * 128], gbuf[:, fc, :T], idx == 0, idx == 10,
                               [('gbuf', fc), Wd[gi][1]], [P(b)])
                            idx += 1
                    residual_add(st, b, dc)

        def load_x(st, src, t0):
            T, TS, NS = st.T, st.TS, st.NS
            for s in range(NS):
                S.dma('sp', xstage[:TS, :], src[t0 + s * TS:t0 + (s + 1) * TS, :], reads=[], writes=['xstage'])
                for k0 in (0, 4):
                    b = bank('gen')
                    for kk2 in range(4):
                        kc = k0 + kk2
                        mm(ps[b][:, kk2 * TS:(kk2 + 1) * TS], xstage[:TS, kc * 128:(kc + 1) * 128], ident_f[:TS, :TS], True, True,
                           ['xstage', 'ident_f'], [P(b)])
                    act(xT[:, k0:k0 + 4, s * TS:(s + 1) * TS], ps[b][:, :4 * TS].rearrange("p (k t) -> p k t", k=4), AF.Copy, [P(b)], ['xT'])

        def store_x(st, dst, t0):
            T, TS, NS = st.T, st.TS, st.NS
            for s in range(NS):
                for k0 in (0, 4):
                    b = bank('gen')
                    for kk2 in range(4):
                        kc = k0 + kk2
                        mm(ps[b][:TS, kk2 * 128:(kk2 + 1) * 128], xT[:, kc, s * TS:(s + 1) * TS], ident_f[:, :], True, True, ['xT', 'ident_f'], [P(b)])
                    act(xstage[:TS, k0 * 128:(k0 + 4) * 128], ps[b][:TS, :], AF.Copy, [P(b)], ['xstage'])
                S.dma('act', dst[t0 + s * TS:t0 + (s + 1) * TS, :], xstage[:TS, :], reads=['xstage'], writes=['O_y'])

        def store_T(src3, key, ncn, R, r0, dst):
            for c0 in range(0, ncn, 4):
                n = min(4, ncn - c0)
                b = bank('gen')
                for c in range(n):
                    mm(ps[b][:R, c * 128:(c + 1) * 128], src3[:, c0 + c, r0:r0 + R], ident_f[:, :], True, True, [key, 'ident_f'], [P(b)])
                act(ostage[:R, :n * 128], ps[b][:R, :n * 128], AF.Copy, [P(b)], ['tmpg'])
                S.dma('act', dst[:, c0 * 128:(c0 + n) * 128], ostage[:R, :n * 128], reads=['tmpg'], writes=['O_st'])

        def run_layers(st, t0, O_k, O_v, O_lf, O_vc):
            for l in range(NL):
                if l % 2 == 0:
                    even_layer(st, l, t0, O_k, O_v, O_lf)
                else:
                    odd_layer(st, l, t0, O_vc)
                if DBG >= 6:
                    ffn(st, l)

        def store_states(st, O_cb, O_cd, O_cf):
            for i2 in range(2):
                store_T(st.histB, st.kB, 4, 2, i2 * 2, O_cb[i2 * 2:i2 * 2 + 2, :])
                store_T(st.histD, st.kD, 4, 30, i2 * 30, O_cd[i2 * 30:i2 * 30 + 30, :])
            for l in range(4):
                store_T(st.histF, st.kF, 44, 2, l * 2, O_cf[l * 2:l * 2 + 2, :])

        if do_sample:
            ss = make_stream('s', DEC, PAST + 512)
            load_T(I['state_conv_ffn'], 8, 5632, ss.histF, ss.kF)
            load_T(I['state_conv_b'], 4, 512, ss.histB, ss.kB)
            load_T(I['state_conv_d'], 60, 512, ss.histD, ss.kD)
            V('pool', 'memset', ss.carry[:], 0.0, reads=[], writes=[ss.kC])
            pre = Stream()
            pre.sid, pre.T, pre.TS, pre.NS = 's', 512, 128, 4
            pre.KT, pre.Vs, pre.carry = ss.KT, ss.Vs, ss.carry
            pre.kC = ss.kC
            for i2 in range(2):
                if 2 * i2 >= NL:
                    continue
                for t0 in range(0, PAST, 512):
                    for s in range(4):
                        r0 = t0 + s * 128
                        S.dma('sp', kf[:, :], I['cache_k'][i2, r0:r0 + 128, :], reads=[], writes=['kf'])
                        act(kb[:, :], kf[:, :], AF.Copy, ['kf'], ['kb'])
                        transp_heads(pre, kb, kT, 0, 'kb', 'kT')
                        S.dma('act', ss.KT[i2][:, 0:64, r0:r0 + 128].rearrange("h d t -> d h t"), kT[:64, :, :128], reads=['kT'], writes=[('KV', 's', i2)])
                        S.dma('sp', vf[:, :], I['cache_v'][i2, r0:r0 + 128, :], reads=[], writes=['vf'])
                        V('dve', 'tensor_copy', vb[:, 0, :, 0:64], vf[:, :].rearrange("p (h d) -> p h d", h=8), reads=['vf'], writes=['vb'])
                        S.dma('act', ss.Vs[i2][:, 0:128, r0 // 128, :].rearrange("h p d -> p h d"), vb[:, 0, :, :], reads=['vb'], writes=[('KV', 's', i2)])
                        S.dma('sp', logf[:, s, :], I['cache_logf'][i2, r0:r0 + 128, :], reads=[], writes=['logf'])
                    kv_finish(pre, i2, t0)
            load_x(ss, I['x_sample'], 0)
            ss.o0 = 0
            run_layers(ss, PAST, O['s_k'], O['s_v'], O['s_lf'], O['s_vc'])
            store_x(ss, O['y_s'], 0)
            store_states(ss, O['s_cb'], O['s_cd'], O['s_cf'])

        if NT > 0:
            sp_ = make_stream('p', 512, NTOK)
            for tname, tk in ((sp_.histF, sp_.kF), (sp_.histB, sp_.kB), (sp_.histD, sp_.kD), (sp_.carry, sp_.kC)):
                V('pool', 'memset', tname[:], 0.0, reads=[], writes=[tk])
            for ti in range(NT):
                t0 = ti * 512
                sp_.o0 = t0
                load_x(sp_, I['x_prompt'], t0)
                run_layers_prompt = run_layers
                run_layers_prompt(sp_, t0, O['p_k'], O['p_v'], O['p_lf'], None)
                store_x(sp_, O['y_p'], t0)
            store_states(sp_, O['p_cb'], O['p_cd'], O['p_cf'])

        S.finish('sp')
        print("SBUF remaining at end:", nc.sbuf_bytes_remaining)
        S.simulate()
        print("instructions:", S.nins, "counts:", {k: v for k, v in S.cnt.items() if v})
    return nc


_NC_CACHE = {}


def _prep_inputs(inputs, c, NT):
    f = lambda a: np.ascontiguousarray(a, dtype=np.float32)
    b = c % 2
    m = {
        'x_prompt': f(inputs['x_prompt'][b, :NT * 512]),
        'x_sample': f(inputs['x_sample'][c]),
        'cache_k': f(inputs['cache_k'][:, c].reshape(2, PAST, 512)),
        'cache_v': f(inputs['cache_v'][:, c].reshape(2, PAST, 512)),
        'cache_logf': f(inputs['cache_logf'][:, c]),
        'state_conv_b': f(inputs['state_conv_b'][:, c].reshape(4, 512)),
        'state_conv_d': f(inputs['state_conv_d'][:, c].reshape(60, 512)),
        'state_conv_ffn': f(inputs['state_conv_ffn'][:, c].reshape(8, 5632)),
        'conv_b': f(inputs['conv_b'].reshape(6, 512)),
        'conv_d': f(inputs['conv_d'].reshape(62, 512)),
        'conv_ffn': f(inputs['conv_ffn'].reshape(12, 5632)),
    }
    for k in ['g_mix', 'w_in_even', 'b_f', 'g_q', 'g_k', 'w_out_even', 'w_in_odd', 'g_vc', 'w_s', 'b_s', 'g_d',
              'w_out_odd', 'g_ffn', 'w_up', 'w_down']:
        m[k] = f(inputs[k])
    return m


def run(inputs, NT=32, NL=4):
    key = (NT, NL)
    if key not in _NC_CACHE:
        _NC_CACHE[key] = build(NT, NL)
    nc = _NC_CACHE[key]
    in_maps = [_prep_inputs(inputs, c, NT) for c in range(8)]
    res = run_bass_kernel_spmd(nc, in_maps, core_ids=list(range(8)))
    R = res.results
    B = 2
    T = NT * 512
    y_p = np.stack([R[b]['y_p'] for b in range(B)])
    y_s = np.stack([R[c]['y_s'] for c in range(8)])
    pk = np.stack([R[b]['p_k'] for b in range(B)], axis=1).reshape(2, B, T, 8, 64)
    pv = np.stack([R[b]['p_v'] for b in range(B)], axis=1).reshape(2, B, T, 8, 64)
    plf = np.stack([R[b]['p_lf'] for b in range(B)], axis=1)
    pcb = np.stack([R[b]['p_cb'].reshape(2, 2, 512) for b in range(B)], axis=1)
    pcd = np.stack([R[b]['p_cd'].reshape(2, 30, 512) for b in range(B)], axis=1)
    pcf = np.stack([R[b]['p_cf'].reshape(4, 2, 5632) for b in range(B)], axis=1)
    sk = np.stack([R[c]['s_k'] for c in range(8)], axis=1).reshape(2, 8, DEC, 8, 64)
    sv = np.stack([R[c]['s_v'] for c in range(8)], axis=1).reshape(2, 8, DEC, 8, 64)
    slf = np.stack([R[c]['s_lf'] for c in range(8)], axis=1)
    scb = np.stack([R[c]['s_cb'].reshape(2, 2, 512) for c in range(8)], axis=1)
    svc = np.stack([R[c]['s_vc'] for c in range(8)], axis=1)
    scd = np.stack([R[c]['s_cd'].reshape(2, 30, 512) for c in range(8)], axis=1)
    scf = np.stack([R[c]['s_cf'].reshape(4, 2, 5632) for c in range(8)], axis=1)
    return tuple(np.ascontiguousarray(a, dtype=np.float32) for a in
                 (y_p, y_s, pk, pv, plf, pcb, pcd, pcf, sk, sv, slf, scb, svc, scd, scf))


def kernel(**inputs):
    return run(inputs, NT=32, NL=4)
```
